# Optimizing a Trainium2 kernel written in Bass

```python
import math
import jax, jax.numpy as jnp
from jax import lax
import numpy as np

D_MODEL = 1024
BATCH = 32
SEQ = 2048
DEPTH = 1
DEC_BATCH = 32
DEC_SEQ = 64
PAST_LEN = 4096

CHUNK = 64
N_META = 16
HEAD_DIM = 64
ATTN_W = D_MODEL // 2
N_Q_HEADS = ATTN_W // HEAD_DIM
N_KV_HEADS = 2
REP = N_Q_HEADS // N_KV_HEADS
KV_W = N_KV_HEADS * HEAD_DIM
WINDOW = 128
WIN_CHUNKS = WINDOW // CHUNK
SSM_W = D_MODEL - ATTN_W
SSM_CH = 16
SSM_G = SSM_W // SSM_CH
SSM_P = 64
D_FF = 4 * D_MODEL
IN_W = ATTN_W + 2 * KV_W + SSM_W
EPS = 1e-6
DT_MIN = 1e-3
DT_MAX = 1e-1
EIG_CLIP = -1e-4
NEG_INF = -1e30

kernel_name = "hymba_swa_sink_s5_streaming_step"


def rmsnorm(x, g):
    xf = x.astype(jnp.float32)
    y = xf * lax.rsqrt(jnp.mean(jnp.square(xf), axis=-1, keepdims=True) + EPS)
    return (y * g.astype(jnp.float32)).astype(x.dtype)


def project(h, p):
    z = rmsnorm(h, p["norm1_g"]) @ p["w_in"]
    q, k, v, u = jnp.split(z, [ATTN_W, ATTN_W + KV_W, ATTN_W + 2 * KV_W], axis=-1)
    lead = h.shape[:-1]
    q = rmsnorm(q.reshape(lead + (N_KV_HEADS, REP, HEAD_DIM)), p["q_norm_g"])
    k = rmsnorm(k.reshape(lead + (N_KV_HEADS, HEAD_DIM)), p["k_norm_g"])
    v = v.reshape(lead + (N_KV_HEADS, HEAD_DIM))
    u = u.reshape(lead + (SSM_G, SSM_CH))
    return q, k, v, u


def attend(q, k, v, sinks, mask):
    s = jnp.einsum('...qhrd,...khd->...hrqk', q, k).astype(jnp.float32) * (HEAD_DIM ** -0.5)
    if mask is not None:
        s = jnp.where(mask, s, NEG_INF)
    sink = jnp.broadcast_to(sinks.astype(jnp.float32).reshape(N_KV_HEADS, REP, 1, 1), s.shape[:-1] + (1,))
    pr = jax.nn.softmax(jnp.concatenate([s, sink], axis=-1), axis=-1)[..., :-1]
    return jnp.einsum('...hrqk,...khd->...qhrd', pr.astype(v.dtype), v)


def swa_prompt(q, k, v, sinks):
    b = q.shape[0]
    s_len = q.shape[1] - N_META
    nc = s_len // CHUNK
    out_meta = attend(q[:, :N_META], k[:, :N_META], v[:, :N_META], sinks, None)
    qr = q[:, N_META:].reshape(b, nc, CHUNK, N_KV_HEADS, REP, HEAD_DIM)

    def band(x):
        xp = jnp.pad(x[:, N_META:], ((0, 0), (WINDOW, 0), (0, 0), (0, 0)))
        xp = xp.reshape(b, nc + WIN_CHUNKS, CHUNK, N_KV_HEADS, HEAD_DIM)
        blocks = jnp.concatenate([xp[:, i:i + nc] for i in range(WIN_CHUNKS + 1)], axis=2)
        meta = jnp.broadcast_to(x[:, None, :N_META], (b, nc, N_META, N_KV_HEADS, HEAD_DIM))
        return jnp.concatenate([meta, blocks], axis=2)

    c = jnp.arange(nc)[:, None]
    j = jnp.arange((WIN_CHUNKS + 1) * CHUNK)[None, :]
    band_ok = (c - WIN_CHUNKS + j // CHUNK) >= 0
    mask = jnp.concatenate([jnp.ones((nc, N_META), dtype=bool), band_ok], axis=1)
    out_r = attend(qr, band(k), band(v), sinks, mask[:, None, None, None, :])
    out_r = out_r.reshape(b, s_len, N_KV_HEADS, REP, HEAD_DIM)
    return jnp.concatenate([out_meta, out_r], axis=1).reshape(b, N_META + s_len, ATTN_W)


def swa_sample(q, k, v, ck, cv, sinks):
    b, t = q.shape[0], q.shape[1]
    keys = jnp.concatenate([ck.astype(k.dtype), k], axis=1)
    vals = jnp.concatenate([cv.astype(v.dtype), v], axis=1)
    return attend(q, keys, vals, sinks, None).reshape(b, t, ATTN_W)


def s5_scan(u, h0_re, h0_im, p):
    uf = u.astype(jnp.float32)
    lam_re = jnp.minimum(p["ssm_A_re"].astype(jnp.float32), EIG_CLIP)
    lam_im = p["ssm_A_im"].astype(jnp.float32)
    dt = jnp.exp(p["ssm_log_dt"].astype(jnp.float32))[:, None]
    mag = jnp.exp(lam_re * dt)
    ar = mag * jnp.cos(lam_im * dt)
    ai = mag * jnp.sin(lam_im * dt)
    den = lam_re * lam_re + lam_im * lam_im
    cr = ((ar - 1.0) * lam_re + ai * lam_im) / den
    ci = (ai * lam_re - (ar - 1.0) * lam_im) / den
    b_re = p["ssm_B_re"].astype(jnp.float32)
    b_im = p["ssm_B_im"].astype(jnp.float32)
    bb_re = cr[..., None] * b_re - ci[..., None] * b_im
    bb_im = cr[..., None] * b_im + ci[..., None] * b_re
    bu_re = jnp.einsum('blgc,gpc->blgp', uf, bb_re)
    bu_im = jnp.einsum('blgc,gpc->blgp', uf, bb_im)
    if h0_re is not None:
        h0r = h0_re.astype(jnp.float32)
        h0i = h0_im.astype(jnp.float32)
        bu_re = bu_re.at[:, 0].add(ar * h0r - ai * h0i)
        bu_im = bu_im.at[:, 0].add(ar * h0i + ai * h0r)
    length = u.shape[1]
    a_re = jnp.broadcast_to(ar, (1, length, SSM_G, SSM_P))
    a_im = jnp.broadcast_to(ai, (1, length, SSM_G, SSM_P))

    def combine(e1, e2):
        a1r, a1i, b1r, b1i = e1
        a2r, a2i, b2r, b2i = e2
        return (a2r * a1r - a2i * a1i,
                a2r * a1i + a2i * a1r,
                a2r * b1r - a2i * b1i + b2r,
                a2r * b1i + a2i * b1r + b2i)

    _, _, xs_re, xs_im = lax.associative_scan(combine, (a_re, a_im, bu_re, bu_im), axis=1)
    y = (jnp.einsum('blgp,gcp->blgc', xs_re, p["ssm_C_re"].astype(jnp.float32))
         - jnp.einsum('blgp,gcp->blgc', xs_im, p["ssm_C_im"].astype(jnp.float32))
         + p["ssm_D"].astype(jnp.float32) * uf)
    return y, xs_re[:, -1], xs_im[:, -1]


def finish(h, attn, y_ssm, p):
    ys = y_ssm.reshape(y_ssm.shape[:-2] + (SSM_W,)).astype(h.dtype)
    zs = jax.nn.gelu(ys)
    s_out = zs * jax.nn.sigmoid(zs @ p["w_glu"] + p["b_glu"])
    merged = jnp.concatenate([rmsnorm(attn.astype(h.dtype), p["attn_out_g"]),
                              rmsnorm(s_out, p["ssm_out_g"])], axis=-1)
    h = h + merged @ p["w_out"]
    f = rmsnorm(h, p["norm2_g"]) @ p["w_up"]
    return h + jnp.square(jax.nn.relu(f)) @ p["w_down"]


def setup_inputs(seed: int = 0) -> dict:
    key = jax.random.key(seed)
    ks = jax.random.split(key, 32)
    f32 = jnp.float32
    nrm = lambda k, shape, scale: scale * jax.random.normal(k, shape, dtype=f32)
    a_im0 = jnp.pi * jnp.arange(SSM_P, dtype=f32)
    return {
        "x_prompt": nrm(ks[0], (BATCH, SEQ, D_MODEL), 1.0),
        "x_sample": nrm(ks[1], (DEC_BATCH, DEC_SEQ, D_MODEL), 1.0),
        "cache_swa_k": nrm(ks[2], (DEPTH, DEC_BATCH, N_META + WINDOW, N_KV_HEADS, HEAD_DIM), 1.0),
        "cache_swa_v": nrm(ks[3], (DEPTH, DEC_BATCH, N_META + WINDOW, N_KV_HEADS, HEAD_DIM), 1.0),
        "state_ssm_re": nrm(ks[4], (DEPTH, DEC_BATCH, SSM_G, SSM_P), 0.1),
        "state_ssm_im": nrm(ks[5], (DEPTH, DEC_BATCH, SSM_G, SSM_P), 0.1),
        "meta_tokens": nrm(ks[6], (N_META, D_MODEL), 1.0),
        "norm1_g": 1.0 + nrm(ks[7], (DEPTH, D_MODEL), 0.01),
        "w_in": nrm(ks[8], (DEPTH, D_MODEL, IN_W), D_MODEL ** -0.5),
        "q_norm_g": 1.0 + nrm(ks[9], (DEPTH, HEAD_DIM), 0.01),
        "k_norm_g": 1.0 + nrm(ks[10], (DEPTH, HEAD_DIM), 0.01),
        "sinks": nrm(ks[11], (DEPTH, N_Q_HEADS), 0.5),
        "ssm_A_re": -0.5 + nrm(ks[12], (DEPTH, SSM_G, SSM_P), 0.01),
        "ssm_A_im": a_im0 + nrm(ks[13], (DEPTH, SSM_G, SSM_P), 0.01),
        "ssm_log_dt": jax.random.uniform(ks[14], (DEPTH, SSM_G), dtype=f32,
                                         minval=math.log(DT_MIN), maxval=math.log(DT_MAX)),
        "ssm_B_re": nrm(ks[15], (DEPTH, SSM_G, SSM_P, SSM_CH), (2 * SSM_CH) ** -0.5),
        "ssm_B_im": nrm(ks[16], (DEPTH, SSM_G, SSM_P, SSM_CH), (2 * SSM_CH) ** -0.5),
        "ssm_C_re": nrm(ks[17], (DEPTH, SSM_G, SSM_CH, SSM_P), SSM_P ** -0.5),
        "ssm_C_im": nrm(ks[18], (DEPTH, SSM_G, SSM_CH, SSM_P), SSM_P ** -0.5),
        "ssm_D": nrm(ks[19], (DEPTH, SSM_G, SSM_CH), 1.0),
        "w_glu": nrm(ks[20], (DEPTH, SSM_W, SSM_W), SSM_W ** -0.5),
        "b_glu": nrm(ks[21], (DEPTH, SSM_W), 0.01),
        "attn_out_g": 1.0 + nrm(ks[22], (DEPTH, ATTN_W), 0.01),
        "ssm_out_g": 1.0 + nrm(ks[23], (DEPTH, SSM_W), 0.01),
        "w_out": nrm(ks[24], (DEPTH, D_MODEL, D_MODEL), D_MODEL ** -0.5),
        "norm2_g": 1.0 + nrm(ks[25], (DEPTH, D_MODEL), 0.01),
        "w_up": nrm(ks[26], (DEPTH, D_MODEL, D_FF), D_MODEL ** -0.5),
        "w_down": nrm(ks[27], (DEPTH, D_FF, D_MODEL), D_FF ** -0.5),
    }


def reference(x_prompt, x_sample, cache_swa_k, cache_swa_v, state_ssm_re, state_ssm_im,
              meta_tokens, norm1_g, w_in, q_norm_g, k_norm_g, sinks,
              ssm_A_re, ssm_A_im, ssm_log_dt, ssm_B_re, ssm_B_im, ssm_C_re, ssm_C_im, ssm_D,
              w_glu, b_glu, attn_out_g, ssm_out_g, w_out, norm2_g, w_up, w_down):
    b = x_prompt.shape[0]
    hp = jnp.concatenate([jnp.broadcast_to(meta_tokens.astype(x_prompt.dtype)[None], (b, N_META, D_MODEL)),
                          x_prompt], axis=1)
    hs = x_sample
    kp_l, vp_l, srp_l, sip_l, ks_l, vs_l, srs_l, sis_l = [], [], [], [], [], [], [], []
    for l in range(DEPTH):
        p = {"norm1_g": norm1_g[l], "w_in": w_in[l], "q_norm_g": q_norm_g[l], "k_norm_g": k_norm_g[l],
             "ssm_A_re": ssm_A_re[l], "ssm_A_im": ssm_A_im[l], "ssm_log_dt": ssm_log_dt[l],
             "ssm_B_re": ssm_B_re[l], "ssm_B_im": ssm_B_im[l], "ssm_C_re": ssm_C_re[l],
             "ssm_C_im": ssm_C_im[l], "ssm_D": ssm_D[l], "w_glu": w_glu[l], "b_glu": b_glu[l],
             "attn_out_g": attn_out_g[l], "ssm_out_g": ssm_out_g[l], "w_out": w_out[l],
             "norm2_g": norm2_g[l], "w_up": w_up[l], "w_down": w_down[l]}
        q, k, v, u = project(hp, p)
        attn = swa_prompt(q, k, v, sinks[l])
        y_ssm, fr, fi = s5_scan(u, None, None, p)
        kp_l.append(jnp.concatenate([k[:, :N_META], k[:, -WINDOW:]], axis=1))
        vp_l.append(jnp.concatenate([v[:, :N_META], v[:, -WINDOW:]], axis=1))
        srp_l.append(fr)
        sip_l.append(fi)
        hp = finish(hp, attn, y_ssm, p)
        q, k, v, u = project(hs, p)
        attn = swa_sample(q, k, v, cache_swa_k[l], cache_swa_v[l], sinks[l])
        y_ssm, fr, fi = s5_scan(u, state_ssm_re[l], state_ssm_im[l], p)
        ks_l.append(k)
        vs_l.append(v)
        srs_l.append(fr)
        sis_l.append(fi)
        hs = finish(hs, attn, y_ssm, p)
    y_prompt = hp[:, N_META:]
    y_sample = hs
    return (y_prompt, y_sample,
            jnp.stack(kp_l, 0), jnp.stack(vp_l, 0), jnp.stack(srp_l, 0), jnp.stack(sip_l, 0),
            jnp.stack(ks_l, 0), jnp.stack(vs_l, 0), jnp.stack(srs_l, 0), jnp.stack(sis_l, 0))
```

```python
import numpy as np
from contextlib import ExitStack
import concourse.bass as bass
import concourse.mybir as mybir
from concourse.bass_utils import run_bass_kernel_spmd

F32 = mybir.dt.float32
BF16 = mybir.dt.bfloat16
I32 = mybir.dt.int32
AF = mybir.ActivationFunctionType
ALU = mybir.AluOpType
AX = mybir.AxisListType

NCORE = 8
D = 1024
SEQ = 2048
NSEQ = 4
NBLK = SEQ // 128
EPS = 1e-6
TWO_PI = float(2 * np.pi)


class Prog:
    NDMASEM = 16
    ENGS = ['pe', 'act', 'dve', 'pool', 'sp']

    def __init__(self):
        self.ops = []

    def op(self, eng, fn, reads=(), writes=(), dma=False, barrier=False):
        self.ops.append(dict(eng=eng, fn=fn, reads=tuple(reads), writes=tuple(writes), dma=dma, barrier=barrier))

    def barrier(self):
        for e in ['pe', 'act', 'dve', 'pool']:
            self.op(e, lambda en: en.nop(nofuse=True), barrier=True)

    def emit(self, nc, stack):
        ops = self.ops
        engs = self.ENGS
        last_w = {}
        readers = {}
        last_op = {}
        recent_dma = {e: [] for e in engs}
        for i, o in enumerate(ops):
            if o['barrier']:
                deps = set(v for k, v in last_op.items())
                for e in engs:
                    deps |= set(recent_dma[e][-self.NDMASEM:])
                deps.discard(i)
                o['deps'] = set(d for d in deps if ops[d]['dma'] or ops[d]['eng'] != o['eng'])
                last_op[o['eng']] = i
                continue
            deps = set()
            for r in o['reads']:
                if r in last_w:
                    deps.add(last_w[r])
            for w in o['writes']:
                if w in last_w:
                    deps.add(last_w[w])
                lastrd = {}
                for rd in readers.get(w, ()):
                    if ops[rd]['dma']:
                        deps.add(rd)
                    else:
                        lastrd[ops[rd]['eng']] = rd
                for rd in lastrd.values():
                    deps.add(rd)
            deps.discard(i)
            nd = set()
            for d in deps:
                od = ops[d]
                if not od['dma'] and not o['dma'] and od['eng'] == o['eng']:
                    if o['eng'] == 'pe':
                        continue
                nd.add(d)
            o['deps'] = nd
            for r in o['reads']:
                readers.setdefault(r, []).append(i)
            for w in o['writes']:
                last_w[w] = i
                readers[w] = []
            if o['dma']:
                recent_dma[o['eng']].append(i)
            else:
                last_op[o['eng']] = i
        needed = set()
        for o in ops:
            needed |= o['deps']
        cnt = {e: 0 for e in engs}
        dcnt = {e: 0 for e in engs}
        for i, o in enumerate(ops):
            e = o['eng']
            if o['dma']:
                k = dcnt[e]
                dcnt[e] += 1
                o['sig'] = ('d', e, k % self.NDMASEM, 16 * (k // self.NDMASEM + 1))
            elif i in needed:
                cnt[e] += 1
                o['sig'] = ('c', e, 0, cnt[e])
            else:
                o['sig'] = None
        sems = {}
        for e in engs:
            sems[('c', e, 0)] = stack.enter_context(nc.semaphore('s_' + e))
        for e in engs:
            if dcnt[e]:
                for k in range(self.NDMASEM):
                    sems[('d', e, k)] = stack.enter_context(nc.semaphore('d_%s_%d' % (e, k)))
        block = stack.enter_context(nc.Block())
        per = {e: [o for o in ops if o['eng'] == e] for e in engs}

        def run(e, engine):
            waited = {}

            def wait(key, val):
                if waited.get(key, 0) >= val:
                    return
                waited[key] = val
                engine.wait_ge(sems[key], val)
            final = {}
            for o in per[e]:
                for d in sorted(o['deps']):
                    s = ops[d]['sig']
                    wait(s[:3], s[3])
                if o['dma']:
                    s = o['sig']
                    if s[3] > 16:
                        wait(s[:3], s[3] - 16)
                    o['fn'](engine).then_inc(sems[s[:3]], 16)
                    final[s[:3]] = s[3]
                else:
                    ins = o['fn'](engine)
                    if o['sig'] is not None:
                        ins.then_inc(sems[o['sig'][:3]], 1)
            for key, val in final.items():
                wait(key, val)

        @block.tensor
        def _(eng):
            run('pe', eng)

        @block.scalar
        def _(eng):
            run('act', eng)

        @block.vector
        def _(eng):
            run('dve', eng)

        @block.gpsimd
        def _(eng):
            run('pool', eng)

        @block.sync
        def _(eng):
            run('sp', eng)


def merge_ops(a, b):
    out = []
    ia = ib = 0
    na, nb = len(a), len(b)
    while ia < na or ib < nb:
        if ib >= nb or (ia < na and ia * nb <= ib * na):
            out.append(a[ia])
            ia += 1
        else:
            out.append(b[ib])
            ib += 1
    return out


def build(NBLK=NBLK, stop=99, do_sample=True, overlap=False):
    nc = bass.Bass("TRN2", target_bir_lowering=False)

    def din(name, shape):
        return nc.dram_tensor(name, list(shape), F32, kind="ExternalInput").ap()

    def dout(name, shape):
        return nc.dram_tensor(name, list(shape), F32, kind="ExternalOutput").ap()

    xp = din("xp", [NSEQ, SEQ, D])
    xs = din("xs", [NSEQ, 64, D])
    ck = din("ck", [NSEQ, 144, 128])
    cv = din("cv", [NSEQ, 144, 128])
    s_re = din("s_re", [NSEQ * 32, 64])
    s_im = din("s_im", [NSEQ * 32, 64])
    meta = din("meta", [16, D])
    norm1_g = din("norm1_g", [1, D])
    w_in = din("w_in", [D, 1280])
    q_g = din("q_g", [1, 64])
    k_g = din("k_g", [1, 64])
    sinks = din("sinks", [1, 8])
    A_re = din("A_re", [32, 64])
    A_im = din("A_im", [32, 64])
    log_dt = din("log_dt", [1, 32])
    B_re = din("B_re", [32, 64, 16])
    B_im = din("B_im", [32, 64, 16])
    C_re = din("C_re", [512, 64])
    C_im = din("C_im", [512, 64])
    Dp = din("Dp", [32, 16])
    w_glu = din("w_glu", [512, 512])
    b_glu = din("b_glu", [1, 512])
    ao_g = din("ao_g", [1, 512])
    so_g = din("so_g", [1, 512])
    w_out = din("w_out", [D, D])
    norm2_g = din("norm2_g", [1, D])
    w_up = din("w_up", [D, 4096])
    w_down = din("w_down", [4096, D])

    yp = dout("yp", [NSEQ, SEQ, D])
    ys = dout("ys", [NSEQ, 64, D])
    kpo = dout("kpo", [NSEQ, 144, 128])
    vpo = dout("vpo", [NSEQ, 144, 128])
    srp = dout("srp", [NSEQ * 32, 64])
    sip = dout("sip", [NSEQ * 32, 64])
    kso = dout("kso", [NSEQ, 64, 128])
    vso = dout("vso", [NSEQ, 64, 128])
    srs = dout("srs", [NSEQ * 32, 64])
    sis = dout("sis", [NSEQ * 32, 64])

    wup_scr = nc.dram_tensor("wup_scr", [128, 8, 4096], BF16, kind="Internal").ap()
    wdn_scr = nc.dram_tensor("wdn_scr", [128, 32, 1024], BF16, kind="Internal").ap()

    P = Prog()

    def op(eng, fn, r=(), w=()):
        P.op(eng, fn, r, w)

    def dma(fn, r=(), w=()):
        P.op('sp', fn, r, w, dma=True)

    with ExitStack() as st:
        NB = 206 * 1024
        raw = st.enter_context(nc.sbuf_tensor("raw", [128, NB // 2], BF16))
        rawb = raw[:]
        rawf = rawb.bitcast(F32)
        rawi = rawb.bitcast(I32)
        state = {'off': 0}

        offs = {}

        def salloc(shape, dt, at=None, name=None):
            n = 1
            for s_ in shape[1:]:
                n *= s_
            nb = n * (2 if dt == BF16 else 4)
            nb = (nb + 3) // 4 * 4
            if at is None:
                off = state['off']
                state['off'] += nb
                assert state['off'] <= NB, ("SBUF overflow", state['off'])
            else:
                off = at
            if dt == BF16:
                v = rawb[0:shape[0], off // 2: off // 2 + n]
            elif dt == F32:
                v = rawf[0:shape[0], off // 4: off // 4 + n]
            else:
                v = rawi[0:shape[0], off // 4: off // 4 + n]
            if len(shape) > 2:
                names = ['a%d' % i for i in range(len(shape) - 1)]
                kw = {names[i]: shape[1 + i] for i in range(len(shape) - 2)}
                v = v.rearrange("p (%s) -> p %s" % (' '.join(names), ' '.join(names)), **kw)
            if name is not None:
                offs[name] = off
            return v

        psf = []
        psb = []
        for b in range(8):
            t = st.enter_context(nc.psum_tensor("ps%d" % b, [128, 512], F32))
            psf.append(t[:])
            psb.append(t[:].bitcast(BF16))
        bank_ctr = {'i': 0}

        def bank():
            if bank_ctr.get('front') and overlap:
                b = 4 + bank_ctr['i'] % 4
            else:
                b = bank_ctr['i'] % 8
            bank_ctr['i'] += 1
            return b

        win = salloc([128, 8, 1280], BF16)
        wglu = salloc([128, 4, 512], BF16)
        wout = salloc([128, 8, 1024], BF16)
        Tm = salloc([128, 32, 128], BF16)
        Pm = salloc([128, 32, 128], BF16)
        Qm = salloc([128, 16, 2, 128], BF16)
        identf = salloc([128, 128], F32)
        identb = salloc([128, 128], BF16)
        onesr = salloc([1, 128], BF16)
        bglu = salloc([128, 512], BF16)
        C1 = salloc([128, 2, 16, 4], F32)
        C2 = salloc([128, 2, 16, 4], F32)
        gk_t = salloc([128, 2, 64], F32)
        esink = salloc([128, 8], F32)
        gq8 = salloc([64, 1], F32)
        epsb = salloc([128, 1], F32)
        dumA = salloc([128, 1], F32)
        dumD = salloc([128, 1], F32)
        KTm = salloc([64, 2, 16], BF16)
        Vm = salloc([16, 2, 65], BF16)
        kmf = salloc([16, 128], F32)
        vmf = salloc([16, 128], F32)
        Zmeta = salloc([128, 3, 16], F32)
        ss = salloc([128, 8], F32)
        rs = salloc([128, 8], F32)
        st10 = salloc([128, 2, 10], F32)
        r10 = salloc([128, 2, 10], F32)
        den = salloc([128, 2, 4], F32)
        ssS = salloc([128, 4], F32)
        rS = salloc([128, 4], F32)
        KTmS = salloc([64, 4, 2, 16], BF16)
        VmS = salloc([16, 4, 2, 65], BF16)
        xt = salloc([128, 4, 1024], F32, name='xt')
        xnb = salloc([128, 2, 1024], BF16)
        XM = salloc([128, 8, 512], BF16, name='XM')
        hnT = salloc([128, 8, 512], BF16, at=offs['XM'])
        qsq = salloc([128, 640], F32)
        qnb = salloc([128, 2, 512], BF16)
        kf = salloc([128, 2, 128], F32)
        kb = salloc([128, 2, 128], BF16)
        vf = salloc([128, 128], F32)
        QT = salloc([64, 4, 8, 128], BF16)
        KT = salloc([64, 2, 2, 4, 128], BF16)
        Vt = salloc([128, 2, 4, 2, 65], BF16)
        Pprev = salloc([128, 2, 4, 128], BF16)
        Pcur = salloc([128, 2, 4, 128], BF16)
        Pmeta = salloc([16, 2, 4, 128], BF16)
        attn = salloc([128, 2, 512], F32)
        anb = salloc([128, 512], BF16)
        U_all = salloc([128, 32, 64], BF16, name='U_all')
        Vs = salloc([128, 2, 16, 64], F32)
        Hb = salloc([128, 2, 16, 64], BF16)
        Z = salloc([128, 2, 3, 64], F32)
        T1 = salloc([128, 2, 64], F32)
        T2 = salloc([128, 2, 64], F32)
        zsT = salloc([128, 4, 4, 128], BF16, at=offs['U_all'])
        rt = salloc([128, 2, 512], F32)
        yt = salloc([128, 2, 512], F32)
        wup_s = salloc([128, 3, 8, 256], BF16)
        wdn_s = salloc([128, 3, 4, 512], BF16)
        yA = salloc([128, 4, 512], F32, name='yA')
        yB = salloc([128, 4, 512], F32)
        gT = salloc([128, 16, 512], BF16, at=offs['yA'])
        u_ks = salloc([64, 32, 8, 16], BF16, name='u_ks')
        zs_bf = salloc([128, 4, 512], BF16, at=offs['u_ks'])
        sn_bf = salloc([128, 4, 512], BF16, at=offs['u_ks'] + 4096)
        print("SBUF bytes/partition used:", state['off'])

        scr0 = offs['xt']
        scr = {'off': scr0}

        def scratch(shape, dt):
            n = 1
            for s_ in shape[1:]:
                n *= s_
            nb = (n * (2 if dt == BF16 else 4) + 3) // 4 * 4
            v = salloc(shape, dt, at=scr['off'])
            scr['off'] += nb
            assert scr['off'] <= state['off'], "scratch overflow"
            return v

        cp_i = {'i': 0}

        def copy_any(out, in_, r, w, engs=('act', 'dve')):
            e = engs[cp_i['i'] % len(engs)]
            cp_i['i'] += 1
            if e == 'act':
                op('act', lambda en: en.copy(out=out, in_=in_), r, w)
            else:
                op(e, lambda en: en.tensor_copy(out=out, in_=in_), r, w)

        def rstd_from(ssap, rsap, n, rname, wname):
            op('act', lambda en: en.activation(out=rsap, in_=ssap, func=AF.Ln, scale=1.0 / n,
                                               bias=epsb[0:ssap.shape[0], 0:1]), [rname], [wname])
            op('act', lambda en: en.activation(out=rsap, in_=rsap, func=AF.Exp, scale=-0.5), [wname], [wname])

        op('pool', lambda en: en.memset(identf, 0.0), [], ['identf'])
        op('pool', lambda en: en.affine_select(out=identf, in_=identf, compare_op=ALU.not_equal, fill=1.0,
                                               base=0, pattern=[[-1, 128]], channel_multiplier=1),
           ['identf'], ['identf'])
        op('dve', lambda en: en.tensor_copy(out=identb, in_=identf), ['identf'], ['identb'])
        op('pool', lambda en: en.memset(onesr, 1.0), [], ['onesr'])
        op('pool', lambda en: en.memset(epsb, EPS), [], ['epsb'])
        op('pool', lambda en: en.memset(Vm, 1.0), [], ['Vm'])
        dma(lambda en: en.dma_start(out=gk_t[:, 0, :], in_=k_g[0:1, :].broadcast_to([128, 64])), [], ['gk_t'])
        dma(lambda en: en.dma_start(out=gk_t[:, 1, :], in_=k_g[0:1, :].broadcast_to([128, 64])), [], ['gk_t'])
        dma(lambda en: en.dma_start(out=esink, in_=sinks[0:1, :].broadcast_to([128, 8])), [], ['esink'])
        op('act', lambda en: en.activation(out=esink, in_=esink, func=AF.Exp), ['esink'], ['esink'])
        dma(lambda en: en.dma_start(out=gq8, in_=q_g.rearrange("o d -> d o"), allow_slow_non_contiguous=True),
            [], ['gq8'])
        op('dve', lambda en: en.tensor_scalar(out=gq8, in0=gq8, scalar1=0.125, scalar2=None, op0=ALU.mult),
           ['gq8'], ['gq8'])

        if stop == 0.5:
            P.emit(nc, st)
            return nc
        g1 = scratch([128, 8], F32)
        mg = scratch([128, 8], F32)
        g2 = scratch([128, 8], F32)
        dma(lambda en: en.dma_start(out=g1, in_=norm1_g.rearrange("o (k p) -> p (o k)", p=128),
                                    allow_slow_non_contiguous=True), [], ['g1'])
        dma(lambda en: en.dma_start(out=mg[:, 0:4], in_=ao_g.rearrange("o (k p) -> p (o k)", p=128),
                                    allow_slow_non_contiguous=True), [], ['mg'])
        dma(lambda en: en.dma_start(out=mg[:, 4:8], in_=so_g.rearrange("o (k p) -> p (o k)", p=128),
                                    allow_slow_non_contiguous=True), [], ['mg'])
        dma(lambda en: en.dma_start(out=g2, in_=norm2_g.rearrange("o (k p) -> p (o k)", p=128),
                                    allow_slow_non_contiguous=True), [], ['g2'])
        stg = scratch([128, 6, 1280], F32)
        stb = scratch([128, 6, 1024], BF16)
        sc = {'i': 0}

        def cast_rows(dst, src_dram, ncol, gain, gname, wname):
            i = sc['i'] % 6
            sc['i'] += 1
            sname = 'stg%d' % i
            dma(lambda en: en.dma_start(out=stg[:, i, 0:ncol], in_=src_dram), [], [sname])
            e = ['act', 'dve'][sc['i'] % 2]
            rr = [sname] + ([gname] if gain is not None else [])
            if gain is None:
                if e == 'act':
                    op('act', lambda en: en.copy(out=dst, in_=stg[:, i, 0:ncol]), rr, [wname])
                else:
                    op(e, lambda en: en.tensor_copy(out=dst, in_=stg[:, i, 0:ncol]), rr, [wname])
            else:
                if e == 'act':
                    op('act', lambda en: en.mul(out=dst, in_=stg[:, i, 0:ncol], mul=gain), rr, [wname])
                else:
                    op(e, lambda en: en.tensor_scalar(out=dst, in0=stg[:, i, 0:ncol], scalar1=gain, scalar2=None,
                                                      op0=ALU.mult), rr, [wname])

        for kc in range(8):
            cast_rows(win[:, kc, :], w_in[kc * 128:(kc + 1) * 128, :], 1280, g1[:, kc:kc + 1], 'g1', 'win')
        if stop == 0.7:
            P.emit(nc, st)
            return nc
        for kc in range(4):
            cast_rows(wglu[:, kc, :], w_glu[kc * 128:(kc + 1) * 128, :], 512, None, None, 'wglu')
        if stop == 0.8:
            P.emit(nc, st)
            return nc
        for kc in range(8):
            cast_rows(wout[:, kc, :], w_out[kc * 128:(kc + 1) * 128, :], 1024, mg[:, kc:kc + 1], 'mg', 'wout')
        if stop == 0.9:
            P.emit(nc, st)
            return nc
        bgf = scratch([128, 512], F32)
        dma(lambda en: en.dma_start(out=bgf, in_=b_glu[0:1, :].broadcast_to([128, 512])), [], ['bgf'])
        if stop == 0.95:
            P.emit(nc, st)
            return nc
        op('dve', lambda en: en.tensor_copy(out=bglu, in_=bgf), ['bgf'], ['bglu'])
        if stop == 1:
            P.emit(nc, st)
            return nc
        i_conv0 = len(P.ops)
        for kc in range(8):
            for c4 in range(4):
                j = sc['i'] % 6
                cast_rows(stb[:, j, :], w_up[kc * 128:(kc + 1) * 128, c4 * 1024:(c4 + 1) * 1024], 1024,
                          g2[:, kc:kc + 1], 'g2', 'stb%d' % j)
                dma(lambda en, j=j, kc=kc, c4=c4: en.dma_start(out=wup_scr[:, kc, c4 * 1024:(c4 + 1) * 1024],
                                                               in_=stb[:, j, :]), ['stb%d' % j], ['wup_scr'])
        for ht in range(32):
            j = sc['i'] % 6
            cast_rows(stb[:, j, :], w_down[ht * 128:(ht + 1) * 128, :], 1024, None, None, 'stb%d' % j)
            dma(lambda en, j=j, ht=ht: en.dma_start(out=wdn_scr[:, ht, :], in_=stb[:, j, :]),
                ['stb%d' % j], ['wdn_scr'])

        i_conv1 = len(P.ops)
        if stop == 2:
            P.emit(nc, st)
            return nc
        Are = scratch([128, 16], F32)
        Aim = scratch([128, 16], F32)
        dtl = scratch([128, 16], F32)
        Bre = scratch([128, 16, 16], F32)
        Bim = scratch([128, 16, 16], F32)
        CCre = scratch([128, 16, 16], F32)
        CCim = scratch([128, 16, 16], F32)
        Dcol = scratch([128, 32], F32)
        for gh in range(2):
            sl = slice(64 * gh, 64 * gh + 64)
            gs = slice(16 * gh, 16 * gh + 16)
            dma(lambda en, sl=sl, gs=gs: en.dma_start(out=Are[sl, :], in_=A_re[gs, :].rearrange("g p -> p g"),
                                                      allow_slow_non_contiguous=True), [], ['Are'])
            dma(lambda en, sl=sl, gs=gs: en.dma_start(out=Aim[sl, :], in_=A_im[gs, :].rearrange("g p -> p g"),
                                                      allow_slow_non_contiguous=True), [], ['Aim'])
            dma(lambda en, sl=sl, gs=gs: en.dma_start(out=dtl[sl, :], in_=log_dt[0:1, gs].broadcast_to([64, 16])),
                [], ['dtl'])
            dma(lambda en, sl=sl, gs=gs: en.dma_start(out=Bre[sl, :, :], in_=B_re[gs].rearrange("g p c -> p g c")),
                [], ['Bre'])
            dma(lambda en, sl=sl, gs=gs: en.dma_start(out=Bim[sl, :, :], in_=B_im[gs].rearrange("g p c -> p g c")),
                [], ['Bim'])
        for s_ in range(8):
            dma(lambda en, s_=s_: en.dma_start(out=Dcol[16 * s_:16 * s_ + 16, :], in_=Dp.rearrange("g c -> c g"),
                                               allow_slow_non_contiguous=True), [], ['Dcol'])
        Cst = scratch([128, 8, 64], F32)
        for ri, Csrc in enumerate([C_re, C_im]):
            for j in range(4):
                dma(lambda en, ri=ri, j=j, Csrc=Csrc: en.dma_start(out=Cst[:, ri * 4 + j, :],
                                                                   in_=Csrc[j * 128:(j + 1) * 128, :]),
                    [], ['Cst%d' % (ri * 4 + j)])
        for ri, CC in enumerate([CCre, CCim]):
            b = bank()
            for j in range(4):
                gh = j // 2
                op('pe', lambda en, ri=ri, j=j, gh=gh, b=b: en.matmul(
                    psf[b][64 * gh:64 * gh + 64, (j % 2) * 128:(j % 2) * 128 + 128],
                    lhsT=Cst[:, ri * 4 + j, :], rhs=identf, start=True, stop=True),
                   ['Cst%d' % (ri * 4 + j), 'identf'], ['ps%d' % b])
            op('dve', lambda en, CC=CC, b=b: en.tensor_copy(
                out=CC, in_=psf[b][:, 0:256].rearrange("p (g c) -> p g c", c=16)), ['ps%d' % b], ['CC%d' % ri])

        def dv(fn, r, w, e='dve'):
            op(e, fn, r, w)

        lre = scratch([128, 16], F32)
        dtv = scratch([128, 16], F32)
        lrd = scratch([128, 16], F32)
        thd = scratch([128, 16], F32)
        op('act', lambda en: en.activation(out=dtv, in_=dtl, func=AF.Exp), ['dtl'], ['dtv'])
        dv(lambda en: en.tensor_scalar(out=lre, in0=Are, scalar1=-1e-4, scalar2=None, op0=ALU.min), ['Are'], ['lre'])
        dv(lambda en: en.tensor_mul(out=lrd, in0=lre, in1=dtv), ['lre', 'dtv'], ['lrd'])
        dv(lambda en: en.tensor_mul(out=thd, in0=Aim, in1=dtv), ['Aim', 'dtv'], ['thd'])
        KV = [0, -1, -2, -3, -4, -5, -6, -7] + list(range(0, 9)) + [7, 6, 5, 4, 3, 2, 1, 0]
        NK = len(KV)
        kv = scratch([128, NK], F32)
        for i, kval in enumerate(KV):
            op('pool', lambda en, i=i, kval=kval: en.memset(kv[:, i:i + 1], float(kval)), [], ['kv'])
        mag = scratch([128, NK, 16], F32)
        ang = scratch([128, 2, NK, 16], F32)
        angn = scratch([128, 2, NK, 16], F32)
        angi = scratch([128, 2, NK, 16], I32)
        kvb = kv.unsqueeze(2).broadcast_to([128, NK, 16])
        dv(lambda en: en.tensor_tensor(out=mag, in0=lrd.unsqueeze(1).broadcast_to([128, NK, 16]), in1=kvb,
                                       op=ALU.mult), ['lrd', 'kv'], ['mag'])
        op('act', lambda en: en.activation(out=mag, in_=mag, func=AF.Exp), ['mag'], ['mag'])
        dv(lambda en: en.tensor_tensor(out=ang[:, 0], in0=thd.unsqueeze(1).broadcast_to([128, NK, 16]), in1=kvb,
                                       op=ALU.mult), ['thd', 'kv'], ['ang'])
        OFFS = TWO_PI * 40
        dv(lambda en: en.tensor_scalar(out=ang[:, 1], in0=ang[:, 0], scalar1=OFFS + float(np.pi / 2), scalar2=None,
                                       op0=ALU.add), ['ang'], ['ang'])
        dv(lambda en: en.tensor_scalar(out=ang[:, 0], in0=ang[:, 0], scalar1=OFFS, scalar2=None, op0=ALU.add),
           ['ang'], ['ang'])
        dv(lambda en: en.tensor_scalar(out=angn, in0=ang, scalar1=1.0 / TWO_PI, scalar2=None, op0=ALU.mult),
           ['ang'], ['angn'])
        dv(lambda en: en.tensor_copy(out=angi, in_=angn), ['angn'], ['angi'])
        dv(lambda en: en.tensor_copy(out=angn, in_=angi), ['angi'], ['angn'])
        dv(lambda en: en.scalar_tensor_tensor(out=ang, in0=angn, scalar=-TWO_PI, in1=ang, op0=ALU.mult, op1=ALU.add),
           ['angn', 'ang'], ['ang'])
        dv(lambda en: en.tensor_scalar(out=angn, in0=ang, scalar1=float(np.pi), scalar2=-TWO_PI, op0=ALU.is_gt,
                                       op1=ALU.mult), ['ang'], ['angn'])
        dv(lambda en: en.tensor_add(out=ang, in0=ang, in1=angn), ['ang', 'angn'], ['ang'])
        dv(lambda en: en.tensor_scalar(out=ang, in0=ang, scalar1=float(np.pi), scalar2=-float(np.pi), op0=ALU.min,
                                       op1=ALU.max), ['ang'], ['ang'])
        op('act', lambda en: en.activation(out=ang, in_=ang, func=AF.Sin), ['ang'], ['ang'])
        AR = scratch([128, NK, 16], F32)
        AI = scratch([128, NK, 16], F32)
        dv(lambda en: en.tensor_mul(out=AI, in0=mag, in1=ang[:, 0]), ['mag', 'ang'], ['AI'])
        dv(lambda en: en.tensor_mul(out=AR, in0=mag, in1=ang[:, 1]), ['mag', 'ang'], ['AR'])
        K1 = 9
        am1 = scratch([128, 16], F32)
        t_a = scratch([128, 16], F32)
        t_b = scratch([128, 16], F32)
        dn = scratch([128, 16], F32)
        cr = scratch([128, 16], F32)
        ci = scratch([128, 16], F32)
        dv(lambda en: en.tensor_scalar(out=am1, in0=AR[:, K1, :], scalar1=-1.0, scalar2=None, op0=ALU.add),
           ['AR'], ['am1'])
        dv(lambda en: en.tensor_mul(out=dn, in0=lre, in1=lre), ['lre'], ['dn'])
        dv(lambda en: en.tensor_mul(out=t_a, in0=Aim, in1=Aim), ['Aim'], ['t_a'])
        dv(lambda en: en.tensor_add(out=dn, in0=dn, in1=t_a), ['dn', 't_a'], ['dn'])
        dv(lambda en: en.reciprocal(out=dn, in_=dn), ['dn'], ['dn'])
        dv(lambda en: en.tensor_mul(out=t_a, in0=am1, in1=lre), ['am1', 'lre'], ['t_a'])
        dv(lambda en: en.tensor_mul(out=t_b, in0=AI[:, K1, :], in1=Aim), ['AI', 'Aim'], ['t_b'])
        dv(lambda en: en.tensor_add(out=cr, in0=t_a, in1=t_b), ['t_a', 't_b'], ['cr'])
        dv(lambda en: en.tensor_mul(out=cr, in0=cr, in1=dn), ['cr', 'dn'], ['cr'])
        dv(lambda en: en.tensor_mul(out=t_a, in0=AI[:, K1, :], in1=lre), ['AI', 'lre'], ['t_a'])
        dv(lambda en: en.tensor_mul(out=t_b, in0=am1, in1=Aim), ['am1', 'Aim'], ['t_b'])
        dv(lambda en: en.tensor_sub(out=ci, in0=t_a, in1=t_b), ['t_a', 't_b'], ['ci'])
        dv(lambda en: en.tensor_mul(out=ci, in0=ci, in1=dn), ['ci', 'dn'], ['ci'])
        bbr = scratch([128, 16, 16], F32)
        bbi = scratch([128, 16, 16], F32)
        tq = scratch([128, 16, 16], F32)
        crb = cr.unsqueeze(2).broadcast_to([128, 16, 16])
        cib = ci.unsqueeze(2).broadcast_to([128, 16, 16])
        dv(lambda en: en.tensor_tensor(out=bbr, in0=Bre, in1=crb, op=ALU.mult), ['Bre', 'cr'], ['bbr'])
        dv(lambda en: en.tensor_tensor(out=tq, in0=Bim, in1=cib, op=ALU.mult), ['Bim', 'ci'], ['tq'])
        dv(lambda en: en.tensor_sub(out=bbr, in0=bbr, in1=tq), ['bbr', 'tq'], ['bbr'])
        dv(lambda en: en.tensor_tensor(out=bbi, in0=Bim, in1=crb, op=ALU.mult), ['Bim', 'cr'], ['bbi'])
        dv(lambda en: en.tensor_tensor(out=tq, in0=Bre, in1=cib, op=ALU.mult), ['Bre', 'ci'], ['tq'])
        dv(lambda en: en.tensor_add(out=bbi, in0=bbi, in1=tq), ['bbi', 'tq'], ['bbi'])

        def cprod(outr, outi, k0, Mr, Mi, nMr, nMi, nr, ni, neg_im=False, eng='dve'):
            pr = AR[:, k0:k0 + 8, :].transpose([0, 2, 1]).unsqueeze(3).broadcast_to([128, 16, 8, 16])
            pi = AI[:, k0:k0 + 8, :].transpose([0, 2, 1]).unsqueeze(3).broadcast_to([128, 16, 8, 16])
            mr = Mr.unsqueeze(2).broadcast_to([128, 16, 8, 16])
            mi = Mi.unsqueeze(2).broadcast_to([128, 16, 8, 16])
            tmp = cp_tmp
            op(eng, lambda en: en.tensor_tensor(out=outr, in0=pr, in1=mr, op=ALU.mult), ['AR', nMr], [nr])
            op(eng, lambda en: en.tensor_tensor(out=tmp, in0=pi, in1=mi, op=ALU.mult), ['AI', nMi], ['cpt'])
            op(eng, lambda en: en.tensor_sub(out=outr, in0=outr, in1=tmp), [nr, 'cpt'], [nr])
            op(eng, lambda en: en.tensor_tensor(out=outi, in0=pr, in1=mi, op=ALU.mult), ['AR', nMi], [ni])
            op(eng, lambda en: en.tensor_tensor(out=tmp, in0=pi, in1=mr, op=ALU.mult), ['AI', nMr], ['cpt'])
            if neg_im:
                op(eng, lambda en: en.scalar_tensor_tensor(out=outi, in0=outi, scalar=-1.0, in1=tmp, op0=ALU.mult,
                                                           op1=ALU.subtract), [ni, 'cpt'], [ni])
            else:
                op(eng, lambda en: en.tensor_add(out=outi, in0=outi, in1=tmp), [ni, 'cpt'], [ni])

        cp_tmp = scratch([128, 16, 8, 16], F32)
        Lr = scratch([128, 16, 8, 16], F32)
        Li = scratch([128, 16, 8, 16], F32)
        Rr = scratch([128, 16, 8, 16], F32)
        Ri = scratch([128, 16, 8, 16], F32)
        cprod(Lr, Li, 0, bbr, bbi, 'bbr', 'bbi', 'Lr', 'Li')
        cprod(Rr, Ri, 8, CCre, CCim, 'CC0', 'CC1', 'Rr', 'Ri', neg_im=True)
        maskT = scratch([128, 128], F32)
        op('pool', lambda en: en.memset(maskT, 1.0), [], ['maskT'])
        op('pool', lambda en: en.affine_select(out=maskT.rearrange("p (t c) -> p t c", c=16),
                                               in_=maskT.rearrange("p (t c) -> p t c", c=16),
                                               compare_op=ALU.is_ge, fill=0.0, base=15,
                                               pattern=[[16, 8], [0, 16]], channel_multiplier=-1),
           ['maskT'], ['maskT'])
        ttmp = scratch([128, 2, 128], F32)
        for g in range(32):
            gh, g16 = g // 16, g % 16
            sl = slice(64 * gh, 64 * gh + 64)
            b = bank()
            op('pe', lambda en, b=b, sl=sl, g16=g16: en.matmul(
                psf[b][:, 0:128], lhsT=Lr[sl, g16].rearrange("p s c -> p (s c)"),
                rhs=Rr[sl, g16].rearrange("p s c -> p (s c)"), start=True, stop=False),
               ['Lr', 'Rr'], ['ps%d' % b])
            op('pe', lambda en, b=b, sl=sl, g16=g16: en.matmul(
                psf[b][:, 0:128], lhsT=Li[sl, g16].rearrange("p s c -> p (s c)"),
                rhs=Ri[sl, g16].rearrange("p s c -> p (s c)"), start=False, stop=True),
               ['Li', 'Ri'], ['ps%d' % b])
            j = g % 2
            op('dve', lambda en, b=b, j=j: en.tensor_tensor(out=ttmp[:, j, :], in0=psf[b][:, 0:128], in1=maskT,
                                                            op=ALU.mult), ['ps%d' % b, 'maskT'], ['ttmp%d' % j])
            op('dve', lambda en, g=g, j=j: en.scalar_tensor_tensor(out=Tm[:, g, :], in0=identf,
                                                                   scalar=Dcol[:, g:g + 1], in1=ttmp[:, j, :],
                                                                   op0=ALU.mult, op1=ALU.add),
               ['ttmp%d' % j, 'identf', 'Dcol'], ['Tm'])
        cprod(Rr, Ri, 9, CCre, CCim, 'CC0', 'CC1', 'Rr', 'Ri', neg_im=True)
        op('dve', lambda en: en.tensor_copy(out=Qm[:, :, 0, :], in_=Rr.rearrange("p g t c -> p g (t c)")),
           ['Rr'], ['Qm'])
        op('dve', lambda en: en.tensor_copy(out=Qm[:, :, 1, :], in_=Ri.rearrange("p g t c -> p g (t c)")),
           ['Ri'], ['Qm'])
        cprod(Lr, Li, 17, bbr, bbi, 'bbr', 'bbi', 'Lr', 'Li')
        for g in range(32):
            gh, g16 = g // 16, g % 16
            sl = slice(64 * gh, 64 * gh + 64)
            b = bank()
            for ri, LL in enumerate([Lr, Li]):
                op('pe', lambda en, b=b, sl=sl, g16=g16, ri=ri, LL=LL: en.matmul(
                    psf[b][:, ri * 64:ri * 64 + 64], lhsT=LL[sl, g16].rearrange("p s c -> p (s c)"),
                    rhs=identf[sl, sl], start=True, stop=True), ['Lr', 'Li', 'identf'], ['ps%d' % b])
            copy_any(Pm[:, g, :], psf[b][:, 0:128], ['ps%d' % b], ['Pm'])
        K8 = 16
        for blk in range(2):
            op('dve', lambda en, blk=blk: en.tensor_copy(
                out=C1[:, blk], in_=AR[:, K8, :].unsqueeze(2).broadcast_to([128, 16, 4])), ['AR'], ['C1'])
        op('dve', lambda en: en.tensor_scalar(out=C2[:, 0], in0=AI[:, K8, :].unsqueeze(2).broadcast_to([128, 16, 4]),
                                              scalar1=-1.0, scalar2=None, op0=ALU.mult), ['AI'], ['C2'])
        op('dve', lambda en: en.tensor_copy(out=C2[:, 1], in_=AI[:, K8, :].unsqueeze(2).broadcast_to([128, 16, 4])),
           ['AI'], ['C2'])

        i_prep1 = len(P.ops)
        convs = P.ops[i_conv0:i_conv1]
        preps = P.ops[i_conv1:i_prep1]
        P.ops[i_conv0:i_prep1] = merge_ops(preps, convs)
        P.barrier()
        op('pool', lambda en: en.memset(Pprev, 0.0), [], ['Pprev0', 'Pprev1'])
        op('pool', lambda en: en.memset(Pcur, 0.0), [], ['Pcur0', 'Pcur1'])
        op('pool', lambda en: en.memset(Pmeta, 0.0), [], ['Pmeta0', 'Pmeta1'])
        op('pool', lambda en: en.memset(Vt, 1.0), [], ['Vt'])
        P.barrier()
        bank_ctr['front'] = True

        def ssm_recurrence(NS, NCH, z0_pp):
            F = 16 * NS
            Vv = Vs[:, :, :, 0:NS * NCH].rearrange("p r g (s k) -> p r g s k", k=NCH)
            Hv = Hb[:, :, :, 0:NS * NCH].rearrange("p r g (s k) -> p r g s k", k=NCH)
            c1 = C1[:, :, :, 0:NS]
            c2 = C2[:, :, :, 0:NS]
            pp = z0_pp
            for k in range(NCH):
                zc = Z[:, pp, :, 0:F].rearrange("p b (g s) -> p b g s", s=NS)
                zn = Z[:, 1 - pp, :, 0:F].rearrange("p b (g s) -> p b g s", s=NS)
                t1 = T1[:, :, 0:F].rearrange("p b (g s) -> p b g s", s=NS)
                t2 = T2[:, :, 0:F].rearrange("p b (g s) -> p b g s", s=NS)
                op('pool', lambda en, zc=zc, k=k: en.tensor_copy(out=Hv[:, :, :, :, k], in_=zc[:, 0:2]),
                   ['Z%d' % pp], ['Hb'])
                op('pool', lambda en, zc=zc, t1=t1: en.tensor_tensor(out=t1, in0=zc[:, 0:2], in1=c1, op=ALU.mult),
                   ['Z%d' % pp, 'C1'], ['T1'])
                op('pool', lambda en, zc=zc, t2=t2: en.tensor_tensor(out=t2, in0=zc[:, 1:3], in1=c2, op=ALU.mult),
                   ['Z%d' % pp, 'C2'], ['T2'])
                op('pool', lambda en, t1=t1, t2=t2: en.tensor_add(out=t1, in0=t1, in1=t2), ['T1', 'T2'], ['T1'])
                op('pool', lambda en, zn=zn, t1=t1, k=k: en.tensor_add(out=zn[:, 0:2], in0=t1, in1=Vv[:, :, :, :, k]),
                   ['T1', 'Vs'], ['Z%d' % (1 - pp)])
                op('pool', lambda en, zn=zn: en.tensor_copy(out=zn[:, 2], in_=zn[:, 0]),
                   ['Z%d' % (1 - pp)], ['Z%d' % (1 - pp)])
                pp = 1 - pp
            return pp

        def norm_transpose(NS, TT, dst, dname):
            for m in range(NS):
                xb = m % 2
                op('act', lambda en, m=m, xb=xb: en.activation(out=xnb[0:TT, xb], in_=xt[0:TT, m, :], func=AF.Square,
                                                               accum_out=ss[0:TT, m:m + 1]),
                   ['xt%d' % m], ['xnb%d' % xb, 'ss%d' % m])
                rstd_from(ss[0:TT, m:m + 1], rs[0:TT, m:m + 1], D, 'ss%d' % m, 'rs%d' % m)
                op('dve', lambda en, m=m, xb=xb: en.tensor_scalar(out=xnb[0:TT, xb], in0=xt[0:TT, m, :],
                                                                  scalar1=rs[0:TT, m:m + 1], scalar2=None,
                                                                  op0=ALU.mult),
                   ['xt%d' % m, 'rs%d' % m], ['xnb%d' % xb])
                b = bank()
                for kc in range(8):
                    op('pe', lambda en, b=b, kc=kc, xb=xb: en.transpose(
                        out=psb[b][:, kc * 128:kc * 128 + TT], in_=xnb[0:TT, xb, kc * 128:(kc + 1) * 128],
                        identity=identb[0:TT, 0:TT]), ['xnb%d' % xb, 'identb'], ['ps%d' % b])
                copy_any(dst[:, :, m * TT:(m + 1) * TT],
                         psb[b].rearrange("p (k t) -> p k t", t=128)[:, :, 0:TT], ['ps%d' % b], ['%s%d' % (dname, m)])

        XMALL_ = ['XM0', 'XM1', 'XM2', 'XM3']
        US_ = ['US'] if overlap else []
        UZ_ = ['UZ'] if overlap else []
        YA_ = ['yA0', 'yA1', 'yA2', 'yA3']
        HNALL_ = ['XM0', 'XM1', 'XM2', 'XM3']
        zstate = {'pp': 0}

        def block(kind, j, phase):
            if kind == 'P':
                NS, TT = 4, 128
            elif kind == 'S':
                NS, TT = 4, 64
            else:
                NS, TT = 1, 16
            NCH = TT // 8
            NQ = NS * NCH
            NT = NS * TT
            slot = j % 2 if kind == 'P' else 0
            pslot = 1 - slot
            full = kind != 'M'

            def ydst(m):
                if kind == 'P':
                    return yp[m, j * 128:(j + 1) * 128, :]
                return ys[m, :, :]

            def xsrc(m):
                if kind == 'P':
                    return xp[m, j * 128:(j + 1) * 128, :]
                if kind == 'S':
                    return xs[m, :, :]
                return meta

            def f1():
                for m in range(NS):
                    dma(lambda en, m=m: en.dma_start(out=xt[0:TT, m, :], in_=xsrc(m)), [], ['xt%d' % m])
                norm_transpose(NS, TT, XM, 'XM')

                if stop == 5.01 and kind == 'P':
                    return True
                for s_ in range(8):
                    b = bank()
                    for kc in range(8):
                        op('pe', lambda en, b=b, kc=kc, s_=s_: en.matmul(
                            psf[b][0:NQ, :], lhsT=XM[:, kc, 0:NT].rearrange("p (q s) -> p q s", s=8)[:, :, s_],
                            rhs=win[:, kc, 768:1280], start=(kc == 0), stop=(kc == 7)), XMALL_ + ['win'], ['ps%d' % b])
                    copy_any(u_ks[0:NQ, :, s_, :], psf[b][0:NQ, :].rearrange("p (g c) -> p g c", c=16),
                             ['ps%d' % b], ['u_ks', *US_])
                if stop == 5.05 and kind == 'P':
                    return True
                def qkv_mm(m):
                        bq = bank()
                        bk = bank()
                        for kc in range(8):
                            op('pe', lambda en, kc=kc, m=m, bq=bq: en.matmul(
                                psf[bq][0:TT, :], lhsT=XM[:, kc, m * TT:(m + 1) * TT], rhs=win[:, kc, 0:512],
                                start=(kc == 0), stop=(kc == 7)), ['XM%d' % m, 'win'], ['ps%d' % bq])
                        for kc in range(8):
                            op('pe', lambda en, kc=kc, m=m, bk=bk: en.matmul(
                                psf[bk][0:TT, 0:256], lhsT=XM[:, kc, m * TT:(m + 1) * TT], rhs=win[:, kc, 512:768],
                                start=(kc == 0), stop=(kc == 7)), ['XM%d' % m, 'win'], ['ps%d' % bk])
                        return bq, bk

                def qk_chain(m, bq, bk):
                        mp = m % 2
                        op('act', lambda en, bq=bq: en.activation(out=qsq[0:TT, 0:512], in_=psf[bq][0:TT, :], func=AF.Square),
                           ['ps%d' % bq], ['qsq'])
                        op('act', lambda en, bk=bk: en.activation(out=qsq[0:TT, 512:640], in_=psf[bk][0:TT, 0:128],
                                                                  func=AF.Square), ['ps%d' % bk], ['qsq'])
                        op('dve', lambda en, mp=mp: en.tensor_reduce(
                            out=st10[0:TT, mp, :], in_=qsq[0:TT, :].rearrange("p (h d) -> p h d", d=64), axis=AX.X,
                            op=ALU.add), ['qsq'], ['st10%d' % mp])
                        rstd_from(st10[0:TT, mp, :], r10[0:TT, mp, :], 64, 'st10%d' % mp, 'r10%d' % mp)
                        op('dve', lambda en, mp=mp, bq=bq: en.tensor_tensor(
                            out=qnb[0:TT, mp, :].rearrange("p (h d) -> p h d", d=64),
                            in0=psf[bq][0:TT, :].rearrange("p (h d) -> p h d", d=64),
                            in1=r10[0:TT, mp, 0:8].unsqueeze(2).broadcast_to([TT, 8, 64]), op=ALU.mult),
                           ['ps%d' % bq, 'r10%d' % mp], ['qnb%d' % mp])
                        op('dve', lambda en, mp=mp, bk=bk: en.tensor_tensor(
                            out=kf[0:TT, mp, :].rearrange("p (h d) -> p h d", d=64),
                            in0=psf[bk][0:TT, 0:128].rearrange("p (h d) -> p h d", d=64),
                            in1=r10[0:TT, mp, 8:10].unsqueeze(2).broadcast_to([TT, 2, 64]), op=ALU.mult),
                           ['ps%d' % bk, 'r10%d' % mp], ['kf%d' % mp])
                        op('dve', lambda en, mp=mp: en.tensor_tensor(out=kf[0:TT, mp, :], in0=kf[0:TT, mp, :],
                                                                     in1=gk_t[0:TT].rearrange("p a d -> p (a d)"),
                                                                     op=ALU.mult), ['kf%d' % mp, 'gk_t'], ['kf%d' % mp])
                        op('act', lambda en, mp=mp: en.copy(out=kb[0:TT, mp, :], in_=kf[0:TT, mp, :]),
                           ['kf%d' % mp], ['kb%d' % mp])
                        vdst = Vm[0:TT, :, 0:64] if kind == 'M' else Vt[0:TT, slot, m, :, 0:64]
                        vname = 'Vm' if kind == 'M' else 'Vt%d_%d' % (slot, m)
                        op('act', lambda en, bk=bk, vdst=vdst: en.copy(
                            out=vdst, in_=psf[bk][0:TT, 128:256].rearrange("p (h d) -> p h d", d=64)),
                           ['ps%d' % bk], [vname])
                        need_out = (kind != 'P') or (j == NBLK - 1)
                        if need_out:
                            op('dve', lambda en, bk=bk: en.tensor_copy(out=vf[0:TT, :], in_=psf[bk][0:TT, 128:256]),
                               ['ps%d' % bk], ['vf'])
                            if kind == 'P':
                                dma(lambda en, m=m, mp=mp: en.dma_start(out=kpo[m, 16:144, :], in_=kf[0:TT, mp, :]),
                                    ['kf%d' % mp], [])
                                dma(lambda en, m=m: en.dma_start(out=vpo[m, 16:144, :], in_=vf[0:TT, :]), ['vf'], [])
                            elif kind == 'S':
                                dma(lambda en, m=m, mp=mp: en.dma_start(out=kso[m, :, :], in_=kf[0:TT, mp, :]),
                                    ['kf%d' % mp], [])
                                dma(lambda en, m=m: en.dma_start(out=vso[m, :, :], in_=vf[0:TT, :]), ['vf'], [])
                            else:
                                for mm in range(NSEQ):
                                    dma(lambda en, mm=mm, mp=mp: en.dma_start(out=kpo[mm, 0:16, :], in_=kf[0:TT, mp, :]),
                                        ['kf%d' % mp], [])
                                    dma(lambda en, mm=mm: en.dma_start(out=vpo[mm, 0:16, :], in_=vf[0:TT, :]), ['vf'], [])
                        b = bank()
                        for hk in range(2):
                            op('pe', lambda en, b=b, hk=hk, mp=mp: en.transpose(
                                out=psb[b][0:64, hk * 128:hk * 128 + TT], in_=kb[0:TT, mp, hk * 64:(hk + 1) * 64],
                                identity=identb[0:TT, 0:TT]), ['kb%d' % mp, 'identb'], ['ps%d' % b])
                        if kind == 'M':
                            ktd = KTm[:, :, 0:TT]
                            ktn = 'KTm'
                        else:
                            ktd = KT[:, :, slot, m, 0:TT]
                            ktn = 'KT%d_%d' % (slot, m)
                        op('act', lambda en, b=b, ktd=ktd: en.mul(
                            out=ktd, in_=psb[b][0:64, 0:256].rearrange("p (h t) -> p h t", t=128)[:, :, 0:TT],
                            mul=gq8[:, 0:1]), ['ps%d' % b, 'gq8'], [ktn])
                        if full:
                            b = bank()
                            for h in range(8):
                                op('pe', lambda en, b=b, h=h, mp=mp: en.transpose(
                                    out=psb[b][0:64, h * 128:h * 128 + TT], in_=qnb[0:TT, mp, h * 64:(h + 1) * 64],
                                    identity=identb[0:TT, 0:TT]), ['qnb%d' % mp, 'identb'], ['ps%d' % b])
                            op('dve', lambda en, b=b, m=m: en.tensor_copy(
                                out=QT[:, m, :, 0:TT], in_=psb[b][0:64, :].rearrange("p (h t) -> p h t", t=128)[:, :, 0:TT]),
                               ['ps%d' % b], ['QT%d' % m])


                def stage_qk():
                    if overlap:
                        for m in range(NS):
                            cur = qkv_mm(m)
                            qk_chain(m, *cur)
                    else:
                        prev = None
                        for m in range(NS):
                            cur = qkv_mm(m)
                            if prev is not None:
                                qk_chain(m - 1, *prev)
                            prev = cur
                        qk_chain(NS - 1, *prev)

                if stop == 5.1 and kind == 'P':
                    return True
                for g0 in range(0, 32, 8):
                    b = bank()
                    for g in range(g0, g0 + 8):
                        op('pe', lambda en, b=b, g=g: en.transpose(
                            out=psb[b][:, (g % 8) * 64:(g % 8) * 64 + NQ],
                            in_=u_ks[0:NQ, g].rearrange("p s c -> p (s c)"), identity=identb[0:NQ, 0:NQ]),
                           ['u_ks', *US_, 'identb'], ['ps%d' % b])
                    copy_any(U_all[:, g0:g0 + 8, 0:NQ],
                             psb[b][:, 0:512].rearrange("p (g q) -> p g q", q=64)[:, :, 0:NQ], ['ps%d' % b], ['U_all', *UZ_])
                for gb in range(4):
                    b = bank()
                    for gh in range(2):
                        for gl in range(4):
                            g16 = gb * 4 + gl
                            g = gh * 16 + g16
                            for ri in range(2):
                                op('pe', lambda en, b=b, g=g, gh=gh, gl=gl, ri=ri: en.matmul(
                                    psf[b][64 * gh:64 * gh + 64, (ri * 4 + gl) * 64:(ri * 4 + gl) * 64 + NQ],
                                    lhsT=Pm[:, g, ri * 64:(ri + 1) * 64], rhs=U_all[:, g, 0:NQ], start=True, stop=True),
                                   ['Pm', 'U_all', *UZ_], ['ps%d' % b])
                    copy_any(Vs[:, :, gb * 4:gb * 4 + 4, 0:NQ],
                             psf[b].rearrange("p (r g q) -> p r g q", r=2, q=64)[:, :, :, 0:NQ], ['ps%d' % b], ['Vs'])
                if stop == 5.11 and kind == 'P':
                    return True
                F = 16 * NS
                if kind == 'M':
                    op('pool', lambda en: en.memset(Z[:, 0], 0.0), [], ['Z0'])
                    zstate['pp'] = 0
                elif kind == 'P' and j == 0:
                    zv = Z[:, 0, :, 0:F].rearrange("p b (g s) -> p b g s", s=NS)
                    op('pool', lambda en, zv=zv: en.tensor_copy(
                        out=zv, in_=Zmeta.unsqueeze(3).broadcast_to([128, 3, 16, NS])), ['Zmeta'], ['Z0'])
                    zstate['pp'] = 0
                elif kind == 'S':
                    b = bank()
                    for ri, src in enumerate([s_re, s_im]):
                        dma(lambda en, ri=ri, src=src: en.dma_start(out=attn[:, ri, 0:64], in_=src), [], ['attn%d' % ri])
                        for gh in range(2):
                            op('pe', lambda en, b=b, ri=ri, gh=gh: en.matmul(
                                psf[b][64 * gh:64 * gh + 64, ri * 128:ri * 128 + 128], lhsT=attn[:, ri, 0:64],
                                rhs=identf, start=True, stop=True), ['attn%d' % ri, 'identf'], ['ps%d' % b])
                    for gh in range(2):
                        sl = slice(64 * gh, 64 * gh + 64)
                        for blk, ri in enumerate([0, 1, 0]):
                            src = psf[b][sl, ri * 128:ri * 128 + 128].rearrange("p (s g) -> p g s", g=32)[:, 16 * gh:16 * gh + 16, :]
                            op('dve', lambda en, sl=sl, blk=blk, src=src: en.tensor_copy(
                                out=Z[sl, 0, blk, 0:64].rearrange("p (g s) -> p g s", s=4), in_=src),
                               ['ps%d' % b], ['Z0'])
                    zstate['pp'] = 0
                if stop == 5.12 and kind == 'P':
                    return True
                pp_end = ssm_recurrence(NS, NCH, zstate['pp'])
                zstate['pp'] = pp_end
                if kind == 'M':
                    op('pool', lambda en: en.tensor_copy(out=Zmeta, in_=Z[:, pp_end, :, 0:16]), ['Z%d' % pp_end], ['Zmeta'])
                if stop == 5.13 and kind == 'P':
                    return True
                if kind != 'M' and (kind == 'S' or j == NBLK - 1):
                    o_re, o_im = (srs, sis) if kind == 'S' else (srp, sip)
                    for ri, dst in enumerate([o_re, o_im]):
                        op('dve', lambda en, ri=ri: en.tensor_copy(
                            out=qsq[:, ri * 64:(ri + 1) * 64].rearrange("p (s g) -> p s g", g=16),
                            in_=Z[:, pp_end, ri, 0:64].rearrange("p (g s) -> p s g", s=4)),
                           ['Z%d' % pp_end], ['qsq'])
                        b = bank()
                        op('pe', lambda en, b=b, ri=ri: en.matmul(
                            psf[b][0:64, 0:128], lhsT=qsq[:, ri * 64:(ri + 1) * 64],
                            rhs=identf, start=True, stop=True), ['qsq', 'identf'], ['ps%d' % b])
                        op('dve', lambda en, b=b, ri=ri: en.tensor_copy(out=attn[0:64, ri, 0:128], in_=psf[b][0:64, 0:128]),
                           ['ps%d' % b], ['attn%d' % ri])
                        for s_ in range(4 if stop != 5.14 else 0):
                            for gh in range(2):
                                dma(lambda en, s_=s_, gh=gh, ri=ri, dst=dst: en.dma_start(
                                    out=dst[s_ * 32 + gh * 16:s_ * 32 + gh * 16 + 16, :],
                                    in_=attn[s_ * 16:s_ * 16 + 16, ri, gh * 64:gh * 64 + 64]), ['attn%d' % ri], [])

                stage_qk()
                if kind == 'M':
                    return
                if stop in (5.2, 5.14) and kind == "P":
                    return True
                if kind == 'S':
                    for m in range(NS):
                        dma(lambda en, m=m: en.dma_start(out=kf[:, 0, :], in_=ck[m, 16:144, :]), [], ['kf0'])
                        dma(lambda en, m=m: en.dma_start(out=kf[0:16, 1, :], in_=ck[m, 0:16, :]), [], ['kf1'])
                        dma(lambda en, m=m: en.dma_start(out=vf[:, :], in_=cv[m, 16:144, :]), [], ['vf'])
                        dma(lambda en, m=m: en.dma_start(out=qsq[0:16, 0:128], in_=cv[m, 0:16, :]), [], ['qsq'])
                        op('pool', lambda en: en.tensor_copy(out=kb[:, 0, :], in_=kf[:, 0, :]), ['kf0'], ['kb0'])
                        op('pool', lambda en: en.tensor_copy(out=kb[0:16, 1, :], in_=kf[0:16, 1, :]), ['kf1'], ['kb1'])
                        op('act', lambda en, m=m: en.copy(out=Vt[:, 1, m, :, 0:64],
                                                          in_=vf[:, :].rearrange("p (h d) -> p h d", d=64)),
                           ['vf'], ['Vt1_%d' % m])
                        op('act', lambda en, m=m: en.copy(out=VmS[:, m, :, 0:64], in_=qsq[0:16, 0:128].rearrange("p (h d) -> p h d", d=64)),
                           ['qsq'], ['VmS%d' % m])
                        b = bank()
                        for hk in range(2):
                            op('pe', lambda en, b=b, hk=hk: en.transpose(
                                out=psb[b][0:64, hk * 128:hk * 128 + 128], in_=kb[:, 0, hk * 64:(hk + 1) * 64],
                                identity=identb), ['kb0', 'identb'], ['ps%d' % b])
                            op('pe', lambda en, b=b, hk=hk: en.transpose(
                                out=psb[b][0:64, 256 + hk * 16:256 + hk * 16 + 16], in_=kb[0:16, 1, hk * 64:(hk + 1) * 64],
                                identity=identb[0:16, 0:16]), ['kb1', 'identb'], ['ps%d' % b])
                        op('act', lambda en, b=b, m=m: en.mul(
                            out=KT[:, :, 1, m, :], in_=psb[b][0:64, 0:256].rearrange("p (h t) -> p h t", t=128),
                            mul=gq8[:, 0:1]), ['ps%d' % b, 'gq8'], ['KT1_%d' % m])
                        op('act', lambda en, b=b, m=m: en.mul(
                            out=KTmS[:, m], in_=psb[b][0:64, 256:288].rearrange("p (h t) -> p h t", t=16),
                            mul=gq8[:, 0:1]), ['ps%d' % b, 'gq8'], ['KTmS%d' % m])
                has_prev = (kind == 'S') or (j > 0)
                NQC = 4 * TT

                def att_S(ui, m, hk):
                    c = dict(m=m, hk=hk, pset=ui % 2)
                    qrhs = QT[:, m, 4 * hk:4 * hk + 4, 0:TT]
                    bp = bank() if has_prev else None
                    bc = bank()
                    bm = bank()
                    c.update(bp=bp, bc=bc, bm=bm)
                    if has_prev:
                        op('pe', lambda en: en.matmul(
                            psf[bp][:, 0:NQC], lhsT=KT[:, hk, pslot, m, :], rhs=qrhs, start=True, stop=True),
                           ['KT%d_%d' % (pslot, m), 'QT%d' % m], ['ps%d' % bp])
                    op('pe', lambda en: en.matmul(
                        psf[bc][0:TT, 0:NQC], lhsT=KT[:, hk, slot, m, 0:TT], rhs=qrhs, start=True, stop=True),
                       ['KT%d_%d' % (slot, m), 'QT%d' % m], ['ps%d' % bc])
                    if kind == 'S':
                        ktm = KTmS[:, m, hk, :]
                        ktmn = 'KTmS%d' % m
                        c.update(vmv=VmS[:, m, hk, :], vmn='VmS%d' % m)
                    else:
                        ktm = KTm[:, hk, :]
                        ktmn = 'KTm'
                        c.update(vmv=Vm[:, hk, :], vmn='Vm')
                    op('pe', lambda en: en.matmul(
                        psf[bm][0:16, 0:NQC], lhsT=ktm, rhs=qrhs, start=True, stop=True),
                       [ktmn, 'QT%d' % m], ['ps%d' % bm])
                    return c

                def att_exp(c):
                    pset, bp, bc, bm = c['pset'], c['bp'], c['bc'], c['bm']
                    pv = Pprev[:, pset, :, 0:TT]
                    pc = Pcur[:, pset, :, 0:TT]
                    pm_ = Pmeta[:, pset, :, 0:TT]
                    c.update(pv=pv, pc=pc, pm_=pm_)
                    if has_prev:
                        sp = psf[bp][:, 0:NQC].rearrange("p (h t) -> p h t", t=TT)
                        if kind == 'P':
                            op('act', lambda en: en.activation(out=pv[64:128], in_=sp[64:128], func=AF.Exp),
                               ['ps%d' % bp], ['Pprev%d' % pset])
                            op('act', lambda en: en.activation(out=pv[0:64, :, 0:64], in_=sp[0:64, :, 0:64], func=AF.Exp),
                               ['ps%d' % bp], ['Pprev%d' % pset])
                        else:
                            op('act', lambda en: en.activation(out=pv, in_=sp, func=AF.Exp),
                               ['ps%d' % bp], ['Pprev%d' % pset])
                    sc_ = psf[bc][0:TT, 0:NQC].rearrange("p (h t) -> p h t", t=TT)
                    if kind == 'P':
                        op('act', lambda en: en.activation(out=pc[0:64], in_=sc_[0:64], func=AF.Exp),
                           ['ps%d' % bc], ['Pcur%d' % pset])
                        op('act', lambda en: en.activation(out=pc[64:128, :, 64:128], in_=sc_[64:128, :, 64:128],
                                                           func=AF.Exp), ['ps%d' % bc], ['Pcur%d' % pset])
                    else:
                        op('act', lambda en: en.activation(out=pc[0:TT], in_=sc_, func=AF.Exp),
                           ['ps%d' % bc], ['Pcur%d' % pset])
                    sm = psf[bm][0:16, 0:NQC].rearrange("p (h t) -> p h t", t=TT)
                    op('act', lambda en: en.activation(out=pm_, in_=sm, func=AF.Exp), ['ps%d' % bm], ['Pmeta%d' % pset])

                def att_PV(c):
                    m, hk, pset = c['m'], c['hk'], c['pset']
                    pv, pc, pm_, vmv, vmn = c['pv'], c['pc'], c['pm_'], c['vmv'], c['vmn']
                    bo = bank()
                    for h in range(4):
                        ov = psf[bo][0:TT, h * 65:h * 65 + 65]
                        first = True
                        if has_prev:
                            op('pe', lambda en, ov=ov, h=h: en.matmul(
                                ov, lhsT=pv[:, h, :], rhs=Vt[:, pslot, m, hk, :], start=True, stop=False),
                               ['Pprev%d' % pset, 'Vt%d_%d' % (pslot, m)], ['ps%d' % bo])
                            first = False
                        op('pe', lambda en, ov=ov, h=h, first=first: en.matmul(
                            ov, lhsT=pc[0:TT, h, :], rhs=Vt[0:TT, slot, m, hk, :], start=first, stop=False),
                           ['Pcur%d' % pset, 'Vt%d_%d' % (slot, m)], ['ps%d' % bo])
                        op('pe', lambda en, ov=ov, h=h: en.matmul(
                            ov, lhsT=pm_[:, h, :], rhs=vmv, start=False, stop=True),
                           ['Pmeta%d' % pset, vmn], ['ps%d' % bo])
                    o3 = psf[bo][0:TT, 0:260].rearrange("p (h e) -> p h e", e=65)
                    mp = m % 2
                    op('dve', lambda en: en.tensor_tensor(
                        out=den[0:TT, pset, :], in0=o3[:, :, 64], in1=esink[0:TT, 4 * hk:4 * hk + 4], op=ALU.add),
                       ['ps%d' % bo, 'esink'], ['den%d' % pset])
                    op('dve', lambda en: en.reciprocal(out=den[0:TT, pset, :], in_=den[0:TT, pset, :]),
                       ['den%d' % pset], ['den%d' % pset])
                    op('dve', lambda en: en.tensor_tensor(
                        out=attn[0:TT, mp, hk * 256:(hk + 1) * 256].rearrange("p (h d) -> p h d", d=64),
                        in0=o3[:, :, 0:64], in1=den[0:TT, pset, :].unsqueeze(2).broadcast_to([TT, 4, 64]),
                        op=ALU.mult), ['ps%d' % bo, 'den%d' % pset], ['attn%d' % mp])

                def att_norm(m):
                    mp = m % 2
                    op('act', lambda en: en.activation(out=anb[0:TT], in_=attn[0:TT, mp, :], func=AF.Square,
                                                       accum_out=ss[0:TT, 4 + m:5 + m]),
                       ['attn%d' % mp], ['anb', 'ssa%d' % m])
                    rstd_from(ss[0:TT, 4 + m:5 + m], rs[0:TT, 4 + m:5 + m], 512, 'ssa%d' % m, 'rsa%d' % m)
                    op('dve', lambda en: en.tensor_scalar(out=anb[0:TT], in0=attn[0:TT, mp, :],
                                                          scalar1=rs[0:TT, 4 + m:5 + m], scalar2=None, op0=ALU.mult),
                       ['attn%d' % mp, 'rsa%d' % m], ['anb'])
                    b = bank()
                    for kc in range(4):
                        op('pe', lambda en, kc=kc: en.transpose(out=psb[b][:, kc * 128:kc * 128 + TT],
                                                                in_=anb[0:TT, kc * 128:(kc + 1) * 128],
                                                                identity=identb[0:TT, 0:TT]),
                           ['anb', 'identb'], ['ps%d' % b])
                    copy_any(XM[:, 0:4, m * TT:(m + 1) * TT],
                             psb[b][:, 0:512].rearrange("p (k t) -> p k t", t=128)[:, :, 0:TT], ['ps%d' % b], ['XM%d' % m])

                units = [(m, hk) for m in range(NS) for hk in range(2)]
                ctxs = [None] * len(units)
                if overlap:
                    for ui in range(len(units)):
                        ctxs[ui] = att_S(ui, *units[ui])
                        att_exp(ctxs[ui])
                        att_PV(ctxs[ui])
                        if units[ui][1] == 1:
                            att_norm(units[ui][0])
                else:
                    for ui in range(len(units) + 1):
                        if ui < len(units):
                            ctxs[ui] = att_S(ui, *units[ui])
                            att_exp(ctxs[ui])
                        if ui >= 1:
                            att_PV(ctxs[ui - 1])
                            if units[ui - 1][1] == 1:
                                att_norm(units[ui - 1][0])


            def f2():
                if stop == 5.3 and kind == 'P':
                    return True
                op('act', lambda en: en.copy(out=dumA, in_=epsb), ['epsb'],
                   ['dumA'] + ['gT%d' % i_ for i_ in range(16)] + ['LOCK2'])
                for g0 in range(0, 32, 8):
                    b = bank()
                    for g in range(g0, g0 + 8):
                        gh, g16 = g // 16, g % 16
                        sl = slice(64 * gh, 64 * gh + 64)
                        for th in range(2):
                            ov = psf[b][64 * th:64 * th + NQ, (g % 8) * 64:(g % 8) * 64 + 64]
                            op('pe', lambda en, ov=ov, g=g, th=th: en.matmul(
                                ov, lhsT=U_all[:, g, 0:NQ], rhs=Tm[:, g, 64 * th:64 * th + 64], start=True, stop=False),
                               ['U_all', *UZ_, 'Tm'], ['ps%d' % b])
                            for ri in range(2):
                                op('pe', lambda en, ov=ov, sl=sl, g16=g16, th=th, ri=ri: en.matmul(
                                    ov, lhsT=Hb[sl, ri, g16, 0:NQ], rhs=Qm[sl, g16, ri, 64 * th:64 * th + 64],
                                    start=False, stop=(ri == 1)), ['Hb', 'Qm'], ['ps%d' % b])
                    for th in range(2):
                        pr = slice(64 * th, 64 * th + NQ)
                        op('act', lambda en, b=b, g0=g0, pr=pr: en.activation(
                            out=yA[pr].rearrange("p t (g c) -> p g t c", c=16)[:, g0:g0 + 8],
                            in_=psf[b][pr, :].rearrange("p (g t c) -> p g t c", t=4, c=16), func=AF.Gelu_apprx_tanh),
                           ['ps%d' % b, 'LOCK2'], YA_)
                    op('dve', lambda en, g0=g0: en.tensor_copy(out=zs_bf[:, :, g0 * 16:(g0 + 8) * 16],
                                                               in_=yA[:, :, g0 * 16:(g0 + 8) * 16]), YA_, ['zs_bf', *US_])
                YA = ['yA0', 'yA1', 'yA2', 'yA3']

                def d2_T(t4):
                    b = bank()
                    for kc in range(4):
                        op('pe', lambda en, kc=kc: en.transpose(
                            out=psb[b][:, kc * 128:(kc + 1) * 128], in_=zs_bf[:, t4, kc * 128:(kc + 1) * 128],
                            identity=identb), ['zs_bf', *US_, 'identb'], ['ps%d' % b])
                    op('dve', lambda en: en.tensor_copy(
                        out=zsT[:, :, t4, :], in_=psb[b][:, 0:512].rearrange("p (k q) -> p k q", q=128)),
                       ['ps%d' % b], ['zsT%d' % t4, *UZ_])

                def d2_G(t4):
                    b = bank()
                    for th in range(2):
                        ov = psf[b][64 * th:64 * th + NQ, :]
                        for kc in range(4):
                            op('pe', lambda en, ov=ov, kc=kc, th=th: en.matmul(
                                ov, lhsT=zsT[:, kc, t4, 64 * th:64 * th + NQ], rhs=wglu[:, kc, :], start=(kc == 0),
                                stop=False), ['zsT%d' % t4, *UZ_, 'wglu'], ['ps%d' % b])
                        op('pe', lambda en, ov=ov: en.matmul(ov, lhsT=onesr[0:1, 0:NQ], rhs=bglu[0:1, :],
                                                             start=False, stop=True),
                           ['onesr', 'bglu'], ['ps%d' % b])
                    for th in range(2):
                        pr = slice(64 * th, 64 * th + NQ)
                        op('act', lambda en, pr=pr: en.activation(out=yB[pr, t4, :], in_=psf[b][pr, :],
                                                                  func=AF.Sigmoid), ['ps%d' % b], ['yB%d' % t4])
                    op('dve', lambda en: en.tensor_mul(out=yB[:, t4, :], in0=yB[:, t4, :], in1=yA[:, t4, :]),
                       ['yA%d' % t4, 'yB%d' % t4], ['yB%d' % t4])
                    op('act', lambda en: en.activation(out=yA[:, t4, :], in_=yB[:, t4, :], func=AF.Square,
                                                       accum_out=ssS[:, t4:t4 + 1]),
                       ['yB%d' % t4], ['yA%d' % t4, 'ssS%d' % t4])

                def d2_N(t4):
                    op('dve', lambda en: en.tensor_scalar(out=sn_bf[:, t4, :], in0=yB[:, t4, :],
                                                          scalar1=rS[:, t4:t4 + 1], scalar2=None, op0=ALU.mult),
                       ['yB%d' % t4, 'rS'], ['sn_bf%d' % t4, *US_])

                def d2_S(t4):
                    b = bank()
                    for kc in range(4):
                        op('pe', lambda en, kc=kc: en.transpose(
                            out=psb[b][:, kc * 128:(kc + 1) * 128], in_=sn_bf[:, t4, kc * 128:(kc + 1) * 128],
                            identity=identb), ['sn_bf%d' % t4, *US_, 'identb'], ['ps%d' % b])
                    src = psb[b][:, 0:512].rearrange("p (k h q) -> p k h q", h=2, q=64)[:, :, :, 0:NQ]
                    dst = XM[:, 4:8, 0:NT].rearrange("p k (q h t) -> p k h q t", h=2, t=4)[:, :, :, :, t4]
                    copy_any(dst, src, ['ps%d' % b], XMALL_)

                for t4 in range(4):
                    d2_T(t4)
                    d2_G(t4)
                op('act', lambda en: en.activation(out=rS, in_=ssS, func=AF.Ln, scale=1.0 / 512, bias=epsb[:, 0:1]),
                   ['ssS0', 'ssS1', 'ssS2', 'ssS3'], ['rS'])
                op('act', lambda en: en.activation(out=rS, in_=rS, func=AF.Exp, scale=-0.5), ['rS'], ['rS'])
                for t4 in range(4):
                    d2_N(t4)
                for t4 in range(4):
                    d2_S(t4)

                if stop == 5.4 and kind == 'P':
                    return True
                def wout_mm(m):
                    b1 = bank()
                    b2 = bank()
                    for n, bb_ in enumerate([b1, b2]):
                        for kc in range(8):
                            op('pe', lambda en, bb_=bb_, kc=kc, n=n: en.matmul(
                                psf[bb_][0:TT, :], lhsT=XM[:, kc, m * TT:(m + 1) * TT],
                                rhs=wout[:, kc, n * 512:(n + 1) * 512],
                                start=(kc == 0), stop=(kc == 7)), ['XM%d' % m, 'wout'], ['ps%d' % bb_])
                    for n, bb_ in enumerate([b1, b2]):
                        op('dve', lambda en, bb_=bb_, n=n: en.tensor_tensor(
                            out=xt[0:TT, m, n * 512:(n + 1) * 512], in0=psf[bb_][0:TT, :],
                            in1=xt[0:TT, m, n * 512:(n + 1) * 512], op=ALU.add),
                           ['ps%d' % bb_, 'xt%d' % m], ['xt%d' % m])
                    if overlap:
                        dma(lambda en: en.dma_start(out=ydst(m), in_=xt[0:TT, m, :]), ['xt%d' % m],
                            ['yd%d_0' % m, 'yd%d_1' % m])

                def norm2(m):
                    xb = m % 2
                    op('act', lambda en: en.activation(out=xnb[0:TT, xb], in_=xt[0:TT, m, :], func=AF.Square,
                                                       accum_out=ss[0:TT, m:m + 1]),
                       ['xt%d' % m], ['xnb%d' % xb, 'ss%d' % m])
                    rstd_from(ss[0:TT, m:m + 1], rs[0:TT, m:m + 1], D, 'ss%d' % m, 'rs%d' % m)
                    op('dve', lambda en: en.tensor_scalar(out=xnb[0:TT, xb], in0=xt[0:TT, m, :],
                                                          scalar1=rs[0:TT, m:m + 1], scalar2=None, op0=ALU.mult),
                       ['xt%d' % m, 'rs%d' % m], ['xnb%d' % xb])

                def norm2_T(m):
                    xb = m % 2
                    b = bank()
                    for kc in range(8):
                        op('pe', lambda en, kc=kc: en.transpose(
                            out=psb[b][:, kc * 128:kc * 128 + TT], in_=xnb[0:TT, xb, kc * 128:(kc + 1) * 128],
                            identity=identb[0:TT, 0:TT]), ['xnb%d' % xb, 'identb'], ['ps%d' % b])
                    copy_any(hnT[:, :, m * TT:(m + 1) * TT],
                             psb[b].rearrange("p (k t) -> p k t", t=128)[:, :, 0:TT], ['ps%d' % b], ['XM%d' % m])

                for m in range(NS + 1):
                    if m < NS:
                        wout_mm(m)
                        norm2(m)
                    if m >= 1:
                        norm2_T(m - 1)
                if stop == 5.5 and kind == 'P':
                    return True

            def g():
                op('dve', lambda en: en.tensor_copy(out=dumD, in_=epsb), ['epsb'],
                   ['dumD'] + YA_ + ['yB0', 'yB1', 'yB2', 'yB3', 'LOCK1'])
                wu_i = mlp_ctr['wu']
                wd_i = mlp_ctr['wd']
                for hh in range(2):
                    for i in range(8):
                        su = wu_i % 3
                        wu_i += 1
                        h0 = (16 * hh + 2 * i) * 128
                        dma(lambda en, su=su, h0=h0: en.dma_start(out=wup_s[:, su], in_=wup_scr[:, :, h0:h0 + 256]),
                            ['wup_scr'], ['wup_s%d' % su])
                        for hl in range(2):
                            ht = 2 * i + hl
                            b = ht % 4
                            for kc in range(8):
                                op('pe', lambda en, b=b, su=su, hl=hl, kc=kc: en.matmul(
                                    psf[b][:, 0:NT], lhsT=wup_s[:, su, kc, hl * 128:(hl + 1) * 128],
                                    rhs=hnT[:, kc, 0:NT], start=(kc == 0), stop=(kc == 7)),
                                   ['wup_s%d' % su] + HNALL_, ['ps%d' % b])
                            rj = ht % 2
                            op('act', lambda en, b=b, rj=rj: en.activation(out=rt[:, rj, 0:NT], in_=psf[b][:, 0:NT],
                                                                           func=AF.Relu), ['ps%d' % b], ['rt%d' % rj])
                            op('dve', lambda en, ht=ht, rj=rj: en.tensor_mul(out=gT[:, ht, 0:NT], in0=rt[:, rj, 0:NT],
                                                                             in1=rt[:, rj, 0:NT]),
                               ['rt%d' % rj, 'LOCK1'], ['gT%d' % ht])
                    pieces = [(nh_, grp_) for nh_ in range(2) for grp_ in range(4)]

                    def wdn_load(pi, sd_):
                        nh_, grp_ = pieces[pi]
                        r0_ = 16 * hh + 4 * grp_
                        dma(lambda en: en.dma_start(
                            out=wdn_s[:, sd_], in_=wdn_scr[:, r0_:r0_ + 4, nh_ * 512:(nh_ + 1) * 512]),
                            ['wdn_scr'], ['wdn_s%d' % sd_])

                    wdn_load(0, wd_i % 3)
                    for nh in range(2):
                        cs = slice(nh * 512, (nh + 1) * 512)
                        for grp in range(4):
                            sd = wd_i % 3
                            wd_i += 1
                            pi = nh * 4 + grp
                            if pi + 1 < len(pieces):
                                wdn_load(pi + 1, wd_i % 3)
                            for m in range(NS):
                                for hl in range(4):
                                    ht = 4 * grp + hl
                                    op('pe', lambda en, m=m, sd=sd, hl=hl, ht=ht, grp=grp: en.matmul(
                                        psf[4 + m][0:TT, :], lhsT=gT[:, ht, m * TT:(m + 1) * TT],
                                        rhs=wdn_s[:, sd, hl, :],
                                        start=(grp == 0 and hl == 0), stop=(grp == 3 and hl == 3)),
                                       ['gT%d' % ht, 'wdn_s%d' % sd], ['ps%d' % (4 + m)])
                        for m in range(NS):
                            if hh == 0:
                                op('dve', lambda en, m=m, cs=cs: en.tensor_tensor(
                                    out=xt[0:TT, m, cs], in0=psf[4 + m][0:TT, :], in1=xt[0:TT, m, cs], op=ALU.add),
                                   ['ps%d' % (4 + m), 'xt%d' % m], ['xt%d' % m])
                            else:
                                yj = mlp_ctr['y'] % 2
                                mlp_ctr['y'] += 1
                                dst = ydst(m)[:, cs]
                                op('dve', lambda en, m=m, cs=cs, yj=yj: en.tensor_tensor(
                                    out=yt[0:TT, yj, :], in0=psf[4 + m][0:TT, :], in1=xt[0:TT, m, cs], op=ALU.add),
                                   ['ps%d' % (4 + m), 'xt%d' % m], ['yt%d' % yj])
                                dma(lambda en, dst=dst, yj=yj: en.dma_start(out=dst, in_=yt[0:TT, yj, :]),
                                    ['yt%d' % yj], [])
                mlp_ctr['wu'] = wu_i
                mlp_ctr['wd'] = wd_i

            if phase == 'F1':
                return f1()
            if phase == 'F2':
                return f2()
            return g()

        mlp_ctr = {'wu': 0, 'wd': 0, 'y': 0}
        op('pool', lambda en: en.memset(VmS, 1.0), [], ['VmS0', 'VmS1', 'VmS2', 'VmS3'])

        if stop == 3:
            P.emit(nc, st)
            return nc
        block('M', 0, 'F1')
        if stop == 4:
            P.emit(nc, st)
            return nc
        blocks = [('P', j_) for j_ in range(NBLK)] + ([('S', 0)] if do_sample else [])
        if block(blocks[0][0], blocks[0][1], 'F1') or block(blocks[0][0], blocks[0][1], 'F2'):
            P.emit(nc, st)
            return nc
        for bi, (bk_, bj_) in enumerate(blocks):
            i0 = len(P.ops)
            block(bk_, bj_, 'G')
            i1 = len(P.ops)
            if bi + 1 < len(blocks):
                nk_, nj_ = blocks[bi + 1]
                block(nk_, nj_, 'F1')
                i2 = len(P.ops)
                if overlap:
                    P.ops[i0:i2] = merge_ops(P.ops[i0:i1], P.ops[i1:i2])
                block(nk_, nj_, 'F2')
        P.emit(nc, st)
    return nc


_NC_CACHE = {}


def kernel(x_prompt, x_sample, cache_swa_k, cache_swa_v, state_ssm_re, state_ssm_im,
           meta_tokens, norm1_g, w_in, q_norm_g, k_norm_g, sinks,
           ssm_A_re, ssm_A_im, ssm_log_dt, ssm_B_re, ssm_B_im, ssm_C_re, ssm_C_im, ssm_D,
           w_glu, b_glu, attn_out_g, ssm_out_g, w_out, norm2_g, w_up, w_down):
    f = lambda a: np.ascontiguousarray(np.asarray(a, dtype=np.float32))
    if 'nc' not in _NC_CACHE:
        _NC_CACHE['nc'] = build()
    nc = _NC_CACHE['nc']
    shared = {
        "meta": f(meta_tokens), "norm1_g": f(norm1_g), "w_in": f(w_in[0]), "q_g": f(q_norm_g), "k_g": f(k_norm_g),
        "sinks": f(sinks), "A_re": f(ssm_A_re[0]), "A_im": f(ssm_A_im[0]), "log_dt": f(ssm_log_dt),
        "B_re": f(ssm_B_re[0]), "B_im": f(ssm_B_im[0]), "C_re": f(ssm_C_re[0]).reshape(512, 64),
        "C_im": f(ssm_C_im[0]).reshape(512, 64), "Dp": f(ssm_D[0]), "w_glu": f(w_glu[0]), "b_glu": f(b_glu),
        "ao_g": f(attn_out_g), "so_g": f(ssm_out_g), "w_out": f(w_out[0]), "norm2_g": f(norm2_g),
        "w_up": f(w_up[0]), "w_down": f(w_down[0]),
    }
    xpf, xsf = f(x_prompt), f(x_sample)
    ckf = f(cache_swa_k[0]).reshape(32, 144, 128)
    cvf = f(cache_swa_v[0]).reshape(32, 144, 128)
    srf = f(state_ssm_re[0]).reshape(32 * 32, 64)
    sif = f(state_ssm_im[0]).reshape(32 * 32, 64)
    in_maps = []
    for c in range(NCORE):
        sl = slice(c * NSEQ, (c + 1) * NSEQ)
        d = dict(shared)
        d.update({"xp": xpf[sl], "xs": xsf[sl], "ck": ckf[sl], "cv": cvf[sl],
                  "s_re": srf[c * 128:(c + 1) * 128], "s_im": sif[c * 128:(c + 1) * 128]})
        in_maps.append(d)
    res = run_bass_kernel_spmd(nc, in_maps, core_ids=list(range(NCORE)))
    R = res.results
    cat = lambda k: np.concatenate([np.asarray(R[c][k], dtype=np.float32) for c in range(NCORE)], axis=0)
    y_prompt = cat("yp")
    y_sample = cat("ys")
    kp = cat("kpo").reshape(1, 32, 144, 2, 64)
    vp = cat("vpo").reshape(1, 32, 144, 2, 64)
    srp_ = cat("srp").reshape(1, 32, 32, 64)
    sip_ = cat("sip").reshape(1, 32, 32, 64)
    ks_ = cat("kso").reshape(1, 32, 64, 2, 64)
    vs_ = cat("vso").reshape(1, 32, 64, 2, 64)
    srs_ = cat("srs").reshape(1, 32, 32, 64)
    sis_ = cat("sis").reshape(1, 32, 32, 64)
    return (y_prompt, y_sample, kp, vp, srp_, sip_, ks_, vs_, srs_, sis_)
```

```python
import numpy as np
from contextlib import ExitStack
import concourse.bass as bass
import concourse.mybir as mybir
from concourse.bass_utils import run_bass_kernel_spmd

F32 = mybir.dt.float32
BF16 = mybir.dt.bfloat16
I32 = mybir.dt.int32
AF = mybir.ActivationFunctionType
ALU = mybir.AluOpType
AX = mybir.AxisListType

NCORE = 8
D = 1024
SEQ = 2048
NSEQ = 4
NBLK = SEQ // 128
EPS = 1e-6
TWO_PI = float(2 * np.pi)


class Prog:
    NDMASEM = 16
    ENGS = ['pe', 'act', 'dve', 'pool', 'sp']

    def __init__(self):
        self.ops = []

    def op(self, eng, fn, reads=(), writes=(), dma=False, barrier=False):
        self.ops.append(dict(eng=eng, fn=fn, reads=tuple(reads), writes=tuple(writes), dma=dma, barrier=barrier))

    def barrier(self):
        for e in ['pe', 'act', 'dve', 'pool']:
            self.op(e, lambda en: en.nop(nofuse=True), barrier=True)
        self.op('sp', None, barrier=True)

    def emit(self, nc, stack):
        ops = self.ops
        engs = self.ENGS
        last_w = {}
        readers = {}
        last_op = {}
        recent_dma = {e: [] for e in engs}
        for i, o in enumerate(ops):
            if o['barrier']:
                deps = set(v for k, v in last_op.items())
                for e in engs:
                    deps |= set(recent_dma[e][-self.NDMASEM:])
                deps.discard(i)
                o['deps'] = set(d for d in deps if ops[d]['dma'] or ops[d]['eng'] != o['eng'])
                if o['eng'] != 'sp':
                    last_op[o['eng']] = i
                continue
            deps = set()
            for r in o['reads']:
                if r in last_w:
                    deps.add(last_w[r])
            for w in o['writes']:
                if w in last_w:
                    deps.add(last_w[w])
                lastrd = {}
                for rd in readers.get(w, ()):
                    if ops[rd]['dma']:
                        deps.add(rd)
                    else:
                        lastrd[ops[rd]['eng']] = rd
                for rd in lastrd.values():
                    deps.add(rd)
            deps.discard(i)
            nd = set()
            for d in deps:
                od = ops[d]
                if not od['dma'] and not o['dma'] and od['eng'] == o['eng']:
                    if o['eng'] == 'pe':
                        continue
                nd.add(d)
            o['deps'] = nd
            for r in o['reads']:
                readers.setdefault(r, []).append(i)
            for w in o['writes']:
                last_w[w] = i
                readers[w] = []
            if o['dma']:
                recent_dma[o['eng']].append(i)
            else:
                last_op[o['eng']] = i
        needed = set()
        for o in ops:
            needed |= o['deps']
        cnt = {e: 0 for e in engs}
        dcnt = {e: 0 for e in engs}
        for i, o in enumerate(ops):
            e = o['eng']
            if o['dma']:
                k = dcnt[e]
                dcnt[e] += 1
                o['sig'] = ('d', e, k % self.NDMASEM, 16 * (k // self.NDMASEM + 1))
            elif i in needed:
                cnt[e] += 1
                o['sig'] = ('c', e, 0, cnt[e])
            else:
                o['sig'] = None
        sems = {}
        for e in engs:
            sems[('c', e, 0)] = stack.enter_context(nc.semaphore('s_' + e))
        for e in engs:
            if dcnt[e]:
                for k in range(self.NDMASEM):
                    sems[('d', e, k)] = stack.enter_context(nc.semaphore('d_%s_%d' % (e, k)))
        block = stack.enter_context(nc.Block())
        per = {e: [o for o in ops if o['eng'] == e] for e in engs}

        def run(e, engine):
            waited = {}

            def wait(key, val):
                if waited.get(key, 0) >= val:
                    return
                waited[key] = val
                engine.wait_ge(sems[key], val)
            final = {}
            for o in per[e]:
                for d in sorted(o['deps']):
                    s = ops[d]['sig']
                    wait(s[:3], s[3])
                if o['dma']:
                    s = o['sig']
                    if s[3] > 16:
                        wait(s[:3], s[3] - 16)
                    o['fn'](engine).then_inc(sems[s[:3]], 16)
                    final[s[:3]] = s[3]
                elif o['fn'] is not None:
                    ins = o['fn'](engine)
                    if o['sig'] is not None:
                        ins.then_inc(sems[o['sig'][:3]], 1)
            for key, val in final.items():
                wait(key, val)

        @block.tensor
        def _(eng):
            run('pe', eng)

        @block.scalar
        def _(eng):
            run('act', eng)

        @block.vector
        def _(eng):
            run('dve', eng)

        @block.gpsimd
        def _(eng):
            run('pool', eng)

        @block.sync
        def _(eng):
            run('sp', eng)


def merge_ops(a, b):
    out = []
    ia = ib = 0
    na, nb = len(a), len(b)
    while ia < na or ib < nb:
        if ib >= nb or (ia < na and ia * nb <= ib * na):
            out.append(a[ia])
            ia += 1
        else:
            out.append(b[ib])
            ib += 1
    return out


def build(NBLK=NBLK, stop=99, do_sample=True, overlap=False):
    nc = bass.Bass("TRN2", target_bir_lowering=False)

    def din(name, shape):
        return nc.dram_tensor(name, list(shape), F32, kind="ExternalInput").ap()

    def dout(name, shape):
        return nc.dram_tensor(name, list(shape), F32, kind="ExternalOutput").ap()

    xp = din("xp", [NSEQ, SEQ, D])
    xs = din("xs", [NSEQ, 64, D])
    ck = din("ck", [NSEQ, 144, 128])
    cv = din("cv", [NSEQ, 144, 128])
    s_re = din("s_re", [NSEQ * 32, 64])
    s_im = din("s_im", [NSEQ * 32, 64])
    meta = din("meta", [16, D])
    norm1_g = din("norm1_g", [1, D])
    w_in = din("w_in", [D, 1280])
    q_g = din("q_g", [1, 64])
    k_g = din("k_g", [1, 64])
    sinks = din("sinks", [1, 8])
    A_re = din("A_re", [32, 64])
    A_im = din("A_im", [32, 64])
    log_dt = din("log_dt", [1, 32])
    B_re = din("B_re", [32, 64, 16])
    B_im = din("B_im", [32, 64, 16])
    C_re = din("C_re", [512, 64])
    C_im = din("C_im", [512, 64])
    Dp = din("Dp", [32, 16])
    w_glu = din("w_glu", [512, 512])
    b_glu = din("b_glu", [1, 512])
    ao_g = din("ao_g", [1, 512])
    so_g = din("so_g", [1, 512])
    w_out = din("w_out", [D, D])
    norm2_g = din("norm2_g", [1, D])
    w_up = din("w_up", [D, 4096])
    w_down = din("w_down", [4096, D])

    yp = dout("yp", [NSEQ, SEQ, D])
    ys = dout("ys", [NSEQ, 64, D])
    kpo = dout("kpo", [NSEQ, 144, 128])
    vpo = dout("vpo", [NSEQ, 144, 128])
    srp = dout("srp", [NSEQ * 32, 64])
    sip = dout("sip", [NSEQ * 32, 64])
    kso = dout("kso", [NSEQ, 64, 128])
    vso = dout("vso", [NSEQ, 64, 128])
    srs = dout("srs", [NSEQ * 32, 64])
    sis = dout("sis", [NSEQ * 32, 64])

    wup_scr = nc.dram_tensor("wup_scr", [128, 8, 4096], BF16, kind="Internal").ap()
    wdn_scr = nc.dram_tensor("wdn_scr", [128, 32, 1024], BF16, kind="Internal").ap()

    P = Prog()

    def op(eng, fn, r=(), w=()):
        P.op(eng, fn, r, w)

    def dma(fn, r=(), w=()):
        P.op('sp', fn, r, w, dma=True)

    with ExitStack() as st:
        NB = 206 * 1024
        raw = st.enter_context(nc.sbuf_tensor("raw", [128, NB // 2], BF16))
        rawb = raw[:]
        rawf = rawb.bitcast(F32)
        rawi = rawb.bitcast(I32)
        state = {'off': 0}

        offs = {}

        def salloc(shape, dt, at=None, name=None):
            n = 1
            for s_ in shape[1:]:
                n *= s_
            nb = n * (2 if dt == BF16 else 4)
            nb = (nb + 3) // 4 * 4
            if at is None:
                off = state['off']
                state['off'] += nb
                assert state['off'] <= NB, ("SBUF overflow", state['off'])
            else:
                off = at
            if dt == BF16:
                v = rawb[0:shape[0], off // 2: off // 2 + n]
            elif dt == F32:
                v = rawf[0:shape[0], off // 4: off // 4 + n]
            else:
                v = rawi[0:shape[0], off // 4: off // 4 + n]
            if len(shape) > 2:
                names = ['a%d' % i for i in range(len(shape) - 1)]
                kw = {names[i]: shape[1 + i] for i in range(len(shape) - 2)}
                v = v.rearrange("p (%s) -> p %s" % (' '.join(names), ' '.join(names)), **kw)
            if name is not None:
                offs[name] = off
            return v

        psf = []
        psb = []
        for b in range(8):
            t = st.enter_context(nc.psum_tensor("ps%d" % b, [128, 512], F32))
            psf.append(t[:])
            psb.append(t[:].bitcast(BF16))
        bank_ctr = {'i': 0}

        def bank():
            if bank_ctr.get('front') and overlap:
                b = 4 + bank_ctr['i'] % 4
            else:
                b = bank_ctr['i'] % 8
            bank_ctr['i'] += 1
            return b

        win = salloc([128, 8, 1280], BF16)
        wglu = salloc([128, 4, 512], BF16)
        wout = salloc([128, 8, 1024], BF16)
        Tm = salloc([128, 32, 128], BF16)
        Pm = salloc([128, 32, 128], BF16)
        Qm = salloc([128, 16, 2, 128], BF16)
        identf = salloc([128, 128], F32)
        identb = salloc([128, 128], BF16)
        onesr = salloc([1, 128], BF16)
        bglu = salloc([128, 512], BF16)
        C1 = salloc([128, 2, 16, 4], F32)
        C2 = salloc([128, 2, 16, 4], F32)
        gk_t = salloc([128, 2, 64], F32)
        esink = salloc([128, 8], F32)
        gq8 = salloc([64, 1], F32)
        epsb = salloc([128, 1], F32)
        dumA = salloc([128, 1], F32)
        dumD = salloc([128, 1], F32)
        KTm = salloc([64, 2, 16], BF16)
        Vm = salloc([16, 2, 65], BF16)
        kmf = salloc([16, 128], F32)
        vmf = salloc([16, 128], F32)
        Zmeta = salloc([128, 3, 16], F32)
        ss = salloc([128, 8], F32)
        rs = salloc([128, 8], F32)
        st10 = salloc([128, 2, 10], F32)
        r10 = salloc([128, 2, 10], F32)
        den = salloc([128, 2, 4], F32)
        ssS = salloc([128, 4], F32)
        rS = salloc([128, 4], F32)
        KTmS = salloc([64, 4, 2, 16], BF16)
        VmS = salloc([16, 4, 2, 65], BF16)
        xt = salloc([128, 4, 1024], F32, name='xt')
        xnb = salloc([128, 2, 1024], BF16)
        XM = salloc([128, 8, 512], BF16, name='XM')
        hnT = salloc([128, 8, 512], BF16, at=offs['XM'])
        qsq = salloc([128, 640], F32)
        qnb = salloc([128, 2, 512], BF16)
        kf = salloc([128, 2, 128], F32)
        kb = salloc([128, 2, 128], BF16)
        vf = salloc([128, 128], F32)
        QT = salloc([64, 4, 8, 128], BF16)
        KT = salloc([64, 2, 2, 4, 128], BF16)
        Vt = salloc([128, 2, 4, 2, 65], BF16)
        Pprev = salloc([128, 2, 4, 128], BF16)
        Pcur = salloc([128, 2, 4, 128], BF16)
        Pmeta = salloc([16, 2, 4, 128], BF16)
        attn = salloc([128, 2, 512], F32)
        anb = salloc([128, 512], BF16)
        U_all = salloc([128, 32, 64], BF16, name='U_all')
        Vs = salloc([128, 2, 16, 64], F32)
        Hb = salloc([128, 2, 16, 64], BF16)
        Z = salloc([128, 2, 3, 64], F32)
        T1 = salloc([128, 2, 64], F32)
        T2 = salloc([128, 2, 64], F32)
        zsT = salloc([128, 4, 4, 128], BF16, at=offs['U_all'])
        rt = salloc([128, 2, 512], F32)
        yt = salloc([128, 2, 512], F32)
        wup_s = salloc([128, 3, 8, 256], BF16)
        wdn_s = salloc([128, 3, 4, 512], BF16)
        yA = salloc([128, 4, 512], F32, name='yA')
        yB = salloc([128, 4, 512], F32)
        gT = salloc([128, 16, 512], BF16, at=offs['yA'])
        u_ks = salloc([64, 32, 8, 16], BF16, name='u_ks')
        zs_bf = salloc([128, 4, 512], BF16, at=offs['u_ks'])
        sn_bf = salloc([128, 4, 512], BF16, at=offs['u_ks'] + 4096)
        print("SBUF bytes/partition used:", state['off'])

        scr0 = offs['xt']
        scr = {'off': scr0}

        def scratch(shape, dt):
            n = 1
            for s_ in shape[1:]:
                n *= s_
            nb = (n * (2 if dt == BF16 else 4) + 3) // 4 * 4
            v = salloc(shape, dt, at=scr['off'])
            scr['off'] += nb
            assert scr['off'] <= state['off'], "scratch overflow"
            return v

        cp_i = {'i': 0}

        def copy_any(out, in_, r, w, engs=('act', 'dve')):
            e = engs[cp_i['i'] % len(engs)]
            cp_i['i'] += 1
            if e == 'act':
                op('act', lambda en: en.copy(out=out, in_=in_), r, w)
            else:
                op(e, lambda en: en.tensor_copy(out=out, in_=in_), r, w)

        def rstd_from(ssap, rsap, n, rname, wname):
            op('act', lambda en: en.activation(out=rsap, in_=ssap, func=AF.Ln, scale=1.0 / n,
                                               bias=epsb[0:ssap.shape[0], 0:1]), [rname], [wname])
            op('act', lambda en: en.activation(out=rsap, in_=rsap, func=AF.Exp, scale=-0.5), [wname], [wname])

        op('pool', lambda en: en.memset(identf, 0.0), [], ['identf'])
        op('pool', lambda en: en.affine_select(out=identf, in_=identf, compare_op=ALU.not_equal, fill=1.0,
                                               base=0, pattern=[[-1, 128]], channel_multiplier=1),
           ['identf'], ['identf'])
        op('dve', lambda en: en.tensor_copy(out=identb, in_=identf), ['identf'], ['identb'])
        op('pool', lambda en: en.memset(onesr, 1.0), [], ['onesr'])
        op('pool', lambda en: en.memset(epsb, EPS), [], ['epsb'])
        op('pool', lambda en: en.memset(Vm, 1.0), [], ['Vm'])
        dma(lambda en: en.dma_start(out=gk_t[:, 0, :], in_=k_g[0:1, :].broadcast_to([128, 64])), [], ['gk_t'])
        dma(lambda en: en.dma_start(out=gk_t[:, 1, :], in_=k_g[0:1, :].broadcast_to([128, 64])), [], ['gk_t'])
        dma(lambda en: en.dma_start(out=esink, in_=sinks[0:1, :].broadcast_to([128, 8])), [], ['esink'])
        op('act', lambda en: en.activation(out=esink, in_=esink, func=AF.Exp), ['esink'], ['esink'])
        dma(lambda en: en.dma_start(out=gq8, in_=q_g.rearrange("o d -> d o"), allow_slow_non_contiguous=True),
            [], ['gq8'])
        op('dve', lambda en: en.tensor_scalar(out=gq8, in0=gq8, scalar1=0.125, scalar2=None, op0=ALU.mult),
           ['gq8'], ['gq8'])

        if stop == 0.5:
            P.emit(nc, st)
            return nc
        g1 = scratch([128, 8], F32)
        mg = scratch([128, 8], F32)
        g2 = scratch([128, 8], F32)
        dma(lambda en: en.dma_start(out=g1, in_=norm1_g.rearrange("o (k p) -> p (o k)", p=128),
                                    allow_slow_non_contiguous=True), [], ['g1'])
        dma(lambda en: en.dma_start(out=mg[:, 0:4], in_=ao_g.rearrange("o (k p) -> p (o k)", p=128),
                                    allow_slow_non_contiguous=True), [], ['mg'])
        dma(lambda en: en.dma_start(out=mg[:, 4:8], in_=so_g.rearrange("o (k p) -> p (o k)", p=128),
                                    allow_slow_non_contiguous=True), [], ['mg'])
        dma(lambda en: en.dma_start(out=g2, in_=norm2_g.rearrange("o (k p) -> p (o k)", p=128),
                                    allow_slow_non_contiguous=True), [], ['g2'])
        stg = scratch([128, 6, 1280], F32)
        stb = scratch([128, 6, 1024], BF16)
        sc = {'i': 0}

        def cast_rows(dst, src_dram, ncol, gain, gname, wname):
            i = sc['i'] % 6
            sc['i'] += 1
            sname = 'stg%d' % i
            dma(lambda en: en.dma_start(out=stg[:, i, 0:ncol], in_=src_dram), [], [sname])
            e = ['act', 'dve'][sc['i'] % 2]
            rr = [sname] + ([gname] if gain is not None else [])
            if gain is None:
                if e == 'act':
                    op('act', lambda en: en.copy(out=dst, in_=stg[:, i, 0:ncol]), rr, [wname])
                else:
                    op(e, lambda en: en.tensor_copy(out=dst, in_=stg[:, i, 0:ncol]), rr, [wname])
            else:
                if e == 'act':
                    op('act', lambda en: en.mul(out=dst, in_=stg[:, i, 0:ncol], mul=gain), rr, [wname])
                else:
                    op(e, lambda en: en.tensor_scalar(out=dst, in0=stg[:, i, 0:ncol], scalar1=gain, scalar2=None,
                                                      op0=ALU.mult), rr, [wname])

        for kc in range(8):
            cast_rows(win[:, kc, :], w_in[kc * 128:(kc + 1) * 128, :], 1280, g1[:, kc:kc + 1], 'g1', 'win')
        if stop == 0.7:
            P.emit(nc, st)
            return nc
        for kc in range(4):
            cast_rows(wglu[:, kc, :], w_glu[kc * 128:(kc + 1) * 128, :], 512, None, None, 'wglu')
        if stop == 0.8:
            P.emit(nc, st)
            return nc
        for kc in range(8):
            cast_rows(wout[:, kc, :], w_out[kc * 128:(kc + 1) * 128, :], 1024, mg[:, kc:kc + 1], 'mg', 'wout')
        if stop == 0.9:
            P.emit(nc, st)
            return nc
        bgf = scratch([128, 512], F32)
        dma(lambda en: en.dma_start(out=bgf, in_=b_glu[0:1, :].broadcast_to([128, 512])), [], ['bgf'])
        if stop == 0.95:
            P.emit(nc, st)
            return nc
        op('dve', lambda en: en.tensor_copy(out=bglu, in_=bgf), ['bgf'], ['bglu'])
        if stop == 1:
            P.emit(nc, st)
            return nc
        i_conv0 = len(P.ops)
        tasks = []
        for kc in range(8):
            for c4 in range(4):
                tasks.append((w_up[kc * 128:(kc + 1) * 128, c4 * 1024:(c4 + 1) * 1024], g2[:, kc:kc + 1], 'g2',
                              wup_scr[:, kc, c4 * 1024:(c4 + 1) * 1024], 'wup_scr'))
        for ht in range(32):
            tasks.append((w_down[ht * 128:(ht + 1) * 128, :], None, None, wdn_scr[:, ht, :], 'wdn_scr'))
        KLOOK = 3

        def conv_load(t):
            i = t % 6
            src = tasks[t][0]
            dma(lambda en, i=i, src=src: en.dma_start(out=stg[:, i, 0:1024], in_=src), [], ['stg%d' % i])

        for t in range(KLOOK):
            conv_load(t)
        for t in range(len(tasks)):
            i = t % 6
            src, gain, gname, dstd, dname = tasks[t]
            e = ['act', 'dve'][t % 2]
            rr = ['stg%d' % i] + ([gname] if gain is not None else [])
            if gain is None:
                if e == 'act':
                    op('act', lambda en, i=i: en.copy(out=stb[:, i, :], in_=stg[:, i, 0:1024]), rr, ['stb%d' % i])
                else:
                    op('dve', lambda en, i=i: en.tensor_copy(out=stb[:, i, :], in_=stg[:, i, 0:1024]), rr,
                       ['stb%d' % i])
            else:
                if e == 'act':
                    op('act', lambda en, i=i, gain=gain: en.mul(out=stb[:, i, :], in_=stg[:, i, 0:1024], mul=gain),
                       rr, ['stb%d' % i])
                else:
                    op('dve', lambda en, i=i, gain=gain: en.tensor_scalar(
                        out=stb[:, i, :], in0=stg[:, i, 0:1024], scalar1=gain, scalar2=None, op0=ALU.mult),
                       rr, ['stb%d' % i])
            if t + KLOOK < len(tasks):
                conv_load(t + KLOOK)
            dma(lambda en, i=i, dstd=dstd: en.dma_start(out=dstd, in_=stb[:, i, :]), ['stb%d' % i], [dname])

        i_conv1 = len(P.ops)
        if stop == 2:
            P.emit(nc, st)
            return nc
        Are = scratch([128, 16], F32)
        Aim = scratch([128, 16], F32)
        dtl = scratch([128, 16], F32)
        Bre = scratch([128, 16, 16], F32)
        Bim = scratch([128, 16, 16], F32)
        CCre = scratch([128, 16, 16], F32)
        CCim = scratch([128, 16, 16], F32)
        Dcol = scratch([128, 32], F32)
        for gh in range(2):
            sl = slice(64 * gh, 64 * gh + 64)
            gs = slice(16 * gh, 16 * gh + 16)
            dma(lambda en, sl=sl, gs=gs: en.dma_start(out=Are[sl, :], in_=A_re[gs, :].rearrange("g p -> p g"),
                                                      allow_slow_non_contiguous=True), [], ['Are'])
            dma(lambda en, sl=sl, gs=gs: en.dma_start(out=Aim[sl, :], in_=A_im[gs, :].rearrange("g p -> p g"),
                                                      allow_slow_non_contiguous=True), [], ['Aim'])
            dma(lambda en, sl=sl, gs=gs: en.dma_start(out=dtl[sl, :], in_=log_dt[0:1, gs].broadcast_to([64, 16])),
                [], ['dtl'])
            dma(lambda en, sl=sl, gs=gs: en.dma_start(out=Bre[sl, :, :], in_=B_re[gs].rearrange("g p c -> p g c")),
                [], ['Bre'])
            dma(lambda en, sl=sl, gs=gs: en.dma_start(out=Bim[sl, :, :], in_=B_im[gs].rearrange("g p c -> p g c")),
                [], ['Bim'])
        for s_ in range(8):
            dma(lambda en, s_=s_: en.dma_start(out=Dcol[16 * s_:16 * s_ + 16, :], in_=Dp.rearrange("g c -> c g"),
                                               allow_slow_non_contiguous=True), [], ['Dcol'])
        Cst = scratch([128, 8, 64], F32)
        for ri, Csrc in enumerate([C_re, C_im]):
            for j in range(4):
                dma(lambda en, ri=ri, j=j, Csrc=Csrc: en.dma_start(out=Cst[:, ri * 4 + j, :],
                                                                   in_=Csrc[j * 128:(j + 1) * 128, :]),
                    [], ['Cst%d' % (ri * 4 + j)])
        for ri, CC in enumerate([CCre, CCim]):
            b = bank()
            for j in range(4):
                gh = j // 2
                op('pe', lambda en, ri=ri, j=j, gh=gh, b=b: en.matmul(
                    psf[b][64 * gh:64 * gh + 64, (j % 2) * 128:(j % 2) * 128 + 128],
                    lhsT=Cst[:, ri * 4 + j, :], rhs=identf, start=True, stop=True),
                   ['Cst%d' % (ri * 4 + j), 'identf'], ['ps%d' % b])
            op('dve', lambda en, CC=CC, b=b: en.tensor_copy(
                out=CC, in_=psf[b][:, 0:256].rearrange("p (g c) -> p g c", c=16)), ['ps%d' % b], ['CC%d' % ri])

        def dv(fn, r, w, e='dve'):
            op(e, fn, r, w)

        lre = scratch([128, 16], F32)
        dtv = scratch([128, 16], F32)
        lrd = scratch([128, 16], F32)
        thd = scratch([128, 16], F32)
        op('act', lambda en: en.activation(out=dtv, in_=dtl, func=AF.Exp), ['dtl'], ['dtv'])
        dv(lambda en: en.tensor_scalar(out=lre, in0=Are, scalar1=-1e-4, scalar2=None, op0=ALU.min), ['Are'], ['lre'])
        dv(lambda en: en.tensor_mul(out=lrd, in0=lre, in1=dtv), ['lre', 'dtv'], ['lrd'])
        dv(lambda en: en.tensor_mul(out=thd, in0=Aim, in1=dtv), ['Aim', 'dtv'], ['thd'])
        KV = [0, -1, -2, -3, -4, -5, -6, -7] + list(range(0, 9)) + [7, 6, 5, 4, 3, 2, 1, 0]
        NK = len(KV)
        kv = scratch([128, NK], F32)
        for i, kval in enumerate(KV):
            op('pool', lambda en, i=i, kval=kval: en.memset(kv[:, i:i + 1], float(kval)), [], ['kv'])
        mag = scratch([128, NK, 16], F32)
        ang = scratch([128, 2, NK, 16], F32)
        angn = scratch([128, 2, NK, 16], F32)
        angi = scratch([128, 2, NK, 16], I32)
        kvb = kv.unsqueeze(2).broadcast_to([128, NK, 16])
        dv(lambda en: en.tensor_tensor(out=mag, in0=lrd.unsqueeze(1).broadcast_to([128, NK, 16]), in1=kvb,
                                       op=ALU.mult), ['lrd', 'kv'], ['mag'])
        op('act', lambda en: en.activation(out=mag, in_=mag, func=AF.Exp), ['mag'], ['mag'])
        dv(lambda en: en.tensor_tensor(out=ang[:, 0], in0=thd.unsqueeze(1).broadcast_to([128, NK, 16]), in1=kvb,
                                       op=ALU.mult), ['thd', 'kv'], ['ang'])
        OFFS = TWO_PI * 40
        dv(lambda en: en.tensor_scalar(out=ang[:, 1], in0=ang[:, 0], scalar1=OFFS + float(np.pi / 2), scalar2=None,
                                       op0=ALU.add), ['ang'], ['ang'])
        dv(lambda en: en.tensor_scalar(out=ang[:, 0], in0=ang[:, 0], scalar1=OFFS, scalar2=None, op0=ALU.add),
           ['ang'], ['ang'])
        dv(lambda en: en.tensor_scalar(out=angn, in0=ang, scalar1=1.0 / TWO_PI, scalar2=None, op0=ALU.mult),
           ['ang'], ['angn'])
        dv(lambda en: en.tensor_copy(out=angi, in_=angn), ['angn'], ['angi'])
        dv(lambda en: en.tensor_copy(out=angn, in_=angi), ['angi'], ['angn'])
        dv(lambda en: en.scalar_tensor_tensor(out=ang, in0=angn, scalar=-TWO_PI, in1=ang, op0=ALU.mult, op1=ALU.add),
           ['angn', 'ang'], ['ang'])
        dv(lambda en: en.tensor_scalar(out=angn, in0=ang, scalar1=float(np.pi), scalar2=-TWO_PI, op0=ALU.is_gt,
                                       op1=ALU.mult), ['ang'], ['angn'])
        dv(lambda en: en.tensor_add(out=ang, in0=ang, in1=angn), ['ang', 'angn'], ['ang'])
        dv(lambda en: en.tensor_scalar(out=ang, in0=ang, scalar1=float(np.pi), scalar2=-float(np.pi), op0=ALU.min,
                                       op1=ALU.max), ['ang'], ['ang'])
        op('act', lambda en: en.activation(out=ang, in_=ang, func=AF.Sin), ['ang'], ['ang'])
        AR = scratch([128, NK, 16], F32)
        AI = scratch([128, NK, 16], F32)
        dv(lambda en: en.tensor_mul(out=AI, in0=mag, in1=ang[:, 0]), ['mag', 'ang'], ['AI'])
        dv(lambda en: en.tensor_mul(out=AR, in0=mag, in1=ang[:, 1]), ['mag', 'ang'], ['AR'])
        K1 = 9
        am1 = scratch([128, 16], F32)
        t_a = scratch([128, 16], F32)
        t_b = scratch([128, 16], F32)
        dn = scratch([128, 16], F32)
        cr = scratch([128, 16], F32)
        ci = scratch([128, 16], F32)
        dv(lambda en: en.tensor_scalar(out=am1, in0=AR[:, K1, :], scalar1=-1.0, scalar2=None, op0=ALU.add),
           ['AR'], ['am1'])
        dv(lambda en: en.tensor_mul(out=dn, in0=lre, in1=lre), ['lre'], ['dn'])
        dv(lambda en: en.tensor_mul(out=t_a, in0=Aim, in1=Aim), ['Aim'], ['t_a'])
        dv(lambda en: en.tensor_add(out=dn, in0=dn, in1=t_a), ['dn', 't_a'], ['dn'])
        dv(lambda en: en.reciprocal(out=dn, in_=dn), ['dn'], ['dn'])
        dv(lambda en: en.tensor_mul(out=t_a, in0=am1, in1=lre), ['am1', 'lre'], ['t_a'])
        dv(lambda en: en.tensor_mul(out=t_b, in0=AI[:, K1, :], in1=Aim), ['AI', 'Aim'], ['t_b'])
        dv(lambda en: en.tensor_add(out=cr, in0=t_a, in1=t_b), ['t_a', 't_b'], ['cr'])
        dv(lambda en: en.tensor_mul(out=cr, in0=cr, in1=dn), ['cr', 'dn'], ['cr'])
        dv(lambda en: en.tensor_mul(out=t_a, in0=AI[:, K1, :], in1=lre), ['AI', 'lre'], ['t_a'])
        dv(lambda en: en.tensor_mul(out=t_b, in0=am1, in1=Aim), ['am1', 'Aim'], ['t_b'])
        dv(lambda en: en.tensor_sub(out=ci, in0=t_a, in1=t_b), ['t_a', 't_b'], ['ci'])
        dv(lambda en: en.tensor_mul(out=ci, in0=ci, in1=dn), ['ci', 'dn'], ['ci'])
        bbr = scratch([128, 16, 16], F32)
        bbi = scratch([128, 16, 16], F32)
        tq = scratch([128, 16, 16], F32)
        crb = cr.unsqueeze(2).broadcast_to([128, 16, 16])
        cib = ci.unsqueeze(2).broadcast_to([128, 16, 16])
        dv(lambda en: en.tensor_tensor(out=bbr, in0=Bre, in1=crb, op=ALU.mult), ['Bre', 'cr'], ['bbr'])
        dv(lambda en: en.tensor_tensor(out=tq, in0=Bim, in1=cib, op=ALU.mult), ['Bim', 'ci'], ['tq'])
        dv(lambda en: en.tensor_sub(out=bbr, in0=bbr, in1=tq), ['bbr', 'tq'], ['bbr'])
        dv(lambda en: en.tensor_tensor(out=bbi, in0=Bim, in1=crb, op=ALU.mult), ['Bim', 'cr'], ['bbi'])
        dv(lambda en: en.tensor_tensor(out=tq, in0=Bre, in1=cib, op=ALU.mult), ['Bre', 'ci'], ['tq'])
        dv(lambda en: en.tensor_add(out=bbi, in0=bbi, in1=tq), ['bbi', 'tq'], ['bbi'])

        def cprod(outr, outi, k0, Mr, Mi, nMr, nMi, nr, ni, neg_im=False, eng='dve'):
            pr = AR[:, k0:k0 + 8, :].transpose([0, 2, 1]).unsqueeze(3).broadcast_to([128, 16, 8, 16])
            pi = AI[:, k0:k0 + 8, :].transpose([0, 2, 1]).unsqueeze(3).broadcast_to([128, 16, 8, 16])
            mr = Mr.unsqueeze(2).broadcast_to([128, 16, 8, 16])
            mi = Mi.unsqueeze(2).broadcast_to([128, 16, 8, 16])
            tmp = cp_tmp
            op(eng, lambda en: en.tensor_tensor(out=outr, in0=pr, in1=mr, op=ALU.mult), ['AR', nMr], [nr])
            op(eng, lambda en: en.tensor_tensor(out=tmp, in0=pi, in1=mi, op=ALU.mult), ['AI', nMi], ['cpt'])
            op(eng, lambda en: en.tensor_sub(out=outr, in0=outr, in1=tmp), [nr, 'cpt'], [nr])
            op(eng, lambda en: en.tensor_tensor(out=outi, in0=pr, in1=mi, op=ALU.mult), ['AR', nMi], [ni])
            op(eng, lambda en: en.tensor_tensor(out=tmp, in0=pi, in1=mr, op=ALU.mult), ['AI', nMr], ['cpt'])
            if neg_im:
                op(eng, lambda en: en.scalar_tensor_tensor(out=outi, in0=outi, scalar=-1.0, in1=tmp, op0=ALU.mult,
                                                           op1=ALU.subtract), [ni, 'cpt'], [ni])
            else:
                op(eng, lambda en: en.tensor_add(out=outi, in0=outi, in1=tmp), [ni, 'cpt'], [ni])

        cp_tmp = scratch([128, 16, 8, 16], F32)
        Lr = scratch([128, 16, 8, 16], F32)
        Li = scratch([128, 16, 8, 16], F32)
        Rr = scratch([128, 16, 8, 16], F32)
        Ri = scratch([128, 16, 8, 16], F32)
        cprod(Lr, Li, 0, bbr, bbi, 'bbr', 'bbi', 'Lr', 'Li')
        cprod(Rr, Ri, 8, CCre, CCim, 'CC0', 'CC1', 'Rr', 'Ri', neg_im=True)
        maskT = scratch([128, 128], F32)
        op('pool', lambda en: en.memset(maskT, 1.0), [], ['maskT'])
        op('pool', lambda en: en.affine_select(out=maskT.rearrange("p (t c) -> p t c", c=16),
                                               in_=maskT.rearrange("p (t c) -> p t c", c=16),
                                               compare_op=ALU.is_ge, fill=0.0, base=15,
                                               pattern=[[16, 8], [0, 16]], channel_multiplier=-1),
           ['maskT'], ['maskT'])
        ttmp = scratch([128, 2, 128], F32)
        for g in range(32):
            gh, g16 = g // 16, g % 16
            sl = slice(64 * gh, 64 * gh + 64)
            b = bank()
            op('pe', lambda en, b=b, sl=sl, g16=g16: en.matmul(
                psf[b][:, 0:128], lhsT=Lr[sl, g16].rearrange("p s c -> p (s c)"),
                rhs=Rr[sl, g16].rearrange("p s c -> p (s c)"), start=True, stop=False),
               ['Lr', 'Rr'], ['ps%d' % b])
            op('pe', lambda en, b=b, sl=sl, g16=g16: en.matmul(
                psf[b][:, 0:128], lhsT=Li[sl, g16].rearrange("p s c -> p (s c)"),
                rhs=Ri[sl, g16].rearrange("p s c -> p (s c)"), start=False, stop=True),
               ['Li', 'Ri'], ['ps%d' % b])
            j = g % 2
            op('dve', lambda en, b=b, j=j: en.tensor_tensor(out=ttmp[:, j, :], in0=psf[b][:, 0:128], in1=maskT,
                                                            op=ALU.mult), ['ps%d' % b, 'maskT'], ['ttmp%d' % j])
            op('dve', lambda en, g=g, j=j: en.scalar_tensor_tensor(out=Tm[:, g, :], in0=identf,
                                                                   scalar=Dcol[:, g:g + 1], in1=ttmp[:, j, :],
                                                                   op0=ALU.mult, op1=ALU.add),
               ['ttmp%d' % j, 'identf', 'Dcol'], ['Tm'])
        cprod(Rr, Ri, 9, CCre, CCim, 'CC0', 'CC1', 'Rr', 'Ri', neg_im=True)
        op('dve', lambda en: en.tensor_copy(out=Qm[:, :, 0, :], in_=Rr.rearrange("p g t c -> p g (t c)")),
           ['Rr'], ['Qm'])
        op('dve', lambda en: en.tensor_copy(out=Qm[:, :, 1, :], in_=Ri.rearrange("p g t c -> p g (t c)")),
           ['Ri'], ['Qm'])
        cprod(Lr, Li, 17, bbr, bbi, 'bbr', 'bbi', 'Lr', 'Li')
        for g in range(32):
            gh, g16 = g // 16, g % 16
            sl = slice(64 * gh, 64 * gh + 64)
            b = bank()
            for ri, LL in enumerate([Lr, Li]):
                op('pe', lambda en, b=b, sl=sl, g16=g16, ri=ri, LL=LL: en.matmul(
                    psf[b][:, ri * 64:ri * 64 + 64], lhsT=LL[sl, g16].rearrange("p s c -> p (s c)"),
                    rhs=identf[sl, sl], start=True, stop=True), ['Lr', 'Li', 'identf'], ['ps%d' % b])
            copy_any(Pm[:, g, :], psf[b][:, 0:128], ['ps%d' % b], ['Pm'])
        K8 = 16
        for blk in range(2):
            op('dve', lambda en, blk=blk: en.tensor_copy(
                out=C1[:, blk], in_=AR[:, K8, :].unsqueeze(2).broadcast_to([128, 16, 4])), ['AR'], ['C1'])
        op('dve', lambda en: en.tensor_scalar(out=C2[:, 0], in0=AI[:, K8, :].unsqueeze(2).broadcast_to([128, 16, 4]),
                                              scalar1=-1.0, scalar2=None, op0=ALU.mult), ['AI'], ['C2'])
        op('dve', lambda en: en.tensor_copy(out=C2[:, 1], in_=AI[:, K8, :].unsqueeze(2).broadcast_to([128, 16, 4])),
           ['AI'], ['C2'])

        i_prep1 = len(P.ops)
        convs = P.ops[i_conv0:i_conv1]
        preps = P.ops[i_conv1:i_prep1]
        P.ops[i_conv0:i_prep1] = merge_ops(preps, convs)
        P.barrier()
        op('pool', lambda en: en.memset(Pprev, 0.0), [], ['Pprev0', 'Pprev1'])
        op('pool', lambda en: en.memset(Pcur, 0.0), [], ['Pcur0', 'Pcur1'])
        op('pool', lambda en: en.memset(Pmeta, 0.0), [], ['Pmeta0', 'Pmeta1'])
        op('pool', lambda en: en.memset(Vt, 1.0), [], ['Vt'])
        P.barrier()
        bank_ctr['front'] = True

        def ssm_recurrence(NS, NCH, z0_pp):
            F = 16 * NS
            Vv = Vs[:, :, :, 0:NS * NCH].rearrange("p r g (s k) -> p r g s k", k=NCH)
            Hv = Hb[:, :, :, 0:NS * NCH].rearrange("p r g (s k) -> p r g s k", k=NCH)
            c1 = C1[:, :, :, 0:NS]
            c2 = C2[:, :, :, 0:NS]
            pp = z0_pp
            for k in range(NCH):
                zc = Z[:, pp, :, 0:F].rearrange("p b (g s) -> p b g s", s=NS)
                zn = Z[:, 1 - pp, :, 0:F].rearrange("p b (g s) -> p b g s", s=NS)
                t1 = T1[:, :, 0:F].rearrange("p b (g s) -> p b g s", s=NS)
                t2 = T2[:, :, 0:F].rearrange("p b (g s) -> p b g s", s=NS)
                op('pool', lambda en, zc=zc, k=k: en.tensor_copy(out=Hv[:, :, :, :, k], in_=zc[:, 0:2]),
                   ['Z%d' % pp], ['Hb'])
                op('pool', lambda en, zc=zc, t1=t1: en.tensor_tensor(out=t1, in0=zc[:, 0:2], in1=c1, op=ALU.mult),
                   ['Z%d' % pp, 'C1'], ['T1'])
                op('pool', lambda en, zc=zc, t2=t2: en.tensor_tensor(out=t2, in0=zc[:, 1:3], in1=c2, op=ALU.mult),
                   ['Z%d' % pp, 'C2'], ['T2'])
                op('pool', lambda en, t1=t1, t2=t2: en.tensor_add(out=t1, in0=t1, in1=t2), ['T1', 'T2'], ['T1'])
                op('pool', lambda en, zn=zn, t1=t1, k=k: en.tensor_add(out=zn[:, 0:2], in0=t1, in1=Vv[:, :, :, :, k]),
                   ['T1', 'Vs'], ['Z%d' % (1 - pp)])
                op('pool', lambda en, zn=zn: en.tensor_copy(out=zn[:, 2], in_=zn[:, 0]),
                   ['Z%d' % (1 - pp)], ['Z%d' % (1 - pp)])
                pp = 1 - pp
            return pp

        def norm_transpose(NS, TT, dst, dname):
            for m in range(NS):
                xb = m % 2
                op('act', lambda en, m=m, xb=xb: en.activation(out=xnb[0:TT, xb], in_=xt[0:TT, m, :], func=AF.Square,
                                                               accum_out=ss[0:TT, m:m + 1]),
                   ['xt%d' % m], ['xnb%d' % xb, 'ss%d' % m])
            ssn = ['ss%d' % m for m in range(NS)]
            rsn = ['rs%d' % m for m in range(NS)]
            op('act', lambda en: en.activation(out=rs[0:TT, 0:NS], in_=ss[0:TT, 0:NS], func=AF.Ln, scale=1.0 / D,
                                               bias=epsb[0:TT, 0:1]), ssn, rsn)
            op('act', lambda en: en.activation(out=rs[0:TT, 0:NS], in_=rs[0:TT, 0:NS], func=AF.Exp, scale=-0.5),
               rsn, rsn)
            for m in range(NS):
                xb = m % 2
                op('dve', lambda en, m=m, xb=xb: en.tensor_scalar(out=xnb[0:TT, xb], in0=xt[0:TT, m, :],
                                                                  scalar1=rs[0:TT, m:m + 1], scalar2=None,
                                                                  op0=ALU.mult),
                   ['xt%d' % m, 'rs%d' % m], ['xnb%d' % xb])
                b = bank()
                for kc in range(8):
                    op('pe', lambda en, b=b, kc=kc, xb=xb: en.transpose(
                        out=psb[b][:, kc * 128:kc * 128 + TT], in_=xnb[0:TT, xb, kc * 128:(kc + 1) * 128],
                        identity=identb[0:TT, 0:TT]), ['xnb%d' % xb, 'identb'], ['ps%d' % b])
                copy_any(dst[:, :, m * TT:(m + 1) * TT],
                         psb[b].rearrange("p (k t) -> p k t", t=128)[:, :, 0:TT], ['ps%d' % b], ['%s%d' % (dname, m)])

        XMALL_ = ['XM0', 'XM1', 'XM2', 'XM3']
        US_ = ['US'] if overlap else []
        UZ_ = ['UZ'] if overlap else []
        YA_ = ['yA0', 'yA1', 'yA2', 'yA3']
        HNALL_ = ['XM0', 'XM1', 'XM2', 'XM3']
        zstate = {'pp': 0}

        def block(kind, j, phase):
            if kind == 'P':
                NS, TT = 4, 128
            elif kind == 'S':
                NS, TT = 4, 64
            else:
                NS, TT = 1, 16
            NCH = TT // 8
            NQ = NS * NCH
            NT = NS * TT
            slot = j % 2 if kind == 'P' else 0
            pslot = 1 - slot
            full = kind != 'M'

            def ydst(m):
                if kind == 'P':
                    return yp[m, j * 128:(j + 1) * 128, :]
                return ys[m, :, :]

            def xsrc(m):
                if kind == 'P':
                    return xp[m, j * 128:(j + 1) * 128, :]
                if kind == 'S':
                    return xs[m, :, :]
                return meta

            def f1():
                for m in range(NS):
                    dma(lambda en, m=m: en.dma_start(out=xt[0:TT, m, :], in_=xsrc(m)), [], ['xt%d' % m])
                norm_transpose(NS, TT, XM, 'XM')

                if stop == 5.01 and kind == 'P':
                    return True
                for s_ in range(8):
                    b = bank()
                    for kc in range(8):
                        op('pe', lambda en, b=b, kc=kc, s_=s_: en.matmul(
                            psf[b][0:NQ, :], lhsT=XM[:, kc, 0:NT].rearrange("p (q s) -> p q s", s=8)[:, :, s_],
                            rhs=win[:, kc, 768:1280], start=(kc == 0), stop=(kc == 7)), XMALL_ + ['win'], ['ps%d' % b])
                    copy_any(u_ks[0:NQ, :, s_, :], psf[b][0:NQ, :].rearrange("p (g c) -> p g c", c=16),
                             ['ps%d' % b], ['u_ks', *US_])
                if stop == 5.05 and kind == 'P':
                    return True
                def qkv_mm(m):
                        bq = bank()
                        bk = bank()
                        for kc in range(8):
                            op('pe', lambda en, kc=kc, m=m, bq=bq: en.matmul(
                                psf[bq][0:TT, :], lhsT=XM[:, kc, m * TT:(m + 1) * TT], rhs=win[:, kc, 0:512],
                                start=(kc == 0), stop=(kc == 7)), ['XM%d' % m, 'win'], ['ps%d' % bq])
                        for kc in range(8):
                            op('pe', lambda en, kc=kc, m=m, bk=bk: en.matmul(
                                psf[bk][0:TT, 0:256], lhsT=XM[:, kc, m * TT:(m + 1) * TT], rhs=win[:, kc, 512:768],
                                start=(kc == 0), stop=(kc == 7)), ['XM%d' % m, 'win'], ['ps%d' % bk])
                        return bq, bk

                def qk_chain(m, bq, bk):
                        mp = m % 2
                        op('act', lambda en, bq=bq: en.activation(out=qsq[0:TT, 0:512], in_=psf[bq][0:TT, :], func=AF.Square),
                           ['ps%d' % bq], ['qsq'])
                        op('act', lambda en, bk=bk: en.activation(out=qsq[0:TT, 512:640], in_=psf[bk][0:TT, 0:128],
                                                                  func=AF.Square), ['ps%d' % bk], ['qsq'])
                        op('dve', lambda en, mp=mp: en.tensor_reduce(
                            out=st10[0:TT, mp, :], in_=qsq[0:TT, :].rearrange("p (h d) -> p h d", d=64), axis=AX.X,
                            op=ALU.add), ['qsq'], ['st10%d' % mp])
                        rstd_from(st10[0:TT, mp, :], r10[0:TT, mp, :], 64, 'st10%d' % mp, 'r10%d' % mp)
                        op('dve', lambda en, mp=mp, bq=bq: en.tensor_tensor(
                            out=qnb[0:TT, mp, :].rearrange("p (h d) -> p h d", d=64),
                            in0=psf[bq][0:TT, :].rearrange("p (h d) -> p h d", d=64),
                            in1=r10[0:TT, mp, 0:8].unsqueeze(2).broadcast_to([TT, 8, 64]), op=ALU.mult),
                           ['ps%d' % bq, 'r10%d' % mp], ['qnb%d' % mp])
                        op('dve', lambda en, mp=mp, bk=bk: en.tensor_tensor(
                            out=kf[0:TT, mp, :].rearrange("p (h d) -> p h d", d=64),
                            in0=psf[bk][0:TT, 0:128].rearrange("p (h d) -> p h d", d=64),
                            in1=r10[0:TT, mp, 8:10].unsqueeze(2).broadcast_to([TT, 2, 64]), op=ALU.mult),
                           ['ps%d' % bk, 'r10%d' % mp], ['kf%d' % mp])
                        op('dve', lambda en, mp=mp: en.tensor_tensor(out=kf[0:TT, mp, :], in0=kf[0:TT, mp, :],
                                                                     in1=gk_t[0:TT].rearrange("p a d -> p (a d)"),
                                                                     op=ALU.mult), ['kf%d' % mp, 'gk_t'], ['kf%d' % mp])
                        op('act', lambda en, mp=mp: en.copy(out=kb[0:TT, mp, :], in_=kf[0:TT, mp, :]),
                           ['kf%d' % mp], ['kb%d' % mp])
                        vdst = Vm[0:TT, :, 0:64] if kind == 'M' else Vt[0:TT, slot, m, :, 0:64]
                        vname = 'Vm' if kind == 'M' else 'Vt%d_%d' % (slot, m)
                        op('act', lambda en, bk=bk, vdst=vdst: en.copy(
                            out=vdst, in_=psf[bk][0:TT, 128:256].rearrange("p (h d) -> p h d", d=64)),
                           ['ps%d' % bk], [vname])
                        need_out = (kind != 'P') or (j == NBLK - 1)
                        if need_out:
                            op('dve', lambda en, bk=bk: en.tensor_copy(out=vf[0:TT, :], in_=psf[bk][0:TT, 128:256]),
                               ['ps%d' % bk], ['vf'])
                            if kind == 'P':
                                dma(lambda en, m=m, mp=mp: en.dma_start(out=kpo[m, 16:144, :], in_=kf[0:TT, mp, :]),
                                    ['kf%d' % mp], [])
                                dma(lambda en, m=m: en.dma_start(out=vpo[m, 16:144, :], in_=vf[0:TT, :]), ['vf'], [])
                            elif kind == 'S':
                                dma(lambda en, m=m, mp=mp: en.dma_start(out=kso[m, :, :], in_=kf[0:TT, mp, :]),
                                    ['kf%d' % mp], [])
                                dma(lambda en, m=m: en.dma_start(out=vso[m, :, :], in_=vf[0:TT, :]), ['vf'], [])
                            else:
                                for mm in range(NSEQ):
                                    dma(lambda en, mm=mm, mp=mp: en.dma_start(out=kpo[mm, 0:16, :], in_=kf[0:TT, mp, :]),
                                        ['kf%d' % mp], [])
                                    dma(lambda en, mm=mm: en.dma_start(out=vpo[mm, 0:16, :], in_=vf[0:TT, :]), ['vf'], [])
                        b = bank()
                        for hk in range(2):
                            op('pe', lambda en, b=b, hk=hk, mp=mp: en.transpose(
                                out=psb[b][0:64, hk * 128:hk * 128 + TT], in_=kb[0:TT, mp, hk * 64:(hk + 1) * 64],
                                identity=identb[0:TT, 0:TT]), ['kb%d' % mp, 'identb'], ['ps%d' % b])
                        if kind == 'M':
                            ktd = KTm[:, :, 0:TT]
                            ktn = 'KTm'
                        else:
                            ktd = KT[:, :, slot, m, 0:TT]
                            ktn = 'KT%d_%d' % (slot, m)
                        op('act', lambda en, b=b, ktd=ktd: en.mul(
                            out=ktd, in_=psb[b][0:64, 0:256].rearrange("p (h t) -> p h t", t=128)[:, :, 0:TT],
                            mul=gq8[:, 0:1]), ['ps%d' % b, 'gq8'], [ktn])
                        if full:
                            b = bank()
                            for h in range(8):
                                op('pe', lambda en, b=b, h=h, mp=mp: en.transpose(
                                    out=psb[b][0:64, h * 128:h * 128 + TT], in_=qnb[0:TT, mp, h * 64:(h + 1) * 64],
                                    identity=identb[0:TT, 0:TT]), ['qnb%d' % mp, 'identb'], ['ps%d' % b])
                            op('dve', lambda en, b=b, m=m: en.tensor_copy(
                                out=QT[:, m, :, 0:TT], in_=psb[b][0:64, :].rearrange("p (h t) -> p h t", t=128)[:, :, 0:TT]),
                               ['ps%d' % b], ['QT%d' % m])


                def stage_qk():
                    if overlap:
                        for m in range(NS):
                            cur = qkv_mm(m)
                            qk_chain(m, *cur)
                    else:
                        prev = None
                        for m in range(NS):
                            cur = qkv_mm(m)
                            if prev is not None:
                                qk_chain(m - 1, *prev)
                            prev = cur
                        qk_chain(NS - 1, *prev)

                if stop == 5.1 and kind == 'P':
                    return True
                for g0 in range(0, 32, 8):
                    b = bank()
                    for g in range(g0, g0 + 8):
                        op('pe', lambda en, b=b, g=g: en.transpose(
                            out=psb[b][:, (g % 8) * 64:(g % 8) * 64 + NQ],
                            in_=u_ks[0:NQ, g].rearrange("p s c -> p (s c)"), identity=identb[0:NQ, 0:NQ]),
                           ['u_ks', *US_, 'identb'], ['ps%d' % b])
                    copy_any(U_all[:, g0:g0 + 8, 0:NQ],
                             psb[b][:, 0:512].rearrange("p (g q) -> p g q", q=64)[:, :, 0:NQ], ['ps%d' % b], ['U_all', *UZ_])
                for gb in range(4):
                    b = bank()
                    for gh in range(2):
                        for gl in range(4):
                            g16 = gb * 4 + gl
                            g = gh * 16 + g16
                            for ri in range(2):
                                op('pe', lambda en, b=b, g=g, gh=gh, gl=gl, ri=ri: en.matmul(
                                    psf[b][64 * gh:64 * gh + 64, (ri * 4 + gl) * 64:(ri * 4 + gl) * 64 + NQ],
                                    lhsT=Pm[:, g, ri * 64:(ri + 1) * 64], rhs=U_all[:, g, 0:NQ], start=True, stop=True),
                                   ['Pm', 'U_all', *UZ_], ['ps%d' % b])
                    copy_any(Vs[:, :, gb * 4:gb * 4 + 4, 0:NQ],
                             psf[b].rearrange("p (r g q) -> p r g q", r=2, q=64)[:, :, :, 0:NQ], ['ps%d' % b], ['Vs'])
                if stop == 5.11 and kind == 'P':
                    return True
                F = 16 * NS
                if kind == 'M':
                    op('pool', lambda en: en.memset(Z[:, 0], 0.0), [], ['Z0'])
                    zstate['pp'] = 0
                elif kind == 'P' and j == 0:
                    zv = Z[:, 0, :, 0:F].rearrange("p b (g s) -> p b g s", s=NS)
                    op('pool', lambda en, zv=zv: en.tensor_copy(
                        out=zv, in_=Zmeta.unsqueeze(3).broadcast_to([128, 3, 16, NS])), ['Zmeta'], ['Z0'])
                    zstate['pp'] = 0
                elif kind == 'S':
                    b = bank()
                    for ri, src in enumerate([s_re, s_im]):
                        dma(lambda en, ri=ri, src=src: en.dma_start(out=attn[:, ri, 0:64], in_=src), [], ['attn%d' % ri])
                        for gh in range(2):
                            op('pe', lambda en, b=b, ri=ri, gh=gh: en.matmul(
                                psf[b][64 * gh:64 * gh + 64, ri * 128:ri * 128 + 128], lhsT=attn[:, ri, 0:64],
                                rhs=identf, start=True, stop=True), ['attn%d' % ri, 'identf'], ['ps%d' % b])
                    for gh in range(2):
                        sl = slice(64 * gh, 64 * gh + 64)
                        for blk, ri in enumerate([0, 1, 0]):
                            src = psf[b][sl, ri * 128:ri * 128 + 128].rearrange("p (s g) -> p g s", g=32)[:, 16 * gh:16 * gh + 16, :]
                            op('dve', lambda en, sl=sl, blk=blk, src=src: en.tensor_copy(
                                out=Z[sl, 0, blk, 0:64].rearrange("p (g s) -> p g s", s=4), in_=src),
                               ['ps%d' % b], ['Z0'])
                    zstate['pp'] = 0
                if stop == 5.12 and kind == 'P':
                    return True
                pp_end = ssm_recurrence(NS, NCH, zstate['pp'])
                zstate['pp'] = pp_end
                if kind == 'M':
                    op('pool', lambda en: en.tensor_copy(out=Zmeta, in_=Z[:, pp_end, :, 0:16]), ['Z%d' % pp_end], ['Zmeta'])
                if stop == 5.13 and kind == 'P':
                    return True
                if kind != 'M' and (kind == 'S' or j == NBLK - 1):
                    o_re, o_im = (srs, sis) if kind == 'S' else (srp, sip)
                    for ri, dst in enumerate([o_re, o_im]):
                        op('dve', lambda en, ri=ri: en.tensor_copy(
                            out=qsq[:, ri * 64:(ri + 1) * 64].rearrange("p (s g) -> p s g", g=16),
                            in_=Z[:, pp_end, ri, 0:64].rearrange("p (g s) -> p s g", s=4)),
                           ['Z%d' % pp_end], ['qsq'])
                        b = bank()
                        op('pe', lambda en, b=b, ri=ri: en.matmul(
                            psf[b][0:64, 0:128], lhsT=qsq[:, ri * 64:(ri + 1) * 64],
                            rhs=identf, start=True, stop=True), ['qsq', 'identf'], ['ps%d' % b])
                        op('dve', lambda en, b=b, ri=ri: en.tensor_copy(out=attn[0:64, ri, 0:128], in_=psf[b][0:64, 0:128]),
                           ['ps%d' % b], ['attn%d' % ri])
                        for s_ in range(4 if stop != 5.14 else 0):
                            for gh in range(2):
                                dma(lambda en, s_=s_, gh=gh, ri=ri, dst=dst: en.dma_start(
                                    out=dst[s_ * 32 + gh * 16:s_ * 32 + gh * 16 + 16, :],
                                    in_=attn[s_ * 16:s_ * 16 + 16, ri, gh * 64:gh * 64 + 64]), ['attn%d' % ri], [])

                stage_qk()
                if kind == 'M':
                    return
                if stop in (5.2, 5.14) and kind == "P":
                    return True
                if kind == 'S':
                    for m in range(NS):
                        dma(lambda en, m=m: en.dma_start(out=kf[:, 0, :], in_=ck[m, 16:144, :]), [], ['kf0'])
                        dma(lambda en, m=m: en.dma_start(out=kf[0:16, 1, :], in_=ck[m, 0:16, :]), [], ['kf1'])
                        dma(lambda en, m=m: en.dma_start(out=vf[:, :], in_=cv[m, 16:144, :]), [], ['vf'])
                        dma(lambda en, m=m: en.dma_start(out=qsq[0:16, 0:128], in_=cv[m, 0:16, :]), [], ['qsq'])
                        op('pool', lambda en: en.tensor_copy(out=kb[:, 0, :], in_=kf[:, 0, :]), ['kf0'], ['kb0'])
                        op('pool', lambda en: en.tensor_copy(out=kb[0:16, 1, :], in_=kf[0:16, 1, :]), ['kf1'], ['kb1'])
                        op('act', lambda en, m=m: en.copy(out=Vt[:, 1, m, :, 0:64],
                                                          in_=vf[:, :].rearrange("p (h d) -> p h d", d=64)),
                           ['vf'], ['Vt1_%d' % m])
                        op('act', lambda en, m=m: en.copy(out=VmS[:, m, :, 0:64], in_=qsq[0:16, 0:128].rearrange("p (h d) -> p h d", d=64)),
                           ['qsq'], ['VmS%d' % m])
                        b = bank()
                        for hk in range(2):
                            op('pe', lambda en, b=b, hk=hk: en.transpose(
                                out=psb[b][0:64, hk * 128:hk * 128 + 128], in_=kb[:, 0, hk * 64:(hk + 1) * 64],
                                identity=identb), ['kb0', 'identb'], ['ps%d' % b])
                            op('pe', lambda en, b=b, hk=hk: en.transpose(
                                out=psb[b][0:64, 256 + hk * 16:256 + hk * 16 + 16], in_=kb[0:16, 1, hk * 64:(hk + 1) * 64],
                                identity=identb[0:16, 0:16]), ['kb1', 'identb'], ['ps%d' % b])
                        op('act', lambda en, b=b, m=m: en.mul(
                            out=KT[:, :, 1, m, :], in_=psb[b][0:64, 0:256].rearrange("p (h t) -> p h t", t=128),
                            mul=gq8[:, 0:1]), ['ps%d' % b, 'gq8'], ['KT1_%d' % m])
                        op('act', lambda en, b=b, m=m: en.mul(
                            out=KTmS[:, m], in_=psb[b][0:64, 256:288].rearrange("p (h t) -> p h t", t=16),
                            mul=gq8[:, 0:1]), ['ps%d' % b, 'gq8'], ['KTmS%d' % m])
                has_prev = (kind == 'S') or (j > 0)
                NQC = 4 * TT

                def att_S(ui, m, hk):
                    c = dict(m=m, hk=hk, pset=ui % 2)
                    qrhs = QT[:, m, 4 * hk:4 * hk + 4, 0:TT]
                    bp = bank() if has_prev else None
                    bc = bank()
                    bm = bank()
                    c.update(bp=bp, bc=bc, bm=bm)
                    if has_prev:
                        op('pe', lambda en: en.matmul(
                            psf[bp][:, 0:NQC], lhsT=KT[:, hk, pslot, m, :], rhs=qrhs, start=True, stop=True),
                           ['KT%d_%d' % (pslot, m), 'QT%d' % m], ['ps%d' % bp])
                    op('pe', lambda en: en.matmul(
                        psf[bc][0:TT, 0:NQC], lhsT=KT[:, hk, slot, m, 0:TT], rhs=qrhs, start=True, stop=True),
                       ['KT%d_%d' % (slot, m), 'QT%d' % m], ['ps%d' % bc])
                    if kind == 'S':
                        ktm = KTmS[:, m, hk, :]
                        ktmn = 'KTmS%d' % m
                        c.update(vmv=VmS[:, m, hk, :], vmn='VmS%d' % m)
                    else:
                        ktm = KTm[:, hk, :]
                        ktmn = 'KTm'
                        c.update(vmv=Vm[:, hk, :], vmn='Vm')
                    op('pe', lambda en: en.matmul(
                        psf[bm][0:16, 0:NQC], lhsT=ktm, rhs=qrhs, start=True, stop=True),
                       [ktmn, 'QT%d' % m], ['ps%d' % bm])
                    return c

                def att_exp(c):
                    pset, bp, bc, bm = c['pset'], c['bp'], c['bc'], c['bm']
                    pv = Pprev[:, pset, :, 0:TT]
                    pc = Pcur[:, pset, :, 0:TT]
                    pm_ = Pmeta[:, pset, :, 0:TT]
                    c.update(pv=pv, pc=pc, pm_=pm_)
                    if has_prev:
                        sp = psf[bp][:, 0:NQC].rearrange("p (h t) -> p h t", t=TT)
                        if kind == 'P':
                            op('act', lambda en: en.activation(out=pv[64:128], in_=sp[64:128], func=AF.Exp),
                               ['ps%d' % bp], ['Pprev%d' % pset])
                            op('act', lambda en: en.activation(out=pv[0:64, :, 0:64], in_=sp[0:64, :, 0:64], func=AF.Exp),
                               ['ps%d' % bp], ['Pprev%d' % pset])
                        else:
                            op('act', lambda en: en.activation(out=pv, in_=sp, func=AF.Exp),
                               ['ps%d' % bp], ['Pprev%d' % pset])
                    sc_ = psf[bc][0:TT, 0:NQC].rearrange("p (h t) -> p h t", t=TT)
                    if kind == 'P':
                        op('act', lambda en: en.activation(out=pc[0:64], in_=sc_[0:64], func=AF.Exp),
                           ['ps%d' % bc], ['Pcur%d' % pset])
                        op('act', lambda en: en.activation(out=pc[64:128, :, 64:128], in_=sc_[64:128, :, 64:128],
                                                           func=AF.Exp), ['ps%d' % bc], ['Pcur%d' % pset])
                    else:
                        op('act', lambda en: en.activation(out=pc[0:TT], in_=sc_, func=AF.Exp),
                           ['ps%d' % bc], ['Pcur%d' % pset])
                    sm = psf[bm][0:16, 0:NQC].rearrange("p (h t) -> p h t", t=TT)
                    op('act', lambda en: en.activation(out=pm_, in_=sm, func=AF.Exp), ['ps%d' % bm], ['Pmeta%d' % pset])

                def att_PV(c):
                    m, hk, pset = c['m'], c['hk'], c['pset']
                    pv, pc, pm_, vmv, vmn = c['pv'], c['pc'], c['pm_'], c['vmv'], c['vmn']
                    bo = bank()
                    for h in range(4):
                        ov = psf[bo][0:TT, h * 65:h * 65 + 65]
                        first = True
                        if has_prev:
                            op('pe', lambda en, ov=ov, h=h: en.matmul(
                                ov, lhsT=pv[:, h, :], rhs=Vt[:, pslot, m, hk, :], start=True, stop=False),
                               ['Pprev%d' % pset, 'Vt%d_%d' % (pslot, m)], ['ps%d' % bo])
                            first = False
                        op('pe', lambda en, ov=ov, h=h, first=first: en.matmul(
                            ov, lhsT=pc[0:TT, h, :], rhs=Vt[0:TT, slot, m, hk, :], start=first, stop=False),
                           ['Pcur%d' % pset, 'Vt%d_%d' % (slot, m)], ['ps%d' % bo])
                        op('pe', lambda en, ov=ov, h=h: en.matmul(
                            ov, lhsT=pm_[:, h, :], rhs=vmv, start=False, stop=True),
                           ['Pmeta%d' % pset, vmn], ['ps%d' % bo])
                    o3 = psf[bo][0:TT, 0:260].rearrange("p (h e) -> p h e", e=65)
                    mp = m % 2
                    op('dve', lambda en: en.tensor_tensor(
                        out=den[0:TT, pset, :], in0=o3[:, :, 64], in1=esink[0:TT, 4 * hk:4 * hk + 4], op=ALU.add),
                       ['ps%d' % bo, 'esink'], ['den%d' % pset])
                    op('dve', lambda en: en.reciprocal(out=den[0:TT, pset, :], in_=den[0:TT, pset, :]),
                       ['den%d' % pset], ['den%d' % pset])
                    op('dve', lambda en: en.tensor_tensor(
                        out=attn[0:TT, mp, hk * 256:(hk + 1) * 256].rearrange("p (h d) -> p h d", d=64),
                        in0=o3[:, :, 0:64], in1=den[0:TT, pset, :].unsqueeze(2).broadcast_to([TT, 4, 64]),
                        op=ALU.mult), ['ps%d' % bo, 'den%d' % pset], ['attn%d' % mp])

                def att_norm(m):
                    mp = m % 2
                    op('act', lambda en: en.activation(out=anb[0:TT], in_=attn[0:TT, mp, :], func=AF.Square,
                                                       accum_out=ss[0:TT, 4 + m:5 + m]),
                       ['attn%d' % mp], ['anb', 'ssa%d' % m])
                    rstd_from(ss[0:TT, 4 + m:5 + m], rs[0:TT, 4 + m:5 + m], 512, 'ssa%d' % m, 'rsa%d' % m)
                    op('dve', lambda en: en.tensor_scalar(out=anb[0:TT], in0=attn[0:TT, mp, :],
                                                          scalar1=rs[0:TT, 4 + m:5 + m], scalar2=None, op0=ALU.mult),
                       ['attn%d' % mp, 'rsa%d' % m], ['anb'])
                    b = bank()
                    for kc in range(4):
                        op('pe', lambda en, kc=kc: en.transpose(out=psb[b][:, kc * 128:kc * 128 + TT],
                                                                in_=anb[0:TT, kc * 128:(kc + 1) * 128],
                                                                identity=identb[0:TT, 0:TT]),
                           ['anb', 'identb'], ['ps%d' % b])
                    copy_any(XM[:, 0:4, m * TT:(m + 1) * TT],
                             psb[b][:, 0:512].rearrange("p (k t) -> p k t", t=128)[:, :, 0:TT], ['ps%d' % b], ['XM%d' % m])

                units = [(m, hk) for m in range(NS) for hk in range(2)]
                ctxs = [None] * len(units)
                if overlap:
                    for ui in range(len(units)):
                        ctxs[ui] = att_S(ui, *units[ui])
                        att_exp(ctxs[ui])
                        att_PV(ctxs[ui])
                        if units[ui][1] == 1:
                            att_norm(units[ui][0])
                else:
                    for ui in range(len(units) + 1):
                        if ui < len(units):
                            ctxs[ui] = att_S(ui, *units[ui])
                            att_exp(ctxs[ui])
                        if ui >= 1:
                            att_PV(ctxs[ui - 1])
                            if units[ui - 1][1] == 1:
                                att_norm(units[ui - 1][0])


            def f2():
                if stop == 5.3 and kind == 'P':
                    return True
                op('act', lambda en: en.copy(out=dumA, in_=epsb), ['epsb'],
                   ['dumA'] + ['gT%d' % i_ for i_ in range(16)] + ['LOCK2'])
                for g0 in range(0, 32, 8):
                    b = bank()
                    for g in range(g0, g0 + 8):
                        gh, g16 = g // 16, g % 16
                        sl = slice(64 * gh, 64 * gh + 64)
                        for th in range(2):
                            ov = psf[b][64 * th:64 * th + NQ, (g % 8) * 64:(g % 8) * 64 + 64]
                            op('pe', lambda en, ov=ov, g=g, th=th: en.matmul(
                                ov, lhsT=U_all[:, g, 0:NQ], rhs=Tm[:, g, 64 * th:64 * th + 64], start=True, stop=False),
                               ['U_all', *UZ_, 'Tm'], ['ps%d' % b])
                            for ri in range(2):
                                op('pe', lambda en, ov=ov, sl=sl, g16=g16, th=th, ri=ri: en.matmul(
                                    ov, lhsT=Hb[sl, ri, g16, 0:NQ], rhs=Qm[sl, g16, ri, 64 * th:64 * th + 64],
                                    start=False, stop=(ri == 1)), ['Hb', 'Qm'], ['ps%d' % b])
                    for th in range(2):
                        pr = slice(64 * th, 64 * th + NQ)
                        op('act', lambda en, b=b, g0=g0, pr=pr: en.activation(
                            out=yA[pr].rearrange("p t (g c) -> p g t c", c=16)[:, g0:g0 + 8],
                            in_=psf[b][pr, :].rearrange("p (g t c) -> p g t c", t=4, c=16), func=AF.Gelu_apprx_tanh),
                           ['ps%d' % b, 'LOCK2'], YA_)
                    op('dve', lambda en, g0=g0: en.tensor_copy(out=zs_bf[:, :, g0 * 16:(g0 + 8) * 16],
                                                               in_=yA[:, :, g0 * 16:(g0 + 8) * 16]), YA_, ['zs_bf', *US_])
                YA = ['yA0', 'yA1', 'yA2', 'yA3']

                def d2_T(t4):
                    b = bank()
                    for kc in range(4):
                        op('pe', lambda en, kc=kc: en.transpose(
                            out=psb[b][:, kc * 128:(kc + 1) * 128], in_=zs_bf[:, t4, kc * 128:(kc + 1) * 128],
                            identity=identb), ['zs_bf', *US_, 'identb'], ['ps%d' % b])
                    op('dve', lambda en: en.tensor_copy(
                        out=zsT[:, :, t4, :], in_=psb[b][:, 0:512].rearrange("p (k q) -> p k q", q=128)),
                       ['ps%d' % b], ['zsT%d' % t4, *UZ_])

                def d2_G(t4):
                    b = bank()
                    for th in range(2):
                        ov = psf[b][64 * th:64 * th + NQ, :]
                        for kc in range(4):
                            op('pe', lambda en, ov=ov, kc=kc, th=th: en.matmul(
                                ov, lhsT=zsT[:, kc, t4, 64 * th:64 * th + NQ], rhs=wglu[:, kc, :], start=(kc == 0),
                                stop=False), ['zsT%d' % t4, *UZ_, 'wglu'], ['ps%d' % b])
                        op('pe', lambda en, ov=ov: en.matmul(ov, lhsT=onesr[0:1, 0:NQ], rhs=bglu[0:1, :],
                                                             start=False, stop=True),
                           ['onesr', 'bglu'], ['ps%d' % b])
                    for th in range(2):
                        pr = slice(64 * th, 64 * th + NQ)
                        op('act', lambda en, pr=pr: en.activation(out=yB[pr, t4, :], in_=psf[b][pr, :],
                                                                  func=AF.Sigmoid), ['ps%d' % b], ['yB%d' % t4])
                    op('dve', lambda en: en.tensor_mul(out=yB[:, t4, :], in0=yB[:, t4, :], in1=yA[:, t4, :]),
                       ['yA%d' % t4, 'yB%d' % t4], ['yB%d' % t4])
                    op('act', lambda en: en.activation(out=yA[:, t4, :], in_=yB[:, t4, :], func=AF.Square,
                                                       accum_out=ssS[:, t4:t4 + 1]),
                       ['yB%d' % t4], ['yA%d' % t4, 'ssS%d' % t4])

                def d2_N(t4):
                    op('dve', lambda en: en.tensor_scalar(out=sn_bf[:, t4, :], in0=yB[:, t4, :],
                                                          scalar1=rS[:, t4:t4 + 1], scalar2=None, op0=ALU.mult),
                       ['yB%d' % t4, 'rS'], ['sn_bf%d' % t4, *US_])

                def d2_S(t4):
                    b = bank()
                    for kc in range(4):
                        op('pe', lambda en, kc=kc: en.transpose(
                            out=psb[b][:, kc * 128:(kc + 1) * 128], in_=sn_bf[:, t4, kc * 128:(kc + 1) * 128],
                            identity=identb), ['sn_bf%d' % t4, *US_, 'identb'], ['ps%d' % b])
                    src = psb[b][:, 0:512].rearrange("p (k h q) -> p k h q", h=2, q=64)[:, :, :, 0:NQ]
                    dst = XM[:, 4:8, 0:NT].rearrange("p k (q h t) -> p k h q t", h=2, t=4)[:, :, :, :, t4]
                    copy_any(dst, src, ['ps%d' % b], XMALL_)

                for t4 in range(4):
                    d2_T(t4)
                    d2_G(t4)
                op('act', lambda en: en.activation(out=rS, in_=ssS, func=AF.Ln, scale=1.0 / 512, bias=epsb[:, 0:1]),
                   ['ssS0', 'ssS1', 'ssS2', 'ssS3'], ['rS'])
                op('act', lambda en: en.activation(out=rS, in_=rS, func=AF.Exp, scale=-0.5), ['rS'], ['rS'])
                for t4 in range(4):
                    d2_N(t4)
                for t4 in range(4):
                    d2_S(t4)

                if stop == 5.4 and kind == 'P':
                    return True
                def wout_mm(m):
                    b1 = bank()
                    b2 = bank()
                    for n, bb_ in enumerate([b1, b2]):
                        for kc in range(8):
                            op('pe', lambda en, bb_=bb_, kc=kc, n=n: en.matmul(
                                psf[bb_][0:TT, :], lhsT=XM[:, kc, m * TT:(m + 1) * TT],
                                rhs=wout[:, kc, n * 512:(n + 1) * 512],
                                start=(kc == 0), stop=(kc == 7)), ['XM%d' % m, 'wout'], ['ps%d' % bb_])
                    for n, bb_ in enumerate([b1, b2]):
                        op('dve', lambda en, bb_=bb_, n=n: en.tensor_tensor(
                            out=xt[0:TT, m, n * 512:(n + 1) * 512], in0=psf[bb_][0:TT, :],
                            in1=xt[0:TT, m, n * 512:(n + 1) * 512], op=ALU.add),
                           ['ps%d' % bb_, 'xt%d' % m], ['xt%d' % m])
                    if overlap:
                        dma(lambda en: en.dma_start(out=ydst(m), in_=xt[0:TT, m, :]), ['xt%d' % m],
                            ['yd%d_0' % m, 'yd%d_1' % m])

                def norm2(m):
                    xb = m % 2
                    op('act', lambda en: en.activation(out=xnb[0:TT, xb], in_=xt[0:TT, m, :], func=AF.Square,
                                                       accum_out=ss[0:TT, m:m + 1]),
                       ['xt%d' % m], ['xnb%d' % xb, 'ss%d' % m])
                    rstd_from(ss[0:TT, m:m + 1], rs[0:TT, m:m + 1], D, 'ss%d' % m, 'rs%d' % m)
                    op('dve', lambda en: en.tensor_scalar(out=xnb[0:TT, xb], in0=xt[0:TT, m, :],
                                                          scalar1=rs[0:TT, m:m + 1], scalar2=None, op0=ALU.mult),
                       ['xt%d' % m, 'rs%d' % m], ['xnb%d' % xb])

                def norm2_T(m):
                    xb = m % 2
                    b = bank()
                    for kc in range(8):
                        op('pe', lambda en, kc=kc: en.transpose(
                            out=psb[b][:, kc * 128:kc * 128 + TT], in_=xnb[0:TT, xb, kc * 128:(kc + 1) * 128],
                            identity=identb[0:TT, 0:TT]), ['xnb%d' % xb, 'identb'], ['ps%d' % b])
                    copy_any(hnT[:, :, m * TT:(m + 1) * TT],
                             psb[b].rearrange("p (k t) -> p k t", t=128)[:, :, 0:TT], ['ps%d' % b], ['XM%d' % m])

                for m in range(NS + 1):
                    if m < NS:
                        wout_mm(m)
                        norm2(m)
                    if m >= 1:
                        norm2_T(m - 1)
                if stop == 5.5 and kind == 'P':
                    return True

            def g():
                op('dve', lambda en: en.tensor_copy(out=dumD, in_=epsb), ['epsb'],
                   ['dumD'] + YA_ + ['yB0', 'yB1', 'yB2', 'yB3', 'LOCK1'])
                wu_i = mlp_ctr['wu']
                wd_i = mlp_ctr['wd']
                for hh in range(2):
                    for i in range(8):
                        su = wu_i % 3
                        wu_i += 1
                        h0 = (16 * hh + 2 * i) * 128
                        dma(lambda en, su=su, h0=h0: en.dma_start(out=wup_s[:, su], in_=wup_scr[:, :, h0:h0 + 256]),
                            ['wup_scr'], ['wup_s%d' % su])
                        for hl in range(2):
                            ht = 2 * i + hl
                            b = ht % 4
                            for kc in range(8):
                                op('pe', lambda en, b=b, su=su, hl=hl, kc=kc: en.matmul(
                                    psf[b][:, 0:NT], lhsT=wup_s[:, su, kc, hl * 128:(hl + 1) * 128],
                                    rhs=hnT[:, kc, 0:NT], start=(kc == 0), stop=(kc == 7)),
                                   ['wup_s%d' % su] + HNALL_, ['ps%d' % b])
                            rj = ht % 2
                            op('act', lambda en, b=b, rj=rj: en.activation(out=rt[:, rj, 0:NT], in_=psf[b][:, 0:NT],
                                                                           func=AF.Relu), ['ps%d' % b], ['rt%d' % rj])
                            op('dve', lambda en, ht=ht, rj=rj: en.tensor_mul(out=gT[:, ht, 0:NT], in0=rt[:, rj, 0:NT],
                                                                             in1=rt[:, rj, 0:NT]),
                               ['rt%d' % rj, 'LOCK1'], ['gT%d' % ht])
                    pieces = [(nh_, grp_) for nh_ in range(2) for grp_ in range(4)]

                    def wdn_load(pi, sd_):
                        nh_, grp_ = pieces[pi]
                        r0_ = 16 * hh + 4 * grp_
                        dma(lambda en: en.dma_start(
                            out=wdn_s[:, sd_], in_=wdn_scr[:, r0_:r0_ + 4, nh_ * 512:(nh_ + 1) * 512]),
                            ['wdn_scr'], ['wdn_s%d' % sd_])

                    wdn_load(0, wd_i % 3)
                    for nh in range(2):
                        cs = slice(nh * 512, (nh + 1) * 512)
                        for grp in range(4):
                            sd = wd_i % 3
                            wd_i += 1
                            pi = nh * 4 + grp
                            if pi + 1 < len(pieces):
                                wdn_load(pi + 1, wd_i % 3)
                            for m in range(NS):
                                for hl in range(4):
                                    ht = 4 * grp + hl
                                    op('pe', lambda en, m=m, sd=sd, hl=hl, ht=ht, grp=grp: en.matmul(
                                        psf[4 + m][0:TT, :], lhsT=gT[:, ht, m * TT:(m + 1) * TT],
                                        rhs=wdn_s[:, sd, hl, :],
                                        start=(grp == 0 and hl == 0), stop=(grp == 3 and hl == 3)),
                                       ['gT%d' % ht, 'wdn_s%d' % sd], ['ps%d' % (4 + m)])
                        for m in range(NS):
                            if hh == 0:
                                op('dve', lambda en, m=m, cs=cs: en.tensor_tensor(
                                    out=xt[0:TT, m, cs], in0=psf[4 + m][0:TT, :], in1=xt[0:TT, m, cs], op=ALU.add),
                                   ['ps%d' % (4 + m), 'xt%d' % m], ['xt%d' % m])
                            else:
                                yj = mlp_ctr['y'] % 2
                                mlp_ctr['y'] += 1
                                dst = ydst(m)[:, cs]
                                op('dve', lambda en, m=m, cs=cs, yj=yj: en.tensor_tensor(
                                    out=yt[0:TT, yj, :], in0=psf[4 + m][0:TT, :], in1=xt[0:TT, m, cs], op=ALU.add),
                                   ['ps%d' % (4 + m), 'xt%d' % m], ['yt%d' % yj])
                                dma(lambda en, dst=dst, yj=yj: en.dma_start(out=dst, in_=yt[0:TT, yj, :]),
                                    ['yt%d' % yj], [])
                mlp_ctr['wu'] = wu_i
                mlp_ctr['wd'] = wd_i

            if phase == 'F1':
                return f1()
            if phase == 'F2':
                return f2()
            return g()

        mlp_ctr = {'wu': 0, 'wd': 0, 'y': 0}
        op('pool', lambda en: en.memset(VmS, 1.0), [], ['VmS0', 'VmS1', 'VmS2', 'VmS3'])

        if stop == 3:
            P.emit(nc, st)
            return nc
        block('M', 0, 'F1')
        if stop == 4:
            P.emit(nc, st)
            return nc
        blocks = [('P', j_) for j_ in range(NBLK)] + ([('S', 0)] if do_sample else [])
        if block(blocks[0][0], blocks[0][1], 'F1') or block(blocks[0][0], blocks[0][1], 'F2'):
            P.emit(nc, st)
            return nc
        for bi, (bk_, bj_) in enumerate(blocks):
            i0 = len(P.ops)
            block(bk_, bj_, 'G')
            i1 = len(P.ops)
            if bi + 1 < len(blocks):
                nk_, nj_ = blocks[bi + 1]
                block(nk_, nj_, 'F1')
                i2 = len(P.ops)
                if overlap:
                    P.ops[i0:i2] = merge_ops(P.ops[i0:i1], P.ops[i1:i2])
                block(nk_, nj_, 'F2')
        P.emit(nc, st)
    return nc


_NC_CACHE = {}


def kernel(x_prompt, x_sample, cache_swa_k, cache_swa_v, state_ssm_re, state_ssm_im,
           meta_tokens, norm1_g, w_in, q_norm_g, k_norm_g, sinks,
           ssm_A_re, ssm_A_im, ssm_log_dt, ssm_B_re, ssm_B_im, ssm_C_re, ssm_C_im, ssm_D,
           w_glu, b_glu, attn_out_g, ssm_out_g, w_out, norm2_g, w_up, w_down):
    f = lambda a: np.ascontiguousarray(np.asarray(a, dtype=np.float32))
    if 'nc' not in _NC_CACHE:
        _NC_CACHE['nc'] = build()
    nc = _NC_CACHE['nc']
    shared = {
        "meta": f(meta_tokens), "norm1_g": f(norm1_g), "w_in": f(w_in[0]), "q_g": f(q_norm_g), "k_g": f(k_norm_g),
        "sinks": f(sinks), "A_re": f(ssm_A_re[0]), "A_im": f(ssm_A_im[0]), "log_dt": f(ssm_log_dt),
        "B_re": f(ssm_B_re[0]), "B_im": f(ssm_B_im[0]), "C_re": f(ssm_C_re[0]).reshape(512, 64),
        "C_im": f(ssm_C_im[0]).reshape(512, 64), "Dp": f(ssm_D[0]), "w_glu": f(w_glu[0]), "b_glu": f(b_glu),
        "ao_g": f(attn_out_g), "so_g": f(ssm_out_g), "w_out": f(w_out[0]), "norm2_g": f(norm2_g),
        "w_up": f(w_up[0]), "w_down": f(w_down[0]),
    }
    xpf, xsf = f(x_prompt), f(x_sample)
    ckf = f(cache_swa_k[0]).reshape(32, 144, 128)
    cvf = f(cache_swa_v[0]).reshape(32, 144, 128)
    srf = f(state_ssm_re[0]).reshape(32 * 32, 64)
    sif = f(state_ssm_im[0]).reshape(32 * 32, 64)
    in_maps = []
    for c in range(NCORE):
        sl = slice(c * NSEQ, (c + 1) * NSEQ)
        d = dict(shared)
        d.update({"xp": xpf[sl], "xs": xsf[sl], "ck": ckf[sl], "cv": cvf[sl],
                  "s_re": srf[c * 128:(c + 1) * 128], "s_im": sif[c * 128:(c + 1) * 128]})
        in_maps.append(d)
    res = run_bass_kernel_spmd(nc, in_maps, core_ids=list(range(NCORE)))
    R = res.results
    cat = lambda k: np.concatenate([np.asarray(R[c][k], dtype=np.float32) for c in range(NCORE)], axis=0)
    y_prompt = cat("yp")
    y_sample = cat("ys")
    kp = cat("kpo").reshape(1, 32, 144, 2, 64)
    vp = cat("vpo").reshape(1, 32, 144, 2, 64)
    srp_ = cat("srp").reshape(1, 32, 32, 64)
    sip_ = cat("sip").reshape(1, 32, 32, 64)
    ks_ = cat("kso").reshape(1, 32, 64, 2, 64)
    vs_ = cat("vso").reshape(1, 32, 64, 2, 64)
    srs_ = cat("srs").reshape(1, 32, 32, 64)
    sis_ = cat("sis").reshape(1, 32, 32, 64)
    return (y_prompt, y_sample, kp, vp, srp_, sip_, ks_, vs_, srs_, sis_)
```

```python
import numpy as np
from contextlib import ExitStack
import concourse.bass as bass
import concourse.mybir as mybir
from concourse.bass_utils import run_bass_kernel_spmd

F32 = mybir.dt.float32
BF16 = mybir.dt.bfloat16
I32 = mybir.dt.int32
AF = mybir.ActivationFunctionType
ALU = mybir.AluOpType
AX = mybir.AxisListType

NCORE = 8
D = 1024
SEQ = 2048
NSEQ = 4
NBLK = SEQ // 128
EPS = 1e-6
TWO_PI = float(2 * np.pi)


class Prog:
    NDMASEM = 16
    ENGS = ['pe', 'act', 'dve', 'pool', 'sp']

    def __init__(self):
        self.ops = []

    def op(self, eng, fn, reads=(), writes=(), dma=False, barrier=False):
        self.ops.append(dict(eng=eng, fn=fn, reads=tuple(reads), writes=tuple(writes), dma=dma, barrier=barrier))

    def barrier(self):
        for e in ['pe', 'act', 'dve', 'pool']:
            self.op(e, lambda en: en.nop(nofuse=True), barrier=True)
        self.op('sp', None, barrier=True)

    def emit(self, nc, stack):
        ops = self.ops
        engs = self.ENGS
        last_w = {}
        readers = {}
        last_op = {}
        recent_dma = {e: [] for e in engs}
        for i, o in enumerate(ops):
            if o['barrier']:
                deps = set(v for k, v in last_op.items())
                for e in engs:
                    deps |= set(recent_dma[e][-self.NDMASEM:])
                deps.discard(i)
                o['deps'] = set(d for d in deps if ops[d]['dma'] or ops[d]['eng'] != o['eng'])
                if o['eng'] != 'sp':
                    last_op[o['eng']] = i
                continue
            deps = set()
            for r in o['reads']:
                if r in last_w:
                    deps.add(last_w[r])
            for w in o['writes']:
                if w in last_w:
                    deps.add(last_w[w])
                lastrd = {}
                for rd in readers.get(w, ()):
                    if ops[rd]['dma']:
                        deps.add(rd)
                    else:
                        lastrd[ops[rd]['eng']] = rd
                for rd in lastrd.values():
                    deps.add(rd)
            deps.discard(i)
            nd = set()
            for d in deps:
                od = ops[d]
                if not od['dma'] and not o['dma'] and od['eng'] == o['eng']:
                    if o['eng'] == 'pe':
                        continue
                nd.add(d)
            o['deps'] = nd
            for r in o['reads']:
                readers.setdefault(r, []).append(i)
            for w in o['writes']:
                last_w[w] = i
                readers[w] = []
            if o['dma']:
                recent_dma[o['eng']].append(i)
            else:
                last_op[o['eng']] = i
        needed = set()
        for o in ops:
            needed |= o['deps']
        cnt = {e: 0 for e in engs}
        dcnt = {e: 0 for e in engs}
        for i, o in enumerate(ops):
            e = o['eng']
            if o['dma']:
                k = dcnt[e]
                dcnt[e] += 1
                o['sig'] = ('d', e, k % self.NDMASEM, 16 * (k // self.NDMASEM + 1))
            elif i in needed:
                cnt[e] += 1
                o['sig'] = ('c', e, 0, cnt[e])
            else:
                o['sig'] = None
        sems = {}
        for e in engs:
            sems[('c', e, 0)] = stack.enter_context(nc.semaphore('s_' + e))
        for e in engs:
            if dcnt[e]:
                for k in range(self.NDMASEM):
                    sems[('d', e, k)] = stack.enter_context(nc.semaphore('d_%s_%d' % (e, k)))
        block = stack.enter_context(nc.Block())
        per = {e: [o for o in ops if o['eng'] == e] for e in engs}

        def run(e, engine):
            waited = {}

            def wait(key, val):
                if waited.get(key, 0) >= val:
                    return
                waited[key] = val
                engine.wait_ge(sems[key], val)
            final = {}
            for o in per[e]:
                for d in sorted(o['deps']):
                    s = ops[d]['sig']
                    wait(s[:3], s[3])
                if o['dma']:
                    s = o['sig']
                    if s[3] > 16:
                        wait(s[:3], s[3] - 16)
                    o['fn'](engine).then_inc(sems[s[:3]], 16)
                    final[s[:3]] = s[3]
                elif o['fn'] is not None:
                    ins = o['fn'](engine)
                    if o['sig'] is not None:
                        ins.then_inc(sems[o['sig'][:3]], 1)
            for key, val in final.items():
                wait(key, val)

        @block.tensor
        def _(eng):
            run('pe', eng)

        @block.scalar
        def _(eng):
            run('act', eng)

        @block.vector
        def _(eng):
            run('dve', eng)

        @block.gpsimd
        def _(eng):
            run('pool', eng)

        @block.sync
        def _(eng):
            run('sp', eng)


def merge_ops(a, b):
    out = []
    ia = ib = 0
    na, nb = len(a), len(b)
    while ia < na or ib < nb:
        if ib >= nb or (ia < na and ia * nb <= ib * na):
            out.append(a[ia])
            ia += 1
        else:
            out.append(b[ib])
            ib += 1
    return out


def build(NBLK=NBLK, stop=99, do_sample=True, overlap=False):
    nc = bass.Bass("TRN2", target_bir_lowering=False)

    def din(name, shape):
        return nc.dram_tensor(name, list(shape), F32, kind="ExternalInput").ap()

    def dout(name, shape):
        return nc.dram_tensor(name, list(shape), F32, kind="ExternalOutput").ap()

    xp = din("xp", [NSEQ, SEQ, D])
    xs = din("xs", [NSEQ, 64, D])
    ck = din("ck", [NSEQ, 144, 128])
    cv = din("cv", [NSEQ, 144, 128])
    s_re = din("s_re", [NSEQ * 32, 64])
    s_im = din("s_im", [NSEQ * 32, 64])
    meta = din("meta", [16, D])
    norm1_g = din("norm1_g", [1, D])
    w_in = din("w_in", [D, 1280])
    q_g = din("q_g", [1, 64])
    k_g = din("k_g", [1, 64])
    sinks = din("sinks", [1, 8])
    A_re = din("A_re", [32, 64])
    A_im = din("A_im", [32, 64])
    log_dt = din("log_dt", [1, 32])
    B_re = din("B_re", [32, 64, 16])
    B_im = din("B_im", [32, 64, 16])
    C_re = din("C_re", [512, 64])
    C_im = din("C_im", [512, 64])
    Dp = din("Dp", [32, 16])
    w_glu = din("w_glu", [512, 512])
    b_glu = din("b_glu", [1, 512])
    ao_g = din("ao_g", [1, 512])
    so_g = din("so_g", [1, 512])
    w_out = din("w_out", [D, D])
    norm2_g = din("norm2_g", [1, D])
    w_up = din("w_up", [D, 4096])
    w_down = din("w_down", [4096, D])

    yp = dout("yp", [NSEQ, SEQ, D])
    ys = dout("ys", [NSEQ, 64, D])
    kpo = dout("kpo", [NSEQ, 144, 128])
    vpo = dout("vpo", [NSEQ, 144, 128])
    srp = dout("srp", [NSEQ * 32, 64])
    sip = dout("sip", [NSEQ * 32, 64])
    kso = dout("kso", [NSEQ, 64, 128])
    vso = dout("vso", [NSEQ, 64, 128])
    srs = dout("srs", [NSEQ * 32, 64])
    sis = dout("sis", [NSEQ * 32, 64])

    wup_scr = nc.dram_tensor("wup_scr", [128, 8, 4096], BF16, kind="Internal").ap()
    wdn_scr = nc.dram_tensor("wdn_scr", [128, 32, 1024], BF16, kind="Internal").ap()

    P = Prog()

    def op(eng, fn, r=(), w=()):
        P.op(eng, fn, r, w)

    def dma(fn, r=(), w=()):
        P.op('sp', fn, r, w, dma=True)

    with ExitStack() as st:
        NB = 206 * 1024
        raw = st.enter_context(nc.sbuf_tensor("raw", [128, NB // 2], BF16))
        rawb = raw[:]
        rawf = rawb.bitcast(F32)
        rawi = rawb.bitcast(I32)
        state = {'off': 0}

        offs = {}

        def salloc(shape, dt, at=None, name=None):
            n = 1
            for s_ in shape[1:]:
                n *= s_
            nb = n * (2 if dt == BF16 else 4)
            nb = (nb + 3) // 4 * 4
            if at is None:
                off = state['off']
                state['off'] += nb
                assert state['off'] <= NB, ("SBUF overflow", state['off'])
            else:
                off = at
            if dt == BF16:
                v = rawb[0:shape[0], off // 2: off // 2 + n]
            elif dt == F32:
                v = rawf[0:shape[0], off // 4: off // 4 + n]
            else:
                v = rawi[0:shape[0], off // 4: off // 4 + n]
            if len(shape) > 2:
                names = ['a%d' % i for i in range(len(shape) - 1)]
                kw = {names[i]: shape[1 + i] for i in range(len(shape) - 2)}
                v = v.rearrange("p (%s) -> p %s" % (' '.join(names), ' '.join(names)), **kw)
            if name is not None:
                offs[name] = off
            return v

        psf = []
        psb = []
        for b in range(8):
            t = st.enter_context(nc.psum_tensor("ps%d" % b, [128, 512], F32))
            psf.append(t[:])
            psb.append(t[:].bitcast(BF16))
        bank_ctr = {'i': 0}

        def bank():
            if bank_ctr.get('front') and overlap:
                b = 4 + bank_ctr['i'] % 4
            else:
                b = bank_ctr['i'] % 8
            bank_ctr['i'] += 1
            return b

        win = salloc([128, 8, 1280], BF16)
        wglu = salloc([128, 4, 512], BF16)
        wout = salloc([128, 8, 1024], BF16)
        Tm = salloc([128, 32, 128], BF16)
        Pm = salloc([128, 32, 128], BF16)
        Qm = salloc([128, 16, 2, 128], BF16)
        identf = salloc([128, 128], F32)
        identb = salloc([128, 128], BF16)
        onesr = salloc([1, 128], BF16)
        bglu = salloc([128, 512], BF16)
        C1 = salloc([128, 2, 16, 4], F32)
        C2 = salloc([128, 2, 16, 4], F32)
        gk_t = salloc([128, 2, 64], F32)
        esink = salloc([128, 8], F32)
        gq8 = salloc([64, 1], F32)
        epsb = salloc([128, 1], F32)
        dumA = salloc([128, 1], F32)
        dumD = salloc([128, 1], F32)
        KTm = salloc([64, 2, 16], BF16)
        Vm = salloc([16, 2, 65], BF16)
        kmf = salloc([16, 128], F32)
        vmf = salloc([16, 128], F32)
        Zmeta = salloc([128, 3, 16], F32)
        ss = salloc([128, 8], F32)
        rs = salloc([128, 8], F32)
        st10 = salloc([128, 2, 10], F32)
        r10 = salloc([128, 2, 10], F32)
        den = salloc([128, 2, 4], F32)
        ssS = salloc([128, 4], F32)
        rS = salloc([128, 4], F32)
        KTmS = salloc([64, 4, 2, 16], BF16)
        VmS = salloc([16, 4, 2, 65], BF16)
        xt = salloc([128, 4, 1024], F32, name='xt')
        xnb = salloc([128, 2, 1024], BF16)
        XM = salloc([128, 8, 512], BF16, name='XM')
        hnT = salloc([128, 8, 512], BF16, at=offs['XM'])
        qsq = salloc([128, 640], F32)
        qnb = salloc([128, 2, 512], BF16)
        kf = salloc([128, 2, 128], F32)
        kb = salloc([128, 2, 128], BF16)
        vf = salloc([128, 128], F32)
        QT = salloc([64, 4, 8, 128], BF16)
        KT = salloc([64, 2, 2, 4, 128], BF16)
        Vt = salloc([128, 2, 4, 2, 65], BF16)
        Pprev = salloc([128, 2, 4, 128], BF16)
        Pcur = salloc([128, 2, 4, 128], BF16)
        Pmeta = salloc([16, 2, 4, 128], BF16)
        attn = salloc([128, 2, 512], F32)
        anb = salloc([128, 512], BF16)
        U_all = salloc([128, 32, 64], BF16, name='U_all')
        Vs = salloc([128, 2, 16, 64], F32)
        Hb = salloc([128, 2, 16, 64], BF16)
        Z = salloc([128, 2, 3, 64], F32)
        T1 = salloc([128, 2, 64], F32)
        T2 = salloc([128, 2, 64], F32)
        zsT = salloc([128, 4, 4, 128], BF16, at=offs['U_all'])
        rt = salloc([128, 2, 512], F32)
        yt = salloc([128, 2, 512], F32)
        wup_s = salloc([128, 3, 8, 256], BF16)
        wdn_s = salloc([128, 3, 4, 512], BF16)
        yA = salloc([128, 4, 512], F32, name='yA')
        yB = salloc([128, 4, 512], F32)
        gT = salloc([128, 16, 512], BF16, at=offs['yA'])
        u_ks = salloc([64, 32, 8, 16], BF16, name='u_ks')
        zs_bf = salloc([128, 4, 512], BF16, at=offs['u_ks'])
        sn_bf = salloc([128, 4, 512], BF16, at=offs['u_ks'] + 4096)
        print("SBUF bytes/partition used:", state['off'])

        scr0 = offs['xt']
        scr = {'off': scr0}

        def scratch(shape, dt):
            n = 1
            for s_ in shape[1:]:
                n *= s_
            nb = (n * (2 if dt == BF16 else 4) + 3) // 4 * 4
            v = salloc(shape, dt, at=scr['off'])
            scr['off'] += nb
            assert scr['off'] <= state['off'], "scratch overflow"
            return v

        cp_i = {'i': 0}

        def copy_any(out, in_, r, w, engs=('act', 'dve')):
            e = engs[cp_i['i'] % len(engs)]
            cp_i['i'] += 1
            if e == 'act':
                op('act', lambda en: en.copy(out=out, in_=in_), r, w)
            else:
                op(e, lambda en: en.tensor_copy(out=out, in_=in_), r, w)

        def rstd_from(ssap, rsap, n, rname, wname):
            op('act', lambda en: en.activation(out=rsap, in_=ssap, func=AF.Ln, scale=1.0 / n,
                                               bias=epsb[0:ssap.shape[0], 0:1]), [rname], [wname])
            op('act', lambda en: en.activation(out=rsap, in_=rsap, func=AF.Exp, scale=-0.5), [wname], [wname])

        op('pool', lambda en: en.memset(identf, 0.0), [], ['identf'])
        op('pool', lambda en: en.affine_select(out=identf, in_=identf, compare_op=ALU.not_equal, fill=1.0,
                                               base=0, pattern=[[-1, 128]], channel_multiplier=1),
           ['identf'], ['identf'])
        op('dve', lambda en: en.tensor_copy(out=identb, in_=identf), ['identf'], ['identb'])
        op('pool', lambda en: en.memset(onesr, 1.0), [], ['onesr'])
        op('pool', lambda en: en.memset(epsb, EPS), [], ['epsb'])
        op('pool', lambda en: en.memset(Vm, 1.0), [], ['Vm'])
        dma(lambda en: en.dma_start(out=gk_t[:, 0, :], in_=k_g[0:1, :].broadcast_to([128, 64])), [], ['gk_t'])
        dma(lambda en: en.dma_start(out=gk_t[:, 1, :], in_=k_g[0:1, :].broadcast_to([128, 64])), [], ['gk_t'])
        dma(lambda en: en.dma_start(out=esink, in_=sinks[0:1, :].broadcast_to([128, 8])), [], ['esink'])
        op('act', lambda en: en.activation(out=esink, in_=esink, func=AF.Exp), ['esink'], ['esink'])
        dma(lambda en: en.dma_start(out=gq8, in_=q_g.rearrange("o d -> d o"), allow_slow_non_contiguous=True),
            [], ['gq8'])
        op('dve', lambda en: en.tensor_scalar(out=gq8, in0=gq8, scalar1=0.125, scalar2=None, op0=ALU.mult),
           ['gq8'], ['gq8'])

        if stop == 0.5:
            P.emit(nc, st)
            return nc
        g1 = scratch([128, 8], F32)
        mg = scratch([128, 8], F32)
        g2 = scratch([128, 8], F32)
        dma(lambda en: en.dma_start(out=g1, in_=norm1_g.rearrange("o (k p) -> p (o k)", p=128),
                                    allow_slow_non_contiguous=True), [], ['g1'])
        dma(lambda en: en.dma_start(out=mg[:, 0:4], in_=ao_g.rearrange("o (k p) -> p (o k)", p=128),
                                    allow_slow_non_contiguous=True), [], ['mg'])
        dma(lambda en: en.dma_start(out=mg[:, 4:8], in_=so_g.rearrange("o (k p) -> p (o k)", p=128),
                                    allow_slow_non_contiguous=True), [], ['mg'])
        dma(lambda en: en.dma_start(out=g2, in_=norm2_g.rearrange("o (k p) -> p (o k)", p=128),
                                    allow_slow_non_contiguous=True), [], ['g2'])
        stg = scratch([128, 6, 1280], F32)
        stb = scratch([128, 6, 1024], BF16)
        sc = {'i': 0}

        def cast_rows(dst, src_dram, ncol, gain, gname, wname):
            i = sc['i'] % 6
            sc['i'] += 1
            sname = 'stg%d' % i
            dma(lambda en: en.dma_start(out=stg[:, i, 0:ncol], in_=src_dram), [], [sname])
            e = ['act', 'dve'][sc['i'] % 2]
            rr = [sname] + ([gname] if gain is not None else [])
            if gain is None:
                if e == 'act':
                    op('act', lambda en: en.copy(out=dst, in_=stg[:, i, 0:ncol]), rr, [wname])
                else:
                    op(e, lambda en: en.tensor_copy(out=dst, in_=stg[:, i, 0:ncol]), rr, [wname])
            else:
                if e == 'act':
                    op('act', lambda en: en.mul(out=dst, in_=stg[:, i, 0:ncol], mul=gain), rr, [wname])
                else:
                    op(e, lambda en: en.tensor_scalar(out=dst, in0=stg[:, i, 0:ncol], scalar1=gain, scalar2=None,
                                                      op0=ALU.mult), rr, [wname])

        for kc in range(8):
            cast_rows(win[:, kc, :], w_in[kc * 128:(kc + 1) * 128, :], 1280, g1[:, kc:kc + 1], 'g1', 'win')
        if stop == 0.7:
            P.emit(nc, st)
            return nc
        for kc in range(4):
            cast_rows(wglu[:, kc, :], w_glu[kc * 128:(kc + 1) * 128, :], 512, None, None, 'wglu')
        if stop == 0.8:
            P.emit(nc, st)
            return nc
        for kc in range(8):
            cast_rows(wout[:, kc, :], w_out[kc * 128:(kc + 1) * 128, :], 1024, mg[:, kc:kc + 1], 'mg', 'wout')
        if stop == 0.9:
            P.emit(nc, st)
            return nc
        bgf = scratch([128, 512], F32)
        dma(lambda en: en.dma_start(out=bgf, in_=b_glu[0:1, :].broadcast_to([128, 512])), [], ['bgf'])
        if stop == 0.95:
            P.emit(nc, st)
            return nc
        op('dve', lambda en: en.tensor_copy(out=bglu, in_=bgf), ['bgf'], ['bglu'])
        if stop == 1:
            P.emit(nc, st)
            return nc
        i_conv0 = len(P.ops)
        tasks = []
        for kc in range(8):
            for c4 in range(4):
                tasks.append((w_up[kc * 128:(kc + 1) * 128, c4 * 1024:(c4 + 1) * 1024], g2[:, kc:kc + 1], 'g2',
                              wup_scr[:, kc, c4 * 1024:(c4 + 1) * 1024], 'wup_scr'))
        for ht in range(32):
            tasks.append((w_down[ht * 128:(ht + 1) * 128, :], None, None, wdn_scr[:, ht, :], 'wdn_scr'))
        KLOOK = 3

        def conv_load(t):
            i = t % 6
            src = tasks[t][0]
            dma(lambda en, i=i, src=src: en.dma_start(out=stg[:, i, 0:1024], in_=src), [], ['stg%d' % i])

        for t in range(KLOOK):
            conv_load(t)
        for t in range(len(tasks)):
            i = t % 6
            src, gain, gname, dstd, dname = tasks[t]
            e = ['act', 'dve'][t % 2]
            rr = ['stg%d' % i] + ([gname] if gain is not None else [])
            if gain is None:
                if e == 'act':
                    op('act', lambda en, i=i: en.copy(out=stb[:, i, :], in_=stg[:, i, 0:1024]), rr, ['stb%d' % i])
                else:
                    op('dve', lambda en, i=i: en.tensor_copy(out=stb[:, i, :], in_=stg[:, i, 0:1024]), rr,
                       ['stb%d' % i])
            else:
                if e == 'act':
                    op('act', lambda en, i=i, gain=gain: en.mul(out=stb[:, i, :], in_=stg[:, i, 0:1024], mul=gain),
                       rr, ['stb%d' % i])
                else:
                    op('dve', lambda en, i=i, gain=gain: en.tensor_scalar(
                        out=stb[:, i, :], in0=stg[:, i, 0:1024], scalar1=gain, scalar2=None, op0=ALU.mult),
                       rr, ['stb%d' % i])
            if t + KLOOK < len(tasks):
                conv_load(t + KLOOK)
            dma(lambda en, i=i, dstd=dstd: en.dma_start(out=dstd, in_=stb[:, i, :]), ['stb%d' % i], [dname])

        i_conv1 = len(P.ops)
        if stop == 2:
            P.emit(nc, st)
            return nc
        Are = scratch([128, 16], F32)
        Aim = scratch([128, 16], F32)
        dtl = scratch([128, 16], F32)
        Bre = scratch([128, 16, 16], F32)
        Bim = scratch([128, 16, 16], F32)
        CCre = scratch([128, 16, 16], F32)
        CCim = scratch([128, 16, 16], F32)
        Dcol = scratch([128, 32], F32)
        for gh in range(2):
            sl = slice(64 * gh, 64 * gh + 64)
            gs = slice(16 * gh, 16 * gh + 16)
            dma(lambda en, sl=sl, gs=gs: en.dma_start(out=Are[sl, :], in_=A_re[gs, :].rearrange("g p -> p g"),
                                                      allow_slow_non_contiguous=True), [], ['Are'])
            dma(lambda en, sl=sl, gs=gs: en.dma_start(out=Aim[sl, :], in_=A_im[gs, :].rearrange("g p -> p g"),
                                                      allow_slow_non_contiguous=True), [], ['Aim'])
            dma(lambda en, sl=sl, gs=gs: en.dma_start(out=dtl[sl, :], in_=log_dt[0:1, gs].broadcast_to([64, 16])),
                [], ['dtl'])
            dma(lambda en, sl=sl, gs=gs: en.dma_start(out=Bre[sl, :, :], in_=B_re[gs].rearrange("g p c -> p g c")),
                [], ['Bre'])
            dma(lambda en, sl=sl, gs=gs: en.dma_start(out=Bim[sl, :, :], in_=B_im[gs].rearrange("g p c -> p g c")),
                [], ['Bim'])
        for s_ in range(8):
            dma(lambda en, s_=s_: en.dma_start(out=Dcol[16 * s_:16 * s_ + 16, :], in_=Dp.rearrange("g c -> c g"),
                                               allow_slow_non_contiguous=True), [], ['Dcol'])
        Cst = scratch([128, 8, 64], F32)
        for ri, Csrc in enumerate([C_re, C_im]):
            for j in range(4):
                dma(lambda en, ri=ri, j=j, Csrc=Csrc: en.dma_start(out=Cst[:, ri * 4 + j, :],
                                                                   in_=Csrc[j * 128:(j + 1) * 128, :]),
                    [], ['Cst%d' % (ri * 4 + j)])
        for ri, CC in enumerate([CCre, CCim]):
            b = bank()
            for j in range(4):
                gh = j // 2
                op('pe', lambda en, ri=ri, j=j, gh=gh, b=b: en.matmul(
                    psf[b][64 * gh:64 * gh + 64, (j % 2) * 128:(j % 2) * 128 + 128],
                    lhsT=Cst[:, ri * 4 + j, :], rhs=identf, start=True, stop=True),
                   ['Cst%d' % (ri * 4 + j), 'identf'], ['ps%d' % b])
            op('dve', lambda en, CC=CC, b=b: en.tensor_copy(
                out=CC, in_=psf[b][:, 0:256].rearrange("p (g c) -> p g c", c=16)), ['ps%d' % b], ['CC%d' % ri])

        def dv(fn, r, w, e='dve'):
            op(e, fn, r, w)

        lre = scratch([128, 16], F32)
        dtv = scratch([128, 16], F32)
        lrd = scratch([128, 16], F32)
        thd = scratch([128, 16], F32)
        op('act', lambda en: en.activation(out=dtv, in_=dtl, func=AF.Exp), ['dtl'], ['dtv'])
        dv(lambda en: en.tensor_scalar(out=lre, in0=Are, scalar1=-1e-4, scalar2=None, op0=ALU.min), ['Are'], ['lre'])
        dv(lambda en: en.tensor_mul(out=lrd, in0=lre, in1=dtv), ['lre', 'dtv'], ['lrd'])
        dv(lambda en: en.tensor_mul(out=thd, in0=Aim, in1=dtv), ['Aim', 'dtv'], ['thd'])
        KV = [0, -1, -2, -3, -4, -5, -6, -7] + list(range(0, 9)) + [7, 6, 5, 4, 3, 2, 1, 0]
        NK = len(KV)
        kv = scratch([128, NK], F32)
        for i, kval in enumerate(KV):
            op('pool', lambda en, i=i, kval=kval: en.memset(kv[:, i:i + 1], float(kval)), [], ['kv'])
        mag = scratch([128, NK, 16], F32)
        ang = scratch([128, 2, NK, 16], F32)
        angn = scratch([128, 2, NK, 16], F32)
        angi = scratch([128, 2, NK, 16], I32)
        kvb = kv.unsqueeze(2).broadcast_to([128, NK, 16])
        dv(lambda en: en.tensor_tensor(out=mag, in0=lrd.unsqueeze(1).broadcast_to([128, NK, 16]), in1=kvb,
                                       op=ALU.mult), ['lrd', 'kv'], ['mag'])
        op('act', lambda en: en.activation(out=mag, in_=mag, func=AF.Exp), ['mag'], ['mag'])
        dv(lambda en: en.tensor_tensor(out=ang[:, 0], in0=thd.unsqueeze(1).broadcast_to([128, NK, 16]), in1=kvb,
                                       op=ALU.mult), ['thd', 'kv'], ['ang'])
        OFFS = TWO_PI * 40
        dv(lambda en: en.tensor_scalar(out=ang[:, 1], in0=ang[:, 0], scalar1=OFFS + float(np.pi / 2), scalar2=None,
                                       op0=ALU.add), ['ang'], ['ang'])
        dv(lambda en: en.tensor_scalar(out=ang[:, 0], in0=ang[:, 0], scalar1=OFFS, scalar2=None, op0=ALU.add),
           ['ang'], ['ang'])
        dv(lambda en: en.tensor_scalar(out=angn, in0=ang, scalar1=1.0 / TWO_PI, scalar2=None, op0=ALU.mult),
           ['ang'], ['angn'])
        dv(lambda en: en.tensor_copy(out=angi, in_=angn), ['angn'], ['angi'])
        dv(lambda en: en.tensor_copy(out=angn, in_=angi), ['angi'], ['angn'])
        dv(lambda en: en.scalar_tensor_tensor(out=ang, in0=angn, scalar=-TWO_PI, in1=ang, op0=ALU.mult, op1=ALU.add),
           ['angn', 'ang'], ['ang'])
        dv(lambda en: en.tensor_scalar(out=angn, in0=ang, scalar1=float(np.pi), scalar2=-TWO_PI, op0=ALU.is_gt,
                                       op1=ALU.mult), ['ang'], ['angn'])
        dv(lambda en: en.tensor_add(out=ang, in0=ang, in1=angn), ['ang', 'angn'], ['ang'])
        dv(lambda en: en.tensor_scalar(out=ang, in0=ang, scalar1=float(np.pi), scalar2=-float(np.pi), op0=ALU.min,
                                       op1=ALU.max), ['ang'], ['ang'])
        op('act', lambda en: en.activation(out=ang, in_=ang, func=AF.Sin), ['ang'], ['ang'])
        AR = scratch([128, NK, 16], F32)
        AI = scratch([128, NK, 16], F32)
        dv(lambda en: en.tensor_mul(out=AI, in0=mag, in1=ang[:, 0]), ['mag', 'ang'], ['AI'])
        dv(lambda en: en.tensor_mul(out=AR, in0=mag, in1=ang[:, 1]), ['mag', 'ang'], ['AR'])
        K1 = 9
        am1 = scratch([128, 16], F32)
        t_a = scratch([128, 16], F32)
        t_b = scratch([128, 16], F32)
        dn = scratch([128, 16], F32)
        cr = scratch([128, 16], F32)
        ci = scratch([128, 16], F32)
        dv(lambda en: en.tensor_scalar(out=am1, in0=AR[:, K1, :], scalar1=-1.0, scalar2=None, op0=ALU.add),
           ['AR'], ['am1'])
        dv(lambda en: en.tensor_mul(out=dn, in0=lre, in1=lre), ['lre'], ['dn'])
        dv(lambda en: en.tensor_mul(out=t_a, in0=Aim, in1=Aim), ['Aim'], ['t_a'])
        dv(lambda en: en.tensor_add(out=dn, in0=dn, in1=t_a), ['dn', 't_a'], ['dn'])
        dv(lambda en: en.reciprocal(out=dn, in_=dn), ['dn'], ['dn'])
        dv(lambda en: en.tensor_mul(out=t_a, in0=am1, in1=lre), ['am1', 'lre'], ['t_a'])
        dv(lambda en: en.tensor_mul(out=t_b, in0=AI[:, K1, :], in1=Aim), ['AI', 'Aim'], ['t_b'])
        dv(lambda en: en.tensor_add(out=cr, in0=t_a, in1=t_b), ['t_a', 't_b'], ['cr'])
        dv(lambda en: en.tensor_mul(out=cr, in0=cr, in1=dn), ['cr', 'dn'], ['cr'])
        dv(lambda en: en.tensor_mul(out=t_a, in0=AI[:, K1, :], in1=lre), ['AI', 'lre'], ['t_a'])
        dv(lambda en: en.tensor_mul(out=t_b, in0=am1, in1=Aim), ['am1', 'Aim'], ['t_b'])
        dv(lambda en: en.tensor_sub(out=ci, in0=t_a, in1=t_b), ['t_a', 't_b'], ['ci'])
        dv(lambda en: en.tensor_mul(out=ci, in0=ci, in1=dn), ['ci', 'dn'], ['ci'])
        bbr = scratch([128, 16, 16], F32)
        bbi = scratch([128, 16, 16], F32)
        tq = scratch([128, 16, 16], F32)
        crb = cr.unsqueeze(2).broadcast_to([128, 16, 16])
        cib = ci.unsqueeze(2).broadcast_to([128, 16, 16])
        dv(lambda en: en.tensor_tensor(out=bbr, in0=Bre, in1=crb, op=ALU.mult), ['Bre', 'cr'], ['bbr'])
        dv(lambda en: en.tensor_tensor(out=tq, in0=Bim, in1=cib, op=ALU.mult), ['Bim', 'ci'], ['tq'])
        dv(lambda en: en.tensor_sub(out=bbr, in0=bbr, in1=tq), ['bbr', 'tq'], ['bbr'])
        dv(lambda en: en.tensor_tensor(out=bbi, in0=Bim, in1=crb, op=ALU.mult), ['Bim', 'cr'], ['bbi'])
        dv(lambda en: en.tensor_tensor(out=tq, in0=Bre, in1=cib, op=ALU.mult), ['Bre', 'ci'], ['tq'])
        dv(lambda en: en.tensor_add(out=bbi, in0=bbi, in1=tq), ['bbi', 'tq'], ['bbi'])

        def cprod(outr, outi, k0, Mr, Mi, nMr, nMi, nr, ni, neg_im=False, eng='dve'):
            pr = AR[:, k0:k0 + 8, :].transpose([0, 2, 1]).unsqueeze(3).broadcast_to([128, 16, 8, 16])
            pi = AI[:, k0:k0 + 8, :].transpose([0, 2, 1]).unsqueeze(3).broadcast_to([128, 16, 8, 16])
            mr = Mr.unsqueeze(2).broadcast_to([128, 16, 8, 16])
            mi = Mi.unsqueeze(2).broadcast_to([128, 16, 8, 16])
            tmp = cp_tmp
            op(eng, lambda en: en.tensor_tensor(out=outr, in0=pr, in1=mr, op=ALU.mult), ['AR', nMr], [nr])
            op(eng, lambda en: en.tensor_tensor(out=tmp, in0=pi, in1=mi, op=ALU.mult), ['AI', nMi], ['cpt'])
            op(eng, lambda en: en.tensor_sub(out=outr, in0=outr, in1=tmp), [nr, 'cpt'], [nr])
            op(eng, lambda en: en.tensor_tensor(out=outi, in0=pr, in1=mi, op=ALU.mult), ['AR', nMi], [ni])
            op(eng, lambda en: en.tensor_tensor(out=tmp, in0=pi, in1=mr, op=ALU.mult), ['AI', nMr], ['cpt'])
            if neg_im:
                op(eng, lambda en: en.scalar_tensor_tensor(out=outi, in0=outi, scalar=-1.0, in1=tmp, op0=ALU.mult,
                                                           op1=ALU.subtract), [ni, 'cpt'], [ni])
            else:
                op(eng, lambda en: en.tensor_add(out=outi, in0=outi, in1=tmp), [ni, 'cpt'], [ni])

        cp_tmp = scratch([128, 16, 8, 16], F32)
        Lr = scratch([128, 16, 8, 16], F32)
        Li = scratch([128, 16, 8, 16], F32)
        Rr = scratch([128, 16, 8, 16], F32)
        Ri = scratch([128, 16, 8, 16], F32)
        cprod(Lr, Li, 0, bbr, bbi, 'bbr', 'bbi', 'Lr', 'Li')
        cprod(Rr, Ri, 8, CCre, CCim, 'CC0', 'CC1', 'Rr', 'Ri', neg_im=True)
        maskT = scratch([128, 128], F32)
        op('pool', lambda en: en.memset(maskT, 1.0), [], ['maskT'])
        op('pool', lambda en: en.affine_select(out=maskT.rearrange("p (t c) -> p t c", c=16),
                                               in_=maskT.rearrange("p (t c) -> p t c", c=16),
                                               compare_op=ALU.is_ge, fill=0.0, base=15,
                                               pattern=[[16, 8], [0, 16]], channel_multiplier=-1),
           ['maskT'], ['maskT'])
        ttmp = scratch([128, 2, 128], F32)
        for g in range(32):
            gh, g16 = g // 16, g % 16
            sl = slice(64 * gh, 64 * gh + 64)
            b = bank()
            op('pe', lambda en, b=b, sl=sl, g16=g16: en.matmul(
                psf[b][:, 0:128], lhsT=Lr[sl, g16].rearrange("p s c -> p (s c)"),
                rhs=Rr[sl, g16].rearrange("p s c -> p (s c)"), start=True, stop=False),
               ['Lr', 'Rr'], ['ps%d' % b])
            op('pe', lambda en, b=b, sl=sl, g16=g16: en.matmul(
                psf[b][:, 0:128], lhsT=Li[sl, g16].rearrange("p s c -> p (s c)"),
                rhs=Ri[sl, g16].rearrange("p s c -> p (s c)"), start=False, stop=True),
               ['Li', 'Ri'], ['ps%d' % b])
            j = g % 2
            op('dve', lambda en, b=b, j=j: en.tensor_tensor(out=ttmp[:, j, :], in0=psf[b][:, 0:128], in1=maskT,
                                                            op=ALU.mult), ['ps%d' % b, 'maskT'], ['ttmp%d' % j])
            op('dve', lambda en, g=g, j=j: en.scalar_tensor_tensor(out=Tm[:, g, :], in0=identf,
                                                                   scalar=Dcol[:, g:g + 1], in1=ttmp[:, j, :],
                                                                   op0=ALU.mult, op1=ALU.add),
               ['ttmp%d' % j, 'identf', 'Dcol'], ['Tm'])
        cprod(Rr, Ri, 9, CCre, CCim, 'CC0', 'CC1', 'Rr', 'Ri', neg_im=True)
        op('dve', lambda en: en.tensor_copy(out=Qm[:, :, 0, :], in_=Rr.rearrange("p g t c -> p g (t c)")),
           ['Rr'], ['Qm'])
        op('dve', lambda en: en.tensor_copy(out=Qm[:, :, 1, :], in_=Ri.rearrange("p g t c -> p g (t c)")),
           ['Ri'], ['Qm'])
        cprod(Lr, Li, 17, bbr, bbi, 'bbr', 'bbi', 'Lr', 'Li')
        for g in range(32):
            gh, g16 = g // 16, g % 16
            sl = slice(64 * gh, 64 * gh + 64)
            b = bank()
            for ri, LL in enumerate([Lr, Li]):
                op('pe', lambda en, b=b, sl=sl, g16=g16, ri=ri, LL=LL: en.matmul(
                    psf[b][:, ri * 64:ri * 64 + 64], lhsT=LL[sl, g16].rearrange("p s c -> p (s c)"),
                    rhs=identf[sl, sl], start=True, stop=True), ['Lr', 'Li', 'identf'], ['ps%d' % b])
            copy_any(Pm[:, g, :], psf[b][:, 0:128], ['ps%d' % b], ['Pm'])
        K8 = 16
        for blk in range(2):
            op('dve', lambda en, blk=blk: en.tensor_copy(
                out=C1[:, blk], in_=AR[:, K8, :].unsqueeze(2).broadcast_to([128, 16, 4])), ['AR'], ['C1'])
        op('dve', lambda en: en.tensor_scalar(out=C2[:, 0], in0=AI[:, K8, :].unsqueeze(2).broadcast_to([128, 16, 4]),
                                              scalar1=-1.0, scalar2=None, op0=ALU.mult), ['AI'], ['C2'])
        op('dve', lambda en: en.tensor_copy(out=C2[:, 1], in_=AI[:, K8, :].unsqueeze(2).broadcast_to([128, 16, 4])),
           ['AI'], ['C2'])

        i_prep1 = len(P.ops)
        convs = P.ops[i_conv0:i_conv1]
        preps = P.ops[i_conv1:i_prep1]
        P.ops[i_conv0:i_prep1] = merge_ops(preps, convs)
        P.barrier()
        op('pool', lambda en: en.memset(Pprev, 0.0), [], ['Pprev0', 'Pprev1'])
        op('pool', lambda en: en.memset(Pcur, 0.0), [], ['Pcur0', 'Pcur1'])
        op('pool', lambda en: en.memset(Pmeta, 0.0), [], ['Pmeta0', 'Pmeta1'])
        op('pool', lambda en: en.memset(Vt, 1.0), [], ['Vt'])
        P.barrier()
        bank_ctr['front'] = True

        def ssm_recurrence(NS, NCH, z0_pp):
            F = 16 * NS
            Vv = Vs[:, :, :, 0:NS * NCH].rearrange("p r g (s k) -> p r g s k", k=NCH)
            Hv = Hb[:, :, :, 0:NS * NCH].rearrange("p r g (s k) -> p r g s k", k=NCH)
            c1 = C1[:, :, :, 0:NS]
            c2 = C2[:, :, :, 0:NS]
            pp = z0_pp
            for k in range(NCH):
                zc = Z[:, pp, :, 0:F].rearrange("p b (g s) -> p b g s", s=NS)
                zn = Z[:, 1 - pp, :, 0:F].rearrange("p b (g s) -> p b g s", s=NS)
                t1 = T1[:, :, 0:F].rearrange("p b (g s) -> p b g s", s=NS)
                t2 = T2[:, :, 0:F].rearrange("p b (g s) -> p b g s", s=NS)
                op('pool', lambda en, zc=zc, k=k: en.tensor_copy(out=Hv[:, :, :, :, k], in_=zc[:, 0:2]),
                   ['Z%d' % pp], ['Hb'])
                op('pool', lambda en, zc=zc, t1=t1: en.tensor_tensor(out=t1, in0=zc[:, 0:2], in1=c1, op=ALU.mult),
                   ['Z%d' % pp, 'C1'], ['T1'])
                op('pool', lambda en, zc=zc, t2=t2: en.tensor_tensor(out=t2, in0=zc[:, 1:3], in1=c2, op=ALU.mult),
                   ['Z%d' % pp, 'C2'], ['T2'])
                op('pool', lambda en, t1=t1, t2=t2: en.tensor_add(out=t1, in0=t1, in1=t2), ['T1', 'T2'], ['T1'])
                op('pool', lambda en, zn=zn, t1=t1, k=k: en.tensor_add(out=zn[:, 0:2], in0=t1, in1=Vv[:, :, :, :, k]),
                   ['T1', 'Vs'], ['Z%d' % (1 - pp)])
                op('pool', lambda en, zn=zn: en.tensor_copy(out=zn[:, 2], in_=zn[:, 0]),
                   ['Z%d' % (1 - pp)], ['Z%d' % (1 - pp)])
                pp = 1 - pp
            return pp

        def norm_transpose(NS, TT, dst, dname):
            for m in range(NS):
                xb = m % 2
                op('act', lambda en, m=m, xb=xb: en.activation(out=xnb[0:TT, xb], in_=xt[0:TT, m, :], func=AF.Square,
                                                               accum_out=ss[0:TT, m:m + 1]),
                   ['xt%d' % m], ['xnb%d' % xb, 'ss%d' % m])
            ssn = ['ss%d' % m for m in range(NS)]
            rsn = ['rs%d' % m for m in range(NS)]
            op('act', lambda en: en.activation(out=rs[0:TT, 0:NS], in_=ss[0:TT, 0:NS], func=AF.Ln, scale=1.0 / D,
                                               bias=epsb[0:TT, 0:1]), ssn, rsn)
            op('act', lambda en: en.activation(out=rs[0:TT, 0:NS], in_=rs[0:TT, 0:NS], func=AF.Exp, scale=-0.5),
               rsn, rsn)
            for m in range(NS):
                xb = m % 2
                op('dve', lambda en, m=m, xb=xb: en.tensor_scalar(out=xnb[0:TT, xb], in0=xt[0:TT, m, :],
                                                                  scalar1=rs[0:TT, m:m + 1], scalar2=None,
                                                                  op0=ALU.mult),
                   ['xt%d' % m, 'rs%d' % m], ['xnb%d' % xb])
                b = bank()
                for kc in range(8):
                    op('pe', lambda en, b=b, kc=kc, xb=xb: en.transpose(
                        out=psb[b][:, kc * 128:kc * 128 + TT], in_=xnb[0:TT, xb, kc * 128:(kc + 1) * 128],
                        identity=identb[0:TT, 0:TT]), ['xnb%d' % xb, 'identb'], ['ps%d' % b])
                copy_any(dst[:, :, m * TT:(m + 1) * TT],
                         psb[b].rearrange("p (k t) -> p k t", t=128)[:, :, 0:TT], ['ps%d' % b], ['%s%d' % (dname, m)])

        XMALL_ = ['XM0', 'XM1', 'XM2', 'XM3']
        US_ = ['US'] if overlap else []
        UZ_ = ['UZ'] if overlap else []
        YA_ = ['yA0', 'yA1', 'yA2', 'yA3']
        HNALL_ = ['XM0', 'XM1', 'XM2', 'XM3']
        zstate = {'pp': 0}

        def block(kind, j, phase):
            if kind == 'P':
                NS, TT = 4, 128
            elif kind == 'S':
                NS, TT = 4, 64
            else:
                NS, TT = 1, 16
            NCH = TT // 8
            NQ = NS * NCH
            NT = NS * TT
            slot = j % 2 if kind == 'P' else 0
            pslot = 1 - slot
            full = kind != 'M'

            def ydst(m):
                if kind == 'P':
                    return yp[m, j * 128:(j + 1) * 128, :]
                return ys[m, :, :]

            def xsrc(m):
                if kind == 'P':
                    return xp[m, j * 128:(j + 1) * 128, :]
                if kind == 'S':
                    return xs[m, :, :]
                return meta

            def f1():
                for m in range(NS):
                    dma(lambda en, m=m: en.dma_start(out=xt[0:TT, m, :], in_=xsrc(m)), [], ['xt%d' % m])
                norm_transpose(NS, TT, XM, 'XM')

                if stop == 5.01 and kind == 'P':
                    return True
                for s_ in range(8):
                    b = bank()
                    for kc in range(8):
                        op('pe', lambda en, b=b, kc=kc, s_=s_: en.matmul(
                            psf[b][0:NQ, :], lhsT=XM[:, kc, 0:NT].rearrange("p (q s) -> p q s", s=8)[:, :, s_],
                            rhs=win[:, kc, 768:1280], start=(kc == 0), stop=(kc == 7)), XMALL_ + ['win'], ['ps%d' % b])
                    copy_any(u_ks[0:NQ, :, s_, :], psf[b][0:NQ, :].rearrange("p (g c) -> p g c", c=16),
                             ['ps%d' % b], ['u_ks', *US_])
                if stop == 5.05 and kind == 'P':
                    return True
                def qkv_mm(m):
                        bq = bank()
                        bk = bank()
                        for kc in range(8):
                            op('pe', lambda en, kc=kc, m=m, bq=bq: en.matmul(
                                psf[bq][0:TT, :], lhsT=XM[:, kc, m * TT:(m + 1) * TT], rhs=win[:, kc, 0:512],
                                start=(kc == 0), stop=(kc == 7)), ['XM%d' % m, 'win'], ['ps%d' % bq])
                        for kc in range(8):
                            op('pe', lambda en, kc=kc, m=m, bk=bk: en.matmul(
                                psf[bk][0:TT, 0:256], lhsT=XM[:, kc, m * TT:(m + 1) * TT], rhs=win[:, kc, 512:768],
                                start=(kc == 0), stop=(kc == 7)), ['XM%d' % m, 'win'], ['ps%d' % bk])
                        return bq, bk

                def qk_chain(m, bq, bk):
                        mp = m % 2
                        op('act', lambda en, bq=bq: en.activation(out=qsq[0:TT, 0:512], in_=psf[bq][0:TT, :], func=AF.Square),
                           ['ps%d' % bq], ['qsq'])
                        op('act', lambda en, bk=bk: en.activation(out=qsq[0:TT, 512:640], in_=psf[bk][0:TT, 0:128],
                                                                  func=AF.Square), ['ps%d' % bk], ['qsq'])
                        op('dve', lambda en, mp=mp: en.tensor_reduce(
                            out=st10[0:TT, mp, :], in_=qsq[0:TT, :].rearrange("p (h d) -> p h d", d=64), axis=AX.X,
                            op=ALU.add), ['qsq'], ['st10%d' % mp])
                        rstd_from(st10[0:TT, mp, :], r10[0:TT, mp, :], 64, 'st10%d' % mp, 'r10%d' % mp)
                        op('dve', lambda en, mp=mp, bq=bq: en.tensor_tensor(
                            out=qnb[0:TT, mp, :].rearrange("p (h d) -> p h d", d=64),
                            in0=psf[bq][0:TT, :].rearrange("p (h d) -> p h d", d=64),
                            in1=r10[0:TT, mp, 0:8].unsqueeze(2).broadcast_to([TT, 8, 64]), op=ALU.mult),
                           ['ps%d' % bq, 'r10%d' % mp], ['qnb%d' % mp])
                        op('dve', lambda en, mp=mp, bk=bk: en.tensor_tensor(
                            out=kf[0:TT, mp, :].rearrange("p (h d) -> p h d", d=64),
                            in0=psf[bk][0:TT, 0:128].rearrange("p (h d) -> p h d", d=64),
                            in1=r10[0:TT, mp, 8:10].unsqueeze(2).broadcast_to([TT, 2, 64]), op=ALU.mult),
                           ['ps%d' % bk, 'r10%d' % mp], ['kf%d' % mp])
                        op('dve', lambda en, mp=mp: en.tensor_tensor(out=kf[0:TT, mp, :], in0=kf[0:TT, mp, :],
                                                                     in1=gk_t[0:TT].rearrange("p a d -> p (a d)"),
                                                                     op=ALU.mult), ['kf%d' % mp, 'gk_t'], ['kf%d' % mp])
                        op('act', lambda en, mp=mp: en.copy(out=kb[0:TT, mp, :], in_=kf[0:TT, mp, :]),
                           ['kf%d' % mp], ['kb%d' % mp])
                        vdst = Vm[0:TT, :, 0:64] if kind == 'M' else Vt[0:TT, slot, m, :, 0:64]
                        vname = 'Vm' if kind == 'M' else 'Vt%d_%d' % (slot, m)
                        op('act', lambda en, bk=bk, vdst=vdst: en.copy(
                            out=vdst, in_=psf[bk][0:TT, 128:256].rearrange("p (h d) -> p h d", d=64)),
                           ['ps%d' % bk], [vname])
                        need_out = (kind != 'P') or (j == NBLK - 1)
                        if need_out:
                            op('dve', lambda en, bk=bk: en.tensor_copy(out=vf[0:TT, :], in_=psf[bk][0:TT, 128:256]),
                               ['ps%d' % bk], ['vf'])
                            if kind == 'P':
                                dma(lambda en, m=m, mp=mp: en.dma_start(out=kpo[m, 16:144, :], in_=kf[0:TT, mp, :]),
                                    ['kf%d' % mp], [])
                                dma(lambda en, m=m: en.dma_start(out=vpo[m, 16:144, :], in_=vf[0:TT, :]), ['vf'], [])
                            elif kind == 'S':
                                dma(lambda en, m=m, mp=mp: en.dma_start(out=kso[m, :, :], in_=kf[0:TT, mp, :]),
                                    ['kf%d' % mp], [])
                                dma(lambda en, m=m: en.dma_start(out=vso[m, :, :], in_=vf[0:TT, :]), ['vf'], [])
                            else:
                                for mm in range(NSEQ):
                                    dma(lambda en, mm=mm, mp=mp: en.dma_start(out=kpo[mm, 0:16, :], in_=kf[0:TT, mp, :]),
                                        ['kf%d' % mp], [])
                                    dma(lambda en, mm=mm: en.dma_start(out=vpo[mm, 0:16, :], in_=vf[0:TT, :]), ['vf'], [])
                        b = bank()
                        for hk in range(2):
                            op('pe', lambda en, b=b, hk=hk, mp=mp: en.transpose(
                                out=psb[b][0:64, hk * 128:hk * 128 + TT], in_=kb[0:TT, mp, hk * 64:(hk + 1) * 64],
                                identity=identb[0:TT, 0:TT]), ['kb%d' % mp, 'identb'], ['ps%d' % b])
                        if kind == 'M':
                            ktd = KTm[:, :, 0:TT]
                            ktn = 'KTm'
                        else:
                            ktd = KT[:, :, slot, m, 0:TT]
                            ktn = 'KT%d_%d' % (slot, m)
                        op('act', lambda en, b=b, ktd=ktd: en.mul(
                            out=ktd, in_=psb[b][0:64, 0:256].rearrange("p (h t) -> p h t", t=128)[:, :, 0:TT],
                            mul=gq8[:, 0:1]), ['ps%d' % b, 'gq8'], [ktn])
                        if full:
                            b = bank()
                            for h in range(8):
                                op('pe', lambda en, b=b, h=h, mp=mp: en.transpose(
                                    out=psb[b][0:64, h * 128:h * 128 + TT], in_=qnb[0:TT, mp, h * 64:(h + 1) * 64],
                                    identity=identb[0:TT, 0:TT]), ['qnb%d' % mp, 'identb'], ['ps%d' % b])
                            op('dve', lambda en, b=b, m=m: en.tensor_copy(
                                out=QT[:, m, :, 0:TT], in_=psb[b][0:64, :].rearrange("p (h t) -> p h t", t=128)[:, :, 0:TT]),
                               ['ps%d' % b], ['QT%d' % m])


                def stage_qk():
                    if overlap:
                        for m in range(NS):
                            cur = qkv_mm(m)
                            qk_chain(m, *cur)
                    else:
                        prev = None
                        for m in range(NS):
                            cur = qkv_mm(m)
                            if prev is not None:
                                qk_chain(m - 1, *prev)
                            prev = cur
                        qk_chain(NS - 1, *prev)

                if stop == 5.1 and kind == 'P':
                    return True
                for g0 in range(0, 32, 8):
                    b = bank()
                    for g in range(g0, g0 + 8):
                        op('pe', lambda en, b=b, g=g: en.transpose(
                            out=psb[b][:, (g % 8) * 64:(g % 8) * 64 + NQ],
                            in_=u_ks[0:NQ, g].rearrange("p s c -> p (s c)"), identity=identb[0:NQ, 0:NQ]),
                           ['u_ks', *US_, 'identb'], ['ps%d' % b])
                    copy_any(U_all[:, g0:g0 + 8, 0:NQ],
                             psb[b][:, 0:512].rearrange("p (g q) -> p g q", q=64)[:, :, 0:NQ], ['ps%d' % b], ['U_all', *UZ_])
                for gb in range(4):
                    b = bank()
                    for gh in range(2):
                        for gl in range(4):
                            g16 = gb * 4 + gl
                            g = gh * 16 + g16
                            for ri in range(2):
                                op('pe', lambda en, b=b, g=g, gh=gh, gl=gl, ri=ri: en.matmul(
                                    psf[b][64 * gh:64 * gh + 64, (ri * 4 + gl) * 64:(ri * 4 + gl) * 64 + NQ],
                                    lhsT=Pm[:, g, ri * 64:(ri + 1) * 64], rhs=U_all[:, g, 0:NQ], start=True, stop=True),
                                   ['Pm', 'U_all', *UZ_], ['ps%d' % b])
                    copy_any(Vs[:, :, gb * 4:gb * 4 + 4, 0:NQ],
                             psf[b].rearrange("p (r g q) -> p r g q", r=2, q=64)[:, :, :, 0:NQ], ['ps%d' % b], ['Vs'])
                if stop == 5.11 and kind == 'P':
                    return True
                F = 16 * NS
                if kind == 'M':
                    op('pool', lambda en: en.memset(Z[:, 0], 0.0), [], ['Z0'])
                    zstate['pp'] = 0
                elif kind == 'P' and j == 0:
                    zv = Z[:, 0, :, 0:F].rearrange("p b (g s) -> p b g s", s=NS)
                    op('pool', lambda en, zv=zv: en.tensor_copy(
                        out=zv, in_=Zmeta.unsqueeze(3).broadcast_to([128, 3, 16, NS])), ['Zmeta'], ['Z0'])
                    zstate['pp'] = 0
                elif kind == 'S':
                    b = bank()
                    for ri, src in enumerate([s_re, s_im]):
                        dma(lambda en, ri=ri, src=src: en.dma_start(out=attn[:, ri, 0:64], in_=src), [], ['attn%d' % ri])
                        for gh in range(2):
                            op('pe', lambda en, b=b, ri=ri, gh=gh: en.matmul(
                                psf[b][64 * gh:64 * gh + 64, ri * 128:ri * 128 + 128], lhsT=attn[:, ri, 0:64],
                                rhs=identf, start=True, stop=True), ['attn%d' % ri, 'identf'], ['ps%d' % b])
                    for gh in range(2):
                        sl = slice(64 * gh, 64 * gh + 64)
                        for blk, ri in enumerate([0, 1, 0]):
                            src = psf[b][sl, ri * 128:ri * 128 + 128].rearrange("p (s g) -> p g s", g=32)[:, 16 * gh:16 * gh + 16, :]
                            op('dve', lambda en, sl=sl, blk=blk, src=src: en.tensor_copy(
                                out=Z[sl, 0, blk, 0:64].rearrange("p (g s) -> p g s", s=4), in_=src),
                               ['ps%d' % b], ['Z0'])
                    zstate['pp'] = 0
                if stop == 5.12 and kind == 'P':
                    return True
                pp_end = ssm_recurrence(NS, NCH, zstate['pp'])
                zstate['pp'] = pp_end
                if kind == 'M':
                    op('pool', lambda en: en.tensor_copy(out=Zmeta, in_=Z[:, pp_end, :, 0:16]), ['Z%d' % pp_end], ['Zmeta'])
                if stop == 5.13 and kind == 'P':
                    return True
                if kind != 'M' and (kind == 'S' or j == NBLK - 1):
                    o_re, o_im = (srs, sis) if kind == 'S' else (srp, sip)
                    for ri, dst in enumerate([o_re, o_im]):
                        op('dve', lambda en, ri=ri: en.tensor_copy(
                            out=qsq[:, ri * 64:(ri + 1) * 64].rearrange("p (s g) -> p s g", g=16),
                            in_=Z[:, pp_end, ri, 0:64].rearrange("p (g s) -> p s g", s=4)),
                           ['Z%d' % pp_end], ['qsq'])
                        b = bank()
                        op('pe', lambda en, b=b, ri=ri: en.matmul(
                            psf[b][0:64, 0:128], lhsT=qsq[:, ri * 64:(ri + 1) * 64],
                            rhs=identf, start=True, stop=True), ['qsq', 'identf'], ['ps%d' % b])
                        op('dve', lambda en, b=b, ri=ri: en.tensor_copy(out=attn[0:64, ri, 0:128], in_=psf[b][0:64, 0:128]),
                           ['ps%d' % b], ['attn%d' % ri])
                        for s_ in range(4 if stop != 5.14 else 0):
                            for gh in range(2):
                                dma(lambda en, s_=s_, gh=gh, ri=ri, dst=dst: en.dma_start(
                                    out=dst[s_ * 32 + gh * 16:s_ * 32 + gh * 16 + 16, :],
                                    in_=attn[s_ * 16:s_ * 16 + 16, ri, gh * 64:gh * 64 + 64]), ['attn%d' % ri], [])

                stage_qk()
                if kind == 'M':
                    return
                if stop in (5.2, 5.14) and kind == "P":
                    return True
                if kind == 'S':
                    for m in range(NS):
                        dma(lambda en, m=m: en.dma_start(out=kf[:, 0, :], in_=ck[m, 16:144, :]), [], ['kf0'])
                        dma(lambda en, m=m: en.dma_start(out=kf[0:16, 1, :], in_=ck[m, 0:16, :]), [], ['kf1'])
                        dma(lambda en, m=m: en.dma_start(out=vf[:, :], in_=cv[m, 16:144, :]), [], ['vf'])
                        dma(lambda en, m=m: en.dma_start(out=qsq[0:16, 0:128], in_=cv[m, 0:16, :]), [], ['qsq'])
                        op('pool', lambda en: en.tensor_copy(out=kb[:, 0, :], in_=kf[:, 0, :]), ['kf0'], ['kb0'])
                        op('pool', lambda en: en.tensor_copy(out=kb[0:16, 1, :], in_=kf[0:16, 1, :]), ['kf1'], ['kb1'])
                        op('act', lambda en, m=m: en.copy(out=Vt[:, 1, m, :, 0:64],
                                                          in_=vf[:, :].rearrange("p (h d) -> p h d", d=64)),
                           ['vf'], ['Vt1_%d' % m])
                        op('act', lambda en, m=m: en.copy(out=VmS[:, m, :, 0:64], in_=qsq[0:16, 0:128].rearrange("p (h d) -> p h d", d=64)),
                           ['qsq'], ['VmS%d' % m])
                        b = bank()
                        for hk in range(2):
                            op('pe', lambda en, b=b, hk=hk: en.transpose(
                                out=psb[b][0:64, hk * 128:hk * 128 + 128], in_=kb[:, 0, hk * 64:(hk + 1) * 64],
                                identity=identb), ['kb0', 'identb'], ['ps%d' % b])
                            op('pe', lambda en, b=b, hk=hk: en.transpose(
                                out=psb[b][0:64, 256 + hk * 16:256 + hk * 16 + 16], in_=kb[0:16, 1, hk * 64:(hk + 1) * 64],
                                identity=identb[0:16, 0:16]), ['kb1', 'identb'], ['ps%d' % b])
                        op('act', lambda en, b=b, m=m: en.mul(
                            out=KT[:, :, 1, m, :], in_=psb[b][0:64, 0:256].rearrange("p (h t) -> p h t", t=128),
                            mul=gq8[:, 0:1]), ['ps%d' % b, 'gq8'], ['KT1_%d' % m])
                        op('act', lambda en, b=b, m=m: en.mul(
                            out=KTmS[:, m], in_=psb[b][0:64, 256:288].rearrange("p (h t) -> p h t", t=16),
                            mul=gq8[:, 0:1]), ['ps%d' % b, 'gq8'], ['KTmS%d' % m])
                has_prev = (kind == 'S') or (j > 0)
                NQC = 4 * TT

                def att_S(ui, m, hk):
                    c = dict(m=m, hk=hk, pset=ui % 2)
                    qrhs = QT[:, m, 4 * hk:4 * hk + 4, 0:TT]
                    bp = bank() if has_prev else None
                    bc = bank()
                    bm = bank()
                    c.update(bp=bp, bc=bc, bm=bm)
                    if has_prev:
                        op('pe', lambda en: en.matmul(
                            psf[bp][:, 0:NQC], lhsT=KT[:, hk, pslot, m, :], rhs=qrhs, start=True, stop=True),
                           ['KT%d_%d' % (pslot, m), 'QT%d' % m], ['ps%d' % bp])
                    op('pe', lambda en: en.matmul(
                        psf[bc][0:TT, 0:NQC], lhsT=KT[:, hk, slot, m, 0:TT], rhs=qrhs, start=True, stop=True),
                       ['KT%d_%d' % (slot, m), 'QT%d' % m], ['ps%d' % bc])
                    if kind == 'S':
                        ktm = KTmS[:, m, hk, :]
                        ktmn = 'KTmS%d' % m
                        c.update(vmv=VmS[:, m, hk, :], vmn='VmS%d' % m)
                    else:
                        ktm = KTm[:, hk, :]
                        ktmn = 'KTm'
                        c.update(vmv=Vm[:, hk, :], vmn='Vm')
                    op('pe', lambda en: en.matmul(
                        psf[bm][0:16, 0:NQC], lhsT=ktm, rhs=qrhs, start=True, stop=True),
                       [ktmn, 'QT%d' % m], ['ps%d' % bm])
                    return c

                def att_exp(c):
                    pset, bp, bc, bm = c['pset'], c['bp'], c['bc'], c['bm']
                    pv = Pprev[:, pset, :, 0:TT]
                    pc = Pcur[:, pset, :, 0:TT]
                    pm_ = Pmeta[:, pset, :, 0:TT]
                    c.update(pv=pv, pc=pc, pm_=pm_)
                    if has_prev:
                        sp = psf[bp][:, 0:NQC].rearrange("p (h t) -> p h t", t=TT)
                        if kind == 'P':
                            op('act', lambda en: en.activation(out=pv[64:128], in_=sp[64:128], func=AF.Exp),
                               ['ps%d' % bp], ['Pprev%d' % pset])
                            op('act', lambda en: en.activation(out=pv[0:64, :, 0:64], in_=sp[0:64, :, 0:64], func=AF.Exp),
                               ['ps%d' % bp], ['Pprev%d' % pset])
                        else:
                            op('act', lambda en: en.activation(out=pv, in_=sp, func=AF.Exp),
                               ['ps%d' % bp], ['Pprev%d' % pset])
                    sc_ = psf[bc][0:TT, 0:NQC].rearrange("p (h t) -> p h t", t=TT)
                    if kind == 'P':
                        op('act', lambda en: en.activation(out=pc[0:64], in_=sc_[0:64], func=AF.Exp),
                           ['ps%d' % bc], ['Pcur%d' % pset])
                        op('act', lambda en: en.activation(out=pc[64:128, :, 64:128], in_=sc_[64:128, :, 64:128],
                                                           func=AF.Exp), ['ps%d' % bc], ['Pcur%d' % pset])
                    else:
                        op('act', lambda en: en.activation(out=pc[0:TT], in_=sc_, func=AF.Exp),
                           ['ps%d' % bc], ['Pcur%d' % pset])
                    sm = psf[bm][0:16, 0:NQC].rearrange("p (h t) -> p h t", t=TT)
                    op('act', lambda en: en.activation(out=pm_, in_=sm, func=AF.Exp), ['ps%d' % bm], ['Pmeta%d' % pset])

                def att_PV(c):
                    m, hk, pset = c['m'], c['hk'], c['pset']
                    pv, pc, pm_, vmv, vmn = c['pv'], c['pc'], c['pm_'], c['vmv'], c['vmn']
                    bo = bank()
                    for h in range(4):
                        ov = psf[bo][0:TT, h * 65:h * 65 + 65]
                        first = True
                        if has_prev:
                            op('pe', lambda en, ov=ov, h=h: en.matmul(
                                ov, lhsT=pv[:, h, :], rhs=Vt[:, pslot, m, hk, :], start=True, stop=False),
                               ['Pprev%d' % pset, 'Vt%d_%d' % (pslot, m)], ['ps%d' % bo])
                            first = False
                        op('pe', lambda en, ov=ov, h=h, first=first: en.matmul(
                            ov, lhsT=pc[0:TT, h, :], rhs=Vt[0:TT, slot, m, hk, :], start=first, stop=False),
                           ['Pcur%d' % pset, 'Vt%d_%d' % (slot, m)], ['ps%d' % bo])
                        op('pe', lambda en, ov=ov, h=h: en.matmul(
                            ov, lhsT=pm_[:, h, :], rhs=vmv, start=False, stop=True),
                           ['Pmeta%d' % pset, vmn], ['ps%d' % bo])
                    o3 = psf[bo][0:TT, 0:260].rearrange("p (h e) -> p h e", e=65)
                    mp = m % 2
                    op('dve', lambda en: en.tensor_tensor(
                        out=den[0:TT, pset, :], in0=o3[:, :, 64], in1=esink[0:TT, 4 * hk:4 * hk + 4], op=ALU.add),
                       ['ps%d' % bo, 'esink'], ['den%d' % pset])
                    op('dve', lambda en: en.reciprocal(out=den[0:TT, pset, :], in_=den[0:TT, pset, :]),
                       ['den%d' % pset], ['den%d' % pset])
                    op('dve', lambda en: en.tensor_tensor(
                        out=attn[0:TT, mp, hk * 256:(hk + 1) * 256].rearrange("p (h d) -> p h d", d=64),
                        in0=o3[:, :, 0:64], in1=den[0:TT, pset, :].unsqueeze(2).broadcast_to([TT, 4, 64]),
                        op=ALU.mult), ['ps%d' % bo, 'den%d' % pset], ['attn%d' % mp])

                def att_norm(m):
                    mp = m % 2
                    op('act', lambda en: en.activation(out=anb[0:TT], in_=attn[0:TT, mp, :], func=AF.Square,
                                                       accum_out=ss[0:TT, 4 + m:5 + m]),
                       ['attn%d' % mp], ['anb', 'ssa%d' % m])
                    rstd_from(ss[0:TT, 4 + m:5 + m], rs[0:TT, 4 + m:5 + m], 512, 'ssa%d' % m, 'rsa%d' % m)
                    op('dve', lambda en: en.tensor_scalar(out=anb[0:TT], in0=attn[0:TT, mp, :],
                                                          scalar1=rs[0:TT, 4 + m:5 + m], scalar2=None, op0=ALU.mult),
                       ['attn%d' % mp, 'rsa%d' % m], ['anb'])
                    b = bank()
                    for kc in range(4):
                        op('pe', lambda en, kc=kc: en.transpose(out=psb[b][:, kc * 128:kc * 128 + TT],
                                                                in_=anb[0:TT, kc * 128:(kc + 1) * 128],
                                                                identity=identb[0:TT, 0:TT]),
                           ['anb', 'identb'], ['ps%d' % b])
                    copy_any(XM[:, 0:4, m * TT:(m + 1) * TT],
                             psb[b][:, 0:512].rearrange("p (k t) -> p k t", t=128)[:, :, 0:TT], ['ps%d' % b], ['XM%d' % m])

                units = [(m, hk) for m in range(NS) for hk in range(2)]
                ctxs = [None] * len(units)
                if overlap:
                    for ui in range(len(units)):
                        ctxs[ui] = att_S(ui, *units[ui])
                        att_exp(ctxs[ui])
                        att_PV(ctxs[ui])
                        if units[ui][1] == 1:
                            att_norm(units[ui][0])
                else:
                    for ui in range(len(units) + 1):
                        if ui < len(units):
                            ctxs[ui] = att_S(ui, *units[ui])
                            att_exp(ctxs[ui])
                        if ui >= 1:
                            att_PV(ctxs[ui - 1])
                            if units[ui - 1][1] == 1:
                                att_norm(units[ui - 1][0])


            def f2():
                if stop == 5.3 and kind == 'P':
                    return True
                op('act', lambda en: en.copy(out=dumA, in_=epsb), ['epsb'],
                   ['dumA'] + ['gT%d' % i_ for i_ in range(16)] + ['LOCK2'])
                for g0 in range(0, 32, 8):
                    b = bank()
                    for g in range(g0, g0 + 8):
                        gh, g16 = g // 16, g % 16
                        sl = slice(64 * gh, 64 * gh + 64)
                        for th in range(2):
                            ov = psf[b][64 * th:64 * th + NQ, (g % 8) * 64:(g % 8) * 64 + 64]
                            op('pe', lambda en, ov=ov, g=g, th=th: en.matmul(
                                ov, lhsT=U_all[:, g, 0:NQ], rhs=Tm[:, g, 64 * th:64 * th + 64], start=True, stop=False),
                               ['U_all', *UZ_, 'Tm'], ['ps%d' % b])
                            for ri in range(2):
                                op('pe', lambda en, ov=ov, sl=sl, g16=g16, th=th, ri=ri: en.matmul(
                                    ov, lhsT=Hb[sl, ri, g16, 0:NQ], rhs=Qm[sl, g16, ri, 64 * th:64 * th + 64],
                                    start=False, stop=(ri == 1)), ['Hb', 'Qm'], ['ps%d' % b])
                    for pr in ([slice(0, 128)] if NQ == 64 else [slice(64 * th_, 64 * th_ + NQ) for th_ in range(2)]):
                        op('act', lambda en, b=b, g0=g0, pr=pr: en.activation(
                            out=yA[pr].rearrange("p t (g c) -> p g t c", c=16)[:, g0:g0 + 8],
                            in_=psf[b][pr, :].rearrange("p (g t c) -> p g t c", t=4, c=16), func=AF.Gelu_apprx_tanh),
                           ['ps%d' % b, 'LOCK2'], YA_)
                    op('dve', lambda en, g0=g0: en.tensor_copy(out=zs_bf[:, :, g0 * 16:(g0 + 8) * 16],
                                                               in_=yA[:, :, g0 * 16:(g0 + 8) * 16]), YA_, ['zs_bf', *US_])
                YA = ['yA0', 'yA1', 'yA2', 'yA3']

                def d2_T(t4):
                    b = bank()
                    for kc in range(4):
                        op('pe', lambda en, kc=kc: en.transpose(
                            out=psb[b][:, kc * 128:(kc + 1) * 128], in_=zs_bf[:, t4, kc * 128:(kc + 1) * 128],
                            identity=identb), ['zs_bf', *US_, 'identb'], ['ps%d' % b])
                    op('dve', lambda en: en.tensor_copy(
                        out=zsT[:, :, t4, :], in_=psb[b][:, 0:512].rearrange("p (k q) -> p k q", q=128)),
                       ['ps%d' % b], ['zsT%d' % t4, *UZ_])

                def d2_G(t4):
                    b = bank()
                    for th in range(2):
                        ov = psf[b][64 * th:64 * th + NQ, :]
                        for kc in range(4):
                            op('pe', lambda en, ov=ov, kc=kc, th=th: en.matmul(
                                ov, lhsT=zsT[:, kc, t4, 64 * th:64 * th + NQ], rhs=wglu[:, kc, :], start=(kc == 0),
                                stop=False), ['zsT%d' % t4, *UZ_, 'wglu'], ['ps%d' % b])
                        op('pe', lambda en, ov=ov: en.matmul(ov, lhsT=onesr[0:1, 0:NQ], rhs=bglu[0:1, :],
                                                             start=False, stop=True),
                           ['onesr', 'bglu'], ['ps%d' % b])
                    for pr in ([slice(0, 128)] if NQ == 64 else [slice(64 * th_, 64 * th_ + NQ) for th_ in range(2)]):
                        op('act', lambda en, pr=pr: en.activation(out=yB[pr, t4, :], in_=psf[b][pr, :],
                                                                  func=AF.Sigmoid), ['ps%d' % b], ['yB%d' % t4])
                    op('dve', lambda en: en.tensor_mul(out=yB[:, t4, :], in0=yB[:, t4, :], in1=yA[:, t4, :]),
                       ['yA%d' % t4, 'yB%d' % t4], ['yB%d' % t4])
                    op('act', lambda en: en.activation(out=yA[:, t4, :], in_=yB[:, t4, :], func=AF.Square,
                                                       accum_out=ssS[:, t4:t4 + 1]),
                       ['yB%d' % t4], ['yA%d' % t4, 'ssS%d' % t4])

                def d2_N(t4):
                    op('dve', lambda en: en.tensor_scalar(out=sn_bf[:, t4, :], in0=yB[:, t4, :],
                                                          scalar1=rS[:, t4:t4 + 1], scalar2=None, op0=ALU.mult),
                       ['yB%d' % t4, 'rS'], ['sn_bf%d' % t4, *US_])

                def d2_S(t4):
                    b = bank()
                    for kc in range(4):
                        op('pe', lambda en, kc=kc: en.transpose(
                            out=psb[b][:, kc * 128:(kc + 1) * 128], in_=sn_bf[:, t4, kc * 128:(kc + 1) * 128],
                            identity=identb), ['sn_bf%d' % t4, *US_, 'identb'], ['ps%d' % b])
                    src = psb[b][:, 0:512].rearrange("p (k h q) -> p k h q", h=2, q=64)[:, :, :, 0:NQ]
                    dst = XM[:, 4:8, 0:NT].rearrange("p k (q h t) -> p k h q t", h=2, t=4)[:, :, :, :, t4]
                    copy_any(dst, src, ['ps%d' % b], XMALL_)

                for t4 in range(4):
                    d2_T(t4)
                    d2_G(t4)
                op('act', lambda en: en.activation(out=rS, in_=ssS, func=AF.Ln, scale=1.0 / 512, bias=epsb[:, 0:1]),
                   ['ssS0', 'ssS1', 'ssS2', 'ssS3'], ['rS'])
                op('act', lambda en: en.activation(out=rS, in_=rS, func=AF.Exp, scale=-0.5), ['rS'], ['rS'])
                for t4 in range(4):
                    d2_N(t4)
                for t4 in range(4):
                    d2_S(t4)

                if stop == 5.4 and kind == 'P':
                    return True
                def wout_mm(m):
                    b1 = bank()
                    b2 = bank()
                    for n, bb_ in enumerate([b1, b2]):
                        for kc in range(8):
                            op('pe', lambda en, bb_=bb_, kc=kc, n=n: en.matmul(
                                psf[bb_][0:TT, :], lhsT=XM[:, kc, m * TT:(m + 1) * TT],
                                rhs=wout[:, kc, n * 512:(n + 1) * 512],
                                start=(kc == 0), stop=(kc == 7)), ['XM%d' % m, 'wout'], ['ps%d' % bb_])
                    for n, bb_ in enumerate([b1, b2]):
                        op('dve', lambda en, bb_=bb_, n=n: en.tensor_tensor(
                            out=xt[0:TT, m, n * 512:(n + 1) * 512], in0=psf[bb_][0:TT, :],
                            in1=xt[0:TT, m, n * 512:(n + 1) * 512], op=ALU.add),
                           ['ps%d' % bb_, 'xt%d' % m], ['xt%d' % m])
                    if overlap:
                        dma(lambda en: en.dma_start(out=ydst(m), in_=xt[0:TT, m, :]), ['xt%d' % m],
                            ['yd%d_0' % m, 'yd%d_1' % m])

                def norm2(m):
                    xb = m % 2
                    op('act', lambda en: en.activation(out=xnb[0:TT, xb], in_=xt[0:TT, m, :], func=AF.Square,
                                                       accum_out=ss[0:TT, m:m + 1]),
                       ['xt%d' % m], ['xnb%d' % xb, 'ss%d' % m])
                    rstd_from(ss[0:TT, m:m + 1], rs[0:TT, m:m + 1], D, 'ss%d' % m, 'rs%d' % m)
                    op('dve', lambda en: en.tensor_scalar(out=xnb[0:TT, xb], in0=xt[0:TT, m, :],
                                                          scalar1=rs[0:TT, m:m + 1], scalar2=None, op0=ALU.mult),
                       ['xt%d' % m, 'rs%d' % m], ['xnb%d' % xb])

                def norm2_T(m):
                    xb = m % 2
                    b = bank()
                    for kc in range(8):
                        op('pe', lambda en, kc=kc: en.transpose(
                            out=psb[b][:, kc * 128:kc * 128 + TT], in_=xnb[0:TT, xb, kc * 128:(kc + 1) * 128],
                            identity=identb[0:TT, 0:TT]), ['xnb%d' % xb, 'identb'], ['ps%d' % b])
                    copy_any(hnT[:, :, m * TT:(m + 1) * TT],
                             psb[b].rearrange("p (k t) -> p k t", t=128)[:, :, 0:TT], ['ps%d' % b], ['XM%d' % m])

                for m in range(NS + 1):
                    if m < NS:
                        wout_mm(m)
                        norm2(m)
                    if m >= 1:
                        norm2_T(m - 1)
                if stop == 5.5 and kind == 'P':
                    return True

            def g():
                op('dve', lambda en: en.tensor_copy(out=dumD, in_=epsb), ['epsb'],
                   ['dumD'] + YA_ + ['yB0', 'yB1', 'yB2', 'yB3', 'LOCK1'])
                wu_i = mlp_ctr['wu']
                wd_i = mlp_ctr['wd']
                for hh in range(2):
                    for i in range(8):
                        su = wu_i % 3
                        wu_i += 1
                        h0 = (16 * hh + 2 * i) * 128
                        dma(lambda en, su=su, h0=h0: en.dma_start(out=wup_s[:, su], in_=wup_scr[:, :, h0:h0 + 256]),
                            ['wup_scr'], ['wup_s%d' % su])
                        for hl in range(2):
                            ht = 2 * i + hl
                            b = ht % 4
                            for kc in range(8):
                                op('pe', lambda en, b=b, su=su, hl=hl, kc=kc: en.matmul(
                                    psf[b][:, 0:NT], lhsT=wup_s[:, su, kc, hl * 128:(hl + 1) * 128],
                                    rhs=hnT[:, kc, 0:NT], start=(kc == 0), stop=(kc == 7)),
                                   ['wup_s%d' % su] + HNALL_, ['ps%d' % b])
                            rj = ht % 2
                            op('act', lambda en, b=b, rj=rj: en.activation(out=rt[:, rj, 0:NT], in_=psf[b][:, 0:NT],
                                                                           func=AF.Relu), ['ps%d' % b], ['rt%d' % rj])
                            op('dve', lambda en, ht=ht, rj=rj: en.tensor_mul(out=gT[:, ht, 0:NT], in0=rt[:, rj, 0:NT],
                                                                             in1=rt[:, rj, 0:NT]),
                               ['rt%d' % rj, 'LOCK1'], ['gT%d' % ht])
                    pieces = [(nh_, grp_) for nh_ in range(2) for grp_ in range(4)]

                    def wdn_load(pi, sd_):
                        nh_, grp_ = pieces[pi]
                        r0_ = 16 * hh + 4 * grp_
                        dma(lambda en: en.dma_start(
                            out=wdn_s[:, sd_], in_=wdn_scr[:, r0_:r0_ + 4, nh_ * 512:(nh_ + 1) * 512]),
                            ['wdn_scr'], ['wdn_s%d' % sd_])

                    wdn_load(0, wd_i % 3)
                    for nh in range(2):
                        cs = slice(nh * 512, (nh + 1) * 512)
                        for grp in range(4):
                            sd = wd_i % 3
                            wd_i += 1
                            pi = nh * 4 + grp
                            if pi + 1 < len(pieces):
                                wdn_load(pi + 1, wd_i % 3)
                            for m in range(NS):
                                for hl in range(4):
                                    ht = 4 * grp + hl
                                    op('pe', lambda en, m=m, sd=sd, hl=hl, ht=ht, grp=grp: en.matmul(
                                        psf[4 + m][0:TT, :], lhsT=gT[:, ht, m * TT:(m + 1) * TT],
                                        rhs=wdn_s[:, sd, hl, :],
                                        start=(grp == 0 and hl == 0), stop=(grp == 3 and hl == 3)),
                                       ['gT%d' % ht, 'wdn_s%d' % sd], ['ps%d' % (4 + m)])
                        for m in range(NS):
                            if hh == 0:
                                op('dve', lambda en, m=m, cs=cs: en.tensor_tensor(
                                    out=xt[0:TT, m, cs], in0=psf[4 + m][0:TT, :], in1=xt[0:TT, m, cs], op=ALU.add),
                                   ['ps%d' % (4 + m), 'xt%d' % m], ['xt%d' % m])
                            else:
                                yj = mlp_ctr['y'] % 2
                                mlp_ctr['y'] += 1
                                dst = ydst(m)[:, cs]
                                op('dve', lambda en, m=m, cs=cs, yj=yj: en.tensor_tensor(
                                    out=yt[0:TT, yj, :], in0=psf[4 + m][0:TT, :], in1=xt[0:TT, m, cs], op=ALU.add),
                                   ['ps%d' % (4 + m), 'xt%d' % m], ['yt%d' % yj])
                                dma(lambda en, dst=dst, yj=yj: en.dma_start(out=dst, in_=yt[0:TT, yj, :]),
                                    ['yt%d' % yj], [])
                mlp_ctr['wu'] = wu_i
                mlp_ctr['wd'] = wd_i

            if phase == 'F1':
                return f1()
            if phase == 'F2':
                return f2()
            return g()

        mlp_ctr = {'wu': 0, 'wd': 0, 'y': 0}
        op('pool', lambda en: en.memset(VmS, 1.0), [], ['VmS0', 'VmS1', 'VmS2', 'VmS3'])

        if stop == 3:
            P.emit(nc, st)
            return nc
        block('M', 0, 'F1')
        if stop == 4:
            P.emit(nc, st)
            return nc
        blocks = [('P', j_) for j_ in range(NBLK)] + ([('S', 0)] if do_sample else [])
        if block(blocks[0][0], blocks[0][1], 'F1') or block(blocks[0][0], blocks[0][1], 'F2'):
            P.emit(nc, st)
            return nc
        for bi, (bk_, bj_) in enumerate(blocks):
            i0 = len(P.ops)
            block(bk_, bj_, 'G')
            i1 = len(P.ops)
            if bi + 1 < len(blocks):
                nk_, nj_ = blocks[bi + 1]
                block(nk_, nj_, 'F1')
                i2 = len(P.ops)
                if overlap:
                    P.ops[i0:i2] = merge_ops(P.ops[i0:i1], P.ops[i1:i2])
                block(nk_, nj_, 'F2')
        P.emit(nc, st)
    return nc


_NC_CACHE = {}


def kernel(x_prompt, x_sample, cache_swa_k, cache_swa_v, state_ssm_re, state_ssm_im,
           meta_tokens, norm1_g, w_in, q_norm_g, k_norm_g, sinks,
           ssm_A_re, ssm_A_im, ssm_log_dt, ssm_B_re, ssm_B_im, ssm_C_re, ssm_C_im, ssm_D,
           w_glu, b_glu, attn_out_g, ssm_out_g, w_out, norm2_g, w_up, w_down):
    f = lambda a: np.ascontiguousarray(np.asarray(a, dtype=np.float32))
    if 'nc' not in _NC_CACHE:
        _NC_CACHE['nc'] = build()
    nc = _NC_CACHE['nc']
    shared = {
        "meta": f(meta_tokens), "norm1_g": f(norm1_g), "w_in": f(w_in[0]), "q_g": f(q_norm_g), "k_g": f(k_norm_g),
        "sinks": f(sinks), "A_re": f(ssm_A_re[0]), "A_im": f(ssm_A_im[0]), "log_dt": f(ssm_log_dt),
        "B_re": f(ssm_B_re[0]), "B_im": f(ssm_B_im[0]), "C_re": f(ssm_C_re[0]).reshape(512, 64),
        "C_im": f(ssm_C_im[0]).reshape(512, 64), "Dp": f(ssm_D[0]), "w_glu": f(w_glu[0]), "b_glu": f(b_glu),
        "ao_g": f(attn_out_g), "so_g": f(ssm_out_g), "w_out": f(w_out[0]), "norm2_g": f(norm2_g),
        "w_up": f(w_up[0]), "w_down": f(w_down[0]),
    }
    xpf, xsf = f(x_prompt), f(x_sample)
    ckf = f(cache_swa_k[0]).reshape(32, 144, 128)
    cvf = f(cache_swa_v[0]).reshape(32, 144, 128)
    srf = f(state_ssm_re[0]).reshape(32 * 32, 64)
    sif = f(state_ssm_im[0]).reshape(32 * 32, 64)
    in_maps = []
    for c in range(NCORE):
        sl = slice(c * NSEQ, (c + 1) * NSEQ)
        d = dict(shared)
        d.update({"xp": xpf[sl], "xs": xsf[sl], "ck": ckf[sl], "cv": cvf[sl],
                  "s_re": srf[c * 128:(c + 1) * 128], "s_im": sif[c * 128:(c + 1) * 128]})
        in_maps.append(d)
    res = run_bass_kernel_spmd(nc, in_maps, core_ids=list(range(NCORE)))
    R = res.results
    cat = lambda k: np.concatenate([np.asarray(R[c][k], dtype=np.float32) for c in range(NCORE)], axis=0)
    y_prompt = cat("yp")
    y_sample = cat("ys")
    kp = cat("kpo").reshape(1, 32, 144, 2, 64)
    vp = cat("vpo").reshape(1, 32, 144, 2, 64)
    srp_ = cat("srp").reshape(1, 32, 32, 64)
    sip_ = cat("sip").reshape(1, 32, 32, 64)
    ks_ = cat("kso").reshape(1, 32, 64, 2, 64)
    vs_ = cat("vso").reshape(1, 32, 64, 2, 64)
    srs_ = cat("srs").reshape(1, 32, 32, 64)
    sis_ = cat("sis").reshape(1, 32, 32, 64)
    return (y_prompt, y_sample, kp, vp, srp_, sip_, ks_, vs_, srs_, sis_)
```

```python
import numpy as np
from contextlib import ExitStack
import concourse.bass as bass
import concourse.mybir as mybir
from concourse.bass_utils import run_bass_kernel_spmd

F32 = mybir.dt.float32
BF16 = mybir.dt.bfloat16
I32 = mybir.dt.int32
AF = mybir.ActivationFunctionType
ALU = mybir.AluOpType
AX = mybir.AxisListType

NCORE = 8
D = 1024
SEQ = 2048
NSEQ = 4
NBLK = SEQ // 128
EPS = 1e-6
TWO_PI = float(2 * np.pi)


class Prog:
    NDMASEM = 16
    ENGS = ['pe', 'act', 'dve', 'pool', 'sp']

    def __init__(self):
        self.ops = []

    def op(self, eng, fn, reads=(), writes=(), dma=False, barrier=False):
        self.ops.append(dict(eng=eng, fn=fn, reads=tuple(reads), writes=tuple(writes), dma=dma, barrier=barrier))

    def barrier(self):
        for e in ['pe', 'act', 'dve', 'pool']:
            self.op(e, lambda en: en.nop(nofuse=True), barrier=True)
        self.op('sp', None, barrier=True)

    def emit(self, nc, stack):
        ops = self.ops
        engs = self.ENGS
        last_w = {}
        readers = {}
        last_op = {}
        recent_dma = {e: [] for e in engs}
        for i, o in enumerate(ops):
            if o['barrier']:
                deps = set(v for k, v in last_op.items())
                for e in engs:
                    deps |= set(recent_dma[e][-self.NDMASEM:])
                deps.discard(i)
                o['deps'] = set(d for d in deps if ops[d]['dma'] or ops[d]['eng'] != o['eng'])
                if o['eng'] != 'sp':
                    last_op[o['eng']] = i
                continue
            deps = set()
            for r in o['reads']:
                if r in last_w:
                    deps.add(last_w[r])
            for w in o['writes']:
                if w in last_w:
                    deps.add(last_w[w])
                lastrd = {}
                for rd in readers.get(w, ()):
                    if ops[rd]['dma']:
                        deps.add(rd)
                    else:
                        lastrd[ops[rd]['eng']] = rd
                for rd in lastrd.values():
                    deps.add(rd)
            deps.discard(i)
            nd = set()
            for d in deps:
                od = ops[d]
                if not od['dma'] and not o['dma'] and od['eng'] == o['eng']:
                    if o['eng'] == 'pe':
                        continue
                nd.add(d)
            o['deps'] = nd
            for r in o['reads']:
                readers.setdefault(r, []).append(i)
            for w in o['writes']:
                last_w[w] = i
                readers[w] = []
            if o['dma']:
                recent_dma[o['eng']].append(i)
            else:
                last_op[o['eng']] = i
        needed = set()
        for o in ops:
            needed |= o['deps']
        cnt = {e: 0 for e in engs}
        dcnt = {e: 0 for e in engs}
        for i, o in enumerate(ops):
            e = o['eng']
            if o['dma']:
                k = dcnt[e]
                dcnt[e] += 1
                o['sig'] = ('d', e, k % self.NDMASEM, 16 * (k // self.NDMASEM + 1))
            elif i in needed:
                cnt[e] += 1
                o['sig'] = ('c', e, 0, cnt[e])
            else:
                o['sig'] = None
        sems = {}
        for e in engs:
            sems[('c', e, 0)] = stack.enter_context(nc.semaphore('s_' + e))
        for e in engs:
            if dcnt[e]:
                for k in range(self.NDMASEM):
                    sems[('d', e, k)] = stack.enter_context(nc.semaphore('d_%s_%d' % (e, k)))
        block = stack.enter_context(nc.Block())
        per = {e: [o for o in ops if o['eng'] == e] for e in engs}

        def run(e, engine):
            waited = {}

            def wait(key, val):
                if waited.get(key, 0) >= val:
                    return
                waited[key] = val
                engine.wait_ge(sems[key], val)
            final = {}
            for o in per[e]:
                for d in sorted(o['deps']):
                    s = ops[d]['sig']
                    wait(s[:3], s[3])
                if o['dma']:
                    s = o['sig']
                    if s[3] > 16:
                        wait(s[:3], s[3] - 16)
                    o['fn'](engine).then_inc(sems[s[:3]], 16)
                    final[s[:3]] = s[3]
                elif o['fn'] is not None:
                    ins = o['fn'](engine)
                    if o['sig'] is not None:
                        ins.then_inc(sems[o['sig'][:3]], 1)
            for key, val in final.items():
                wait(key, val)

        @block.tensor
        def _(eng):
            run('pe', eng)

        @block.scalar
        def _(eng):
            run('act', eng)

        @block.vector
        def _(eng):
            run('dve', eng)

        @block.gpsimd
        def _(eng):
            run('pool', eng)

        @block.sync
        def _(eng):
            run('sp', eng)


def merge_ops(a, b):
    out = []
    ia = ib = 0
    na, nb = len(a), len(b)
    while ia < na or ib < nb:
        if ib >= nb or (ia < na and ia * nb <= ib * na):
            out.append(a[ia])
            ia += 1
        else:
            out.append(b[ib])
            ib += 1
    return out


def build(NBLK=NBLK, stop=99, do_sample=True, overlap=False):
    nc = bass.Bass("TRN2", target_bir_lowering=False)

    def din(name, shape):
        return nc.dram_tensor(name, list(shape), F32, kind="ExternalInput").ap()

    def dout(name, shape):
        return nc.dram_tensor(name, list(shape), F32, kind="ExternalOutput").ap()

    xp = din("xp", [NSEQ, SEQ, D])
    xs = din("xs", [NSEQ, 64, D])
    ck = din("ck", [NSEQ, 144, 128])
    cv = din("cv", [NSEQ, 144, 128])
    s_re = din("s_re", [NSEQ * 32, 64])
    s_im = din("s_im", [NSEQ * 32, 64])
    meta = din("meta", [16, D])
    norm1_g = din("norm1_g", [1, D])
    w_in = din("w_in", [D, 1280])
    q_g = din("q_g", [1, 64])
    k_g = din("k_g", [1, 64])
    sinks = din("sinks", [1, 8])
    A_re = din("A_re", [32, 64])
    A_im = din("A_im", [32, 64])
    log_dt = din("log_dt", [1, 32])
    B_re = din("B_re", [32, 64, 16])
    B_im = din("B_im", [32, 64, 16])
    C_re = din("C_re", [512, 64])
    C_im = din("C_im", [512, 64])
    Dp = din("Dp", [32, 16])
    w_glu = din("w_glu", [512, 512])
    b_glu = din("b_glu", [1, 512])
    ao_g = din("ao_g", [1, 512])
    so_g = din("so_g", [1, 512])
    w_out = din("w_out", [D, D])
    norm2_g = din("norm2_g", [1, D])
    w_up = din("w_up", [D, 4096])
    w_down = din("w_down", [4096, D])

    yp = dout("yp", [NSEQ, SEQ, D])
    ys = dout("ys", [NSEQ, 64, D])
    kpo = dout("kpo", [NSEQ, 144, 128])
    vpo = dout("vpo", [NSEQ, 144, 128])
    srp = dout("srp", [NSEQ * 32, 64])
    sip = dout("sip", [NSEQ * 32, 64])
    kso = dout("kso", [NSEQ, 64, 128])
    vso = dout("vso", [NSEQ, 64, 128])
    srs = dout("srs", [NSEQ * 32, 64])
    sis = dout("sis", [NSEQ * 32, 64])

    wup_scr = nc.dram_tensor("wup_scr", [128, 8, 4096], BF16, kind="Internal").ap()
    wdn_scr = nc.dram_tensor("wdn_scr", [128, 32, 1024], BF16, kind="Internal").ap()

    P = Prog()

    def op(eng, fn, r=(), w=()):
        P.op(eng, fn, r, w)

    def dma(fn, r=(), w=()):
        P.op('sp', fn, r, w, dma=True)

    with ExitStack() as st:
        NB = 206 * 1024
        raw = st.enter_context(nc.sbuf_tensor("raw", [128, NB // 2], BF16))
        rawb = raw[:]
        rawf = rawb.bitcast(F32)
        rawi = rawb.bitcast(I32)
        state = {'off': 0}

        offs = {}

        def salloc(shape, dt, at=None, name=None):
            n = 1
            for s_ in shape[1:]:
                n *= s_
            nb = n * (2 if dt == BF16 else 4)
            nb = (nb + 3) // 4 * 4
            if at is None:
                off = state['off']
                state['off'] += nb
                assert state['off'] <= NB, ("SBUF overflow", state['off'])
            else:
                off = at
            if dt == BF16:
                v = rawb[0:shape[0], off // 2: off // 2 + n]
            elif dt == F32:
                v = rawf[0:shape[0], off // 4: off // 4 + n]
            else:
                v = rawi[0:shape[0], off // 4: off // 4 + n]
            if len(shape) > 2:
                names = ['a%d' % i for i in range(len(shape) - 1)]
                kw = {names[i]: shape[1 + i] for i in range(len(shape) - 2)}
                v = v.rearrange("p (%s) -> p %s" % (' '.join(names), ' '.join(names)), **kw)
            if name is not None:
                offs[name] = off
            return v

        psf = []
        psb = []
        for b in range(8):
            t = st.enter_context(nc.psum_tensor("ps%d" % b, [128, 512], F32))
            psf.append(t[:])
            psb.append(t[:].bitcast(BF16))
        bank_ctr = {'i': 0}

        def bank():
            if bank_ctr.get('front') and overlap:
                b = 4 + bank_ctr['i'] % 4
            else:
                b = bank_ctr['i'] % 8
            bank_ctr['i'] += 1
            return b

        win = salloc([128, 8, 1280], BF16)
        wglu = salloc([128, 4, 512], BF16)
        wout = salloc([128, 8, 1024], BF16)
        Tm = salloc([128, 32, 128], BF16)
        Pm = salloc([128, 32, 128], BF16)
        Qm = salloc([128, 16, 2, 128], BF16)
        identf = salloc([128, 128], F32)
        identb = salloc([128, 128], BF16)
        onesr = salloc([1, 128], BF16)
        bglu = salloc([128, 512], BF16)
        C1 = salloc([128, 2, 16, 4], F32)
        C2 = salloc([128, 2, 16, 4], F32)
        gk_t = salloc([128, 2, 64], F32)
        esink = salloc([128, 8], F32)
        gq8 = salloc([64, 1], F32)
        epsb = salloc([128, 1], F32)
        dumA = salloc([128, 1], F32)
        dumD = salloc([128, 1], F32)
        KTm = salloc([64, 2, 16], BF16)
        Vm = salloc([16, 2, 65], BF16)
        kmf = salloc([16, 128], F32)
        vmf = salloc([16, 128], F32)
        Zmeta = salloc([128, 3, 16], F32)
        ss = salloc([128, 8], F32)
        rs = salloc([128, 8], F32)
        st10 = salloc([128, 2, 10], F32)
        r10 = salloc([128, 2, 10], F32)
        den = salloc([128, 2, 4], F32)
        ssS = salloc([128, 4], F32)
        rS = salloc([128, 4], F32)
        KTmS = salloc([64, 4, 2, 16], BF16)
        VmS = salloc([16, 4, 2, 65], BF16)
        xt = salloc([128, 4, 1024], F32, name='xt')
        xnb = salloc([128, 2, 1024], BF16)
        XM = salloc([128, 8, 512], BF16, name='XM')
        hnT = salloc([128, 8, 512], BF16, at=offs['XM'])
        qsq = salloc([128, 640], F32)
        qnb = salloc([128, 2, 512], BF16)
        kf = salloc([128, 2, 128], F32)
        kb = salloc([128, 2, 128], BF16)
        vf = salloc([128, 128], F32)
        QT = salloc([64, 4, 8, 128], BF16)
        KT = salloc([64, 2, 2, 4, 128], BF16)
        Vt = salloc([128, 2, 4, 2, 65], BF16)
        Pprev = salloc([128, 2, 4, 128], BF16)
        Pcur = salloc([128, 2, 4, 128], BF16)
        Pmeta = salloc([16, 2, 4, 128], BF16)
        attn = salloc([128, 2, 512], F32)
        anb = salloc([128, 512], BF16)
        U_all = salloc([128, 32, 64], BF16, name='U_all')
        Vs = salloc([128, 2, 16, 64], F32)
        Hb = salloc([128, 2, 16, 64], BF16)
        Z = salloc([128, 2, 3, 64], F32)
        T1 = salloc([128, 2, 64], F32)
        T2 = salloc([128, 2, 64], F32)
        zsT = salloc([128, 4, 4, 128], BF16, at=offs['U_all'])
        rt = salloc([128, 2, 512], F32)
        yt = salloc([128, 2, 512], F32)
        wup_s = salloc([128, 3, 8, 256], BF16)
        wdn_s = salloc([128, 3, 4, 512], BF16)
        yA = salloc([128, 4, 512], F32, name='yA')
        yB = salloc([128, 4, 512], F32)
        gT = salloc([128, 16, 512], BF16, at=offs['yA'])
        u_ks = salloc([64, 32, 8, 16], BF16, name='u_ks')
        zs_bf = salloc([128, 4, 512], BF16, at=offs['u_ks'])
        sn_bf = salloc([128, 4, 512], BF16, at=offs['u_ks'] + 4096)
        print("SBUF bytes/partition used:", state['off'])

        scr0 = offs['xt']
        scr = {'off': scr0}

        def scratch(shape, dt):
            n = 1
            for s_ in shape[1:]:
                n *= s_
            nb = (n * (2 if dt == BF16 else 4) + 3) // 4 * 4
            v = salloc(shape, dt, at=scr['off'])
            scr['off'] += nb
            assert scr['off'] <= state['off'], "scratch overflow"
            return v

        cp_i = {'i': 0}

        def copy_any(out, in_, r, w, engs=('act', 'dve')):
            e = engs[cp_i['i'] % len(engs)]
            cp_i['i'] += 1
            if e == 'act':
                op('act', lambda en: en.copy(out=out, in_=in_), r, w)
            else:
                op(e, lambda en: en.tensor_copy(out=out, in_=in_), r, w)

        def rstd_from(ssap, rsap, n, rname, wname):
            op('act', lambda en: en.activation(out=rsap, in_=ssap, func=AF.Ln, scale=1.0 / n,
                                               bias=epsb[0:ssap.shape[0], 0:1]), [rname], [wname])
            op('act', lambda en: en.activation(out=rsap, in_=rsap, func=AF.Exp, scale=-0.5), [wname], [wname])

        op('pool', lambda en: en.memset(identf, 0.0), [], ['identf'])
        op('pool', lambda en: en.affine_select(out=identf, in_=identf, compare_op=ALU.not_equal, fill=1.0,
                                               base=0, pattern=[[-1, 128]], channel_multiplier=1),
           ['identf'], ['identf'])
        op('dve', lambda en: en.tensor_copy(out=identb, in_=identf), ['identf'], ['identb'])
        op('pool', lambda en: en.memset(onesr, 1.0), [], ['onesr'])
        op('pool', lambda en: en.memset(epsb, EPS), [], ['epsb'])
        op('pool', lambda en: en.memset(Vm, 1.0), [], ['Vm'])
        dma(lambda en: en.dma_start(out=gk_t[:, 0, :], in_=k_g[0:1, :].broadcast_to([128, 64])), [], ['gk_t'])
        dma(lambda en: en.dma_start(out=gk_t[:, 1, :], in_=k_g[0:1, :].broadcast_to([128, 64])), [], ['gk_t'])
        dma(lambda en: en.dma_start(out=esink, in_=sinks[0:1, :].broadcast_to([128, 8])), [], ['esink'])
        op('act', lambda en: en.activation(out=esink, in_=esink, func=AF.Exp), ['esink'], ['esink'])
        dma(lambda en: en.dma_start(out=gq8, in_=q_g.rearrange("o d -> d o"), allow_slow_non_contiguous=True),
            [], ['gq8'])
        op('dve', lambda en: en.tensor_scalar(out=gq8, in0=gq8, scalar1=0.125, scalar2=None, op0=ALU.mult),
           ['gq8'], ['gq8'])

        if stop == 0.5:
            P.emit(nc, st)
            return nc
        g1 = scratch([128, 8], F32)
        mg = scratch([128, 8], F32)
        g2 = scratch([128, 8], F32)
        dma(lambda en: en.dma_start(out=g1, in_=norm1_g.rearrange("o (k p) -> p (o k)", p=128),
                                    allow_slow_non_contiguous=True), [], ['g1'])
        dma(lambda en: en.dma_start(out=mg[:, 0:4], in_=ao_g.rearrange("o (k p) -> p (o k)", p=128),
                                    allow_slow_non_contiguous=True), [], ['mg'])
        dma(lambda en: en.dma_start(out=mg[:, 4:8], in_=so_g.rearrange("o (k p) -> p (o k)", p=128),
                                    allow_slow_non_contiguous=True), [], ['mg'])
        dma(lambda en: en.dma_start(out=g2, in_=norm2_g.rearrange("o (k p) -> p (o k)", p=128),
                                    allow_slow_non_contiguous=True), [], ['g2'])
        stg = scratch([128, 6, 1280], F32)
        stb = scratch([128, 6, 1024], BF16)
        sc = {'i': 0}

        def cast_rows(dst, src_dram, ncol, gain, gname, wname):
            i = sc['i'] % 6
            sc['i'] += 1
            sname = 'stg%d' % i
            dma(lambda en: en.dma_start(out=stg[:, i, 0:ncol], in_=src_dram), [], [sname])
            e = ['act', 'dve'][sc['i'] % 2]
            rr = [sname] + ([gname] if gain is not None else [])
            if gain is None:
                if e == 'act':
                    op('act', lambda en: en.copy(out=dst, in_=stg[:, i, 0:ncol]), rr, [wname])
                else:
                    op(e, lambda en: en.tensor_copy(out=dst, in_=stg[:, i, 0:ncol]), rr, [wname])
            else:
                if e == 'act':
                    op('act', lambda en: en.mul(out=dst, in_=stg[:, i, 0:ncol], mul=gain), rr, [wname])
                else:
                    op(e, lambda en: en.tensor_scalar(out=dst, in0=stg[:, i, 0:ncol], scalar1=gain, scalar2=None,
                                                      op0=ALU.mult), rr, [wname])

        for kc in range(8):
            cast_rows(win[:, kc, :], w_in[kc * 128:(kc + 1) * 128, :], 1280, g1[:, kc:kc + 1], 'g1', 'win')
        if stop == 0.7:
            P.emit(nc, st)
            return nc
        for kc in range(4):
            cast_rows(wglu[:, kc, :], w_glu[kc * 128:(kc + 1) * 128, :], 512, None, None, 'wglu')
        if stop == 0.8:
            P.emit(nc, st)
            return nc
        for kc in range(8):
            cast_rows(wout[:, kc, :], w_out[kc * 128:(kc + 1) * 128, :], 1024, mg[:, kc:kc + 1], 'mg', 'wout')
        if stop == 0.9:
            P.emit(nc, st)
            return nc
        bgf = scratch([128, 512], F32)
        dma(lambda en: en.dma_start(out=bgf, in_=b_glu[0:1, :].broadcast_to([128, 512])), [], ['bgf'])
        if stop == 0.95:
            P.emit(nc, st)
            return nc
        op('dve', lambda en: en.tensor_copy(out=bglu, in_=bgf), ['bgf'], ['bglu'])
        if stop == 1:
            P.emit(nc, st)
            return nc
        i_conv0 = len(P.ops)
        tasks = []
        for kc in range(8):
            for c4 in range(4):
                tasks.append((w_up[kc * 128:(kc + 1) * 128, c4 * 1024:(c4 + 1) * 1024], g2[:, kc:kc + 1], 'g2',
                              wup_scr[:, kc, c4 * 1024:(c4 + 1) * 1024], 'wup_scr'))
        for ht in range(32):
            tasks.append((w_down[ht * 128:(ht + 1) * 128, :], None, None, wdn_scr[:, ht, :], 'wdn_scr'))
        KLOOK = 5

        def conv_load(t):
            i = t % 6
            src = tasks[t][0]
            dma(lambda en, i=i, src=src: en.dma_start(out=stg[:, i, 0:1024], in_=src), [], ['stg%d' % i])

        for t in range(KLOOK):
            conv_load(t)
        for t in range(len(tasks)):
            i = t % 6
            src, gain, gname, dstd, dname = tasks[t]
            e = ['act', 'dve'][t % 2]
            rr = ['stg%d' % i] + ([gname] if gain is not None else [])
            if gain is None:
                if e == 'act':
                    op('act', lambda en, i=i: en.copy(out=stb[:, i, :], in_=stg[:, i, 0:1024]), rr, ['stb%d' % i])
                else:
                    op('dve', lambda en, i=i: en.tensor_copy(out=stb[:, i, :], in_=stg[:, i, 0:1024]), rr,
                       ['stb%d' % i])
            else:
                if e == 'act':
                    op('act', lambda en, i=i, gain=gain: en.mul(out=stb[:, i, :], in_=stg[:, i, 0:1024], mul=gain),
                       rr, ['stb%d' % i])
                else:
                    op('dve', lambda en, i=i, gain=gain: en.tensor_scalar(
                        out=stb[:, i, :], in0=stg[:, i, 0:1024], scalar1=gain, scalar2=None, op0=ALU.mult),
                       rr, ['stb%d' % i])
            if t + KLOOK < len(tasks):
                conv_load(t + KLOOK)
            dma(lambda en, i=i, dstd=dstd: en.dma_start(out=dstd, in_=stb[:, i, :]), ['stb%d' % i], [dname])

        i_conv1 = len(P.ops)
        if stop == 2:
            P.emit(nc, st)
            return nc
        Are = scratch([128, 16], F32)
        Aim = scratch([128, 16], F32)
        dtl = scratch([128, 16], F32)
        Bre = scratch([128, 16, 16], F32)
        Bim = scratch([128, 16, 16], F32)
        CCre = scratch([128, 16, 16], F32)
        CCim = scratch([128, 16, 16], F32)
        Dcol = scratch([128, 32], F32)
        for gh in range(2):
            sl = slice(64 * gh, 64 * gh + 64)
            gs = slice(16 * gh, 16 * gh + 16)
            dma(lambda en, sl=sl, gs=gs: en.dma_start(out=Are[sl, :], in_=A_re[gs, :].rearrange("g p -> p g"),
                                                      allow_slow_non_contiguous=True), [], ['Are'])
            dma(lambda en, sl=sl, gs=gs: en.dma_start(out=Aim[sl, :], in_=A_im[gs, :].rearrange("g p -> p g"),
                                                      allow_slow_non_contiguous=True), [], ['Aim'])
            dma(lambda en, sl=sl, gs=gs: en.dma_start(out=dtl[sl, :], in_=log_dt[0:1, gs].broadcast_to([64, 16])),
                [], ['dtl'])
            dma(lambda en, sl=sl, gs=gs: en.dma_start(out=Bre[sl, :, :], in_=B_re[gs].rearrange("g p c -> p g c")),
                [], ['Bre'])
            dma(lambda en, sl=sl, gs=gs: en.dma_start(out=Bim[sl, :, :], in_=B_im[gs].rearrange("g p c -> p g c")),
                [], ['Bim'])
        for s_ in range(8):
            dma(lambda en, s_=s_: en.dma_start(out=Dcol[16 * s_:16 * s_ + 16, :], in_=Dp.rearrange("g c -> c g"),
                                               allow_slow_non_contiguous=True), [], ['Dcol'])
        Cst = scratch([128, 8, 64], F32)
        for ri, Csrc in enumerate([C_re, C_im]):
            for j in range(4):
                dma(lambda en, ri=ri, j=j, Csrc=Csrc: en.dma_start(out=Cst[:, ri * 4 + j, :],
                                                                   in_=Csrc[j * 128:(j + 1) * 128, :]),
                    [], ['Cst%d' % (ri * 4 + j)])
        for ri, CC in enumerate([CCre, CCim]):
            b = bank()
            for j in range(4):
                gh = j // 2
                op('pe', lambda en, ri=ri, j=j, gh=gh, b=b: en.matmul(
                    psf[b][64 * gh:64 * gh + 64, (j % 2) * 128:(j % 2) * 128 + 128],
                    lhsT=Cst[:, ri * 4 + j, :], rhs=identf, start=True, stop=True),
                   ['Cst%d' % (ri * 4 + j), 'identf'], ['ps%d' % b])
            op('dve', lambda en, CC=CC, b=b: en.tensor_copy(
                out=CC, in_=psf[b][:, 0:256].rearrange("p (g c) -> p g c", c=16)), ['ps%d' % b], ['CC%d' % ri])

        def dv(fn, r, w, e='dve'):
            op(e, fn, r, w)

        lre = scratch([128, 16], F32)
        dtv = scratch([128, 16], F32)
        lrd = scratch([128, 16], F32)
        thd = scratch([128, 16], F32)
        op('act', lambda en: en.activation(out=dtv, in_=dtl, func=AF.Exp), ['dtl'], ['dtv'])
        dv(lambda en: en.tensor_scalar(out=lre, in0=Are, scalar1=-1e-4, scalar2=None, op0=ALU.min), ['Are'], ['lre'])
        dv(lambda en: en.tensor_mul(out=lrd, in0=lre, in1=dtv), ['lre', 'dtv'], ['lrd'])
        dv(lambda en: en.tensor_mul(out=thd, in0=Aim, in1=dtv), ['Aim', 'dtv'], ['thd'])
        KV = [0, -1, -2, -3, -4, -5, -6, -7] + list(range(0, 9)) + [7, 6, 5, 4, 3, 2, 1, 0]
        NK = len(KV)
        kv = scratch([128, NK], F32)
        for i, kval in enumerate(KV):
            op('pool', lambda en, i=i, kval=kval: en.memset(kv[:, i:i + 1], float(kval)), [], ['kv'])
        mag = scratch([128, NK, 16], F32)
        ang = scratch([128, 2, NK, 16], F32)
        angn = scratch([128, 2, NK, 16], F32)
        angi = scratch([128, 2, NK, 16], I32)
        kvb = kv.unsqueeze(2).broadcast_to([128, NK, 16])
        dv(lambda en: en.tensor_tensor(out=mag, in0=lrd.unsqueeze(1).broadcast_to([128, NK, 16]), in1=kvb,
                                       op=ALU.mult), ['lrd', 'kv'], ['mag'])
        op('act', lambda en: en.activation(out=mag, in_=mag, func=AF.Exp), ['mag'], ['mag'])
        dv(lambda en: en.tensor_tensor(out=ang[:, 0], in0=thd.unsqueeze(1).broadcast_to([128, NK, 16]), in1=kvb,
                                       op=ALU.mult), ['thd', 'kv'], ['ang'])
        OFFS = TWO_PI * 40
        dv(lambda en: en.tensor_scalar(out=ang[:, 1], in0=ang[:, 0], scalar1=OFFS + float(np.pi / 2), scalar2=None,
                                       op0=ALU.add), ['ang'], ['ang'])
        dv(lambda en: en.tensor_scalar(out=ang[:, 0], in0=ang[:, 0], scalar1=OFFS, scalar2=None, op0=ALU.add),
           ['ang'], ['ang'])
        dv(lambda en: en.tensor_scalar(out=angn, in0=ang, scalar1=1.0 / TWO_PI, scalar2=None, op0=ALU.mult),
           ['ang'], ['angn'])
        dv(lambda en: en.tensor_copy(out=angi, in_=angn), ['angn'], ['angi'])
        dv(lambda en: en.tensor_copy(out=angn, in_=angi), ['angi'], ['angn'])
        dv(lambda en: en.scalar_tensor_tensor(out=ang, in0=angn, scalar=-TWO_PI, in1=ang, op0=ALU.mult, op1=ALU.add),
           ['angn', 'ang'], ['ang'])
        dv(lambda en: en.tensor_scalar(out=angn, in0=ang, scalar1=float(np.pi), scalar2=-TWO_PI, op0=ALU.is_gt,
                                       op1=ALU.mult), ['ang'], ['angn'])
        dv(lambda en: en.tensor_add(out=ang, in0=ang, in1=angn), ['ang', 'angn'], ['ang'])
        dv(lambda en: en.tensor_scalar(out=ang, in0=ang, scalar1=float(np.pi), scalar2=-float(np.pi), op0=ALU.min,
                                       op1=ALU.max), ['ang'], ['ang'])
        op('act', lambda en: en.activation(out=ang, in_=ang, func=AF.Sin), ['ang'], ['ang'])
        AR = scratch([128, NK, 16], F32)
        AI = scratch([128, NK, 16], F32)
        dv(lambda en: en.tensor_mul(out=AI, in0=mag, in1=ang[:, 0]), ['mag', 'ang'], ['AI'])
        dv(lambda en: en.tensor_mul(out=AR, in0=mag, in1=ang[:, 1]), ['mag', 'ang'], ['AR'])
        K1 = 9
        am1 = scratch([128, 16], F32)
        t_a = scratch([128, 16], F32)
        t_b = scratch([128, 16], F32)
        dn = scratch([128, 16], F32)
        cr = scratch([128, 16], F32)
        ci = scratch([128, 16], F32)
        dv(lambda en: en.tensor_scalar(out=am1, in0=AR[:, K1, :], scalar1=-1.0, scalar2=None, op0=ALU.add),
           ['AR'], ['am1'])
        dv(lambda en: en.tensor_mul(out=dn, in0=lre, in1=lre), ['lre'], ['dn'])
        dv(lambda en: en.tensor_mul(out=t_a, in0=Aim, in1=Aim), ['Aim'], ['t_a'])
        dv(lambda en: en.tensor_add(out=dn, in0=dn, in1=t_a), ['dn', 't_a'], ['dn'])
        dv(lambda en: en.reciprocal(out=dn, in_=dn), ['dn'], ['dn'])
        dv(lambda en: en.tensor_mul(out=t_a, in0=am1, in1=lre), ['am1', 'lre'], ['t_a'])
        dv(lambda en: en.tensor_mul(out=t_b, in0=AI[:, K1, :], in1=Aim), ['AI', 'Aim'], ['t_b'])
        dv(lambda en: en.tensor_add(out=cr, in0=t_a, in1=t_b), ['t_a', 't_b'], ['cr'])
        dv(lambda en: en.tensor_mul(out=cr, in0=cr, in1=dn), ['cr', 'dn'], ['cr'])
        dv(lambda en: en.tensor_mul(out=t_a, in0=AI[:, K1, :], in1=lre), ['AI', 'lre'], ['t_a'])
        dv(lambda en: en.tensor_mul(out=t_b, in0=am1, in1=Aim), ['am1', 'Aim'], ['t_b'])
        dv(lambda en: en.tensor_sub(out=ci, in0=t_a, in1=t_b), ['t_a', 't_b'], ['ci'])
        dv(lambda en: en.tensor_mul(out=ci, in0=ci, in1=dn), ['ci', 'dn'], ['ci'])
        bbr = scratch([128, 16, 16], F32)
        bbi = scratch([128, 16, 16], F32)
        tq = scratch([128, 16, 16], F32)
        crb = cr.unsqueeze(2).broadcast_to([128, 16, 16])
        cib = ci.unsqueeze(2).broadcast_to([128, 16, 16])
        dv(lambda en: en.tensor_tensor(out=bbr, in0=Bre, in1=crb, op=ALU.mult), ['Bre', 'cr'], ['bbr'])
        dv(lambda en: en.tensor_tensor(out=tq, in0=Bim, in1=cib, op=ALU.mult), ['Bim', 'ci'], ['tq'])
        dv(lambda en: en.tensor_sub(out=bbr, in0=bbr, in1=tq), ['bbr', 'tq'], ['bbr'])
        dv(lambda en: en.tensor_tensor(out=bbi, in0=Bim, in1=crb, op=ALU.mult), ['Bim', 'cr'], ['bbi'])
        dv(lambda en: en.tensor_tensor(out=tq, in0=Bre, in1=cib, op=ALU.mult), ['Bre', 'ci'], ['tq'])
        dv(lambda en: en.tensor_add(out=bbi, in0=bbi, in1=tq), ['bbi', 'tq'], ['bbi'])

        def cprod(outr, outi, k0, Mr, Mi, nMr, nMi, nr, ni, neg_im=False, eng='dve'):
            pr = AR[:, k0:k0 + 8, :].transpose([0, 2, 1]).unsqueeze(3).broadcast_to([128, 16, 8, 16])
            pi = AI[:, k0:k0 + 8, :].transpose([0, 2, 1]).unsqueeze(3).broadcast_to([128, 16, 8, 16])
            mr = Mr.unsqueeze(2).broadcast_to([128, 16, 8, 16])
            mi = Mi.unsqueeze(2).broadcast_to([128, 16, 8, 16])
            tmp = cp_tmp
            op(eng, lambda en: en.tensor_tensor(out=outr, in0=pr, in1=mr, op=ALU.mult), ['AR', nMr], [nr])
            op(eng, lambda en: en.tensor_tensor(out=tmp, in0=pi, in1=mi, op=ALU.mult), ['AI', nMi], ['cpt'])
            op(eng, lambda en: en.tensor_sub(out=outr, in0=outr, in1=tmp), [nr, 'cpt'], [nr])
            op(eng, lambda en: en.tensor_tensor(out=outi, in0=pr, in1=mi, op=ALU.mult), ['AR', nMi], [ni])
            op(eng, lambda en: en.tensor_tensor(out=tmp, in0=pi, in1=mr, op=ALU.mult), ['AI', nMr], ['cpt'])
            if neg_im:
                op(eng, lambda en: en.scalar_tensor_tensor(out=outi, in0=outi, scalar=-1.0, in1=tmp, op0=ALU.mult,
                                                           op1=ALU.subtract), [ni, 'cpt'], [ni])
            else:
                op(eng, lambda en: en.tensor_add(out=outi, in0=outi, in1=tmp), [ni, 'cpt'], [ni])

        cp_tmp = scratch([128, 16, 8, 16], F32)
        Lr = scratch([128, 16, 8, 16], F32)
        Li = scratch([128, 16, 8, 16], F32)
        Rr = scratch([128, 16, 8, 16], F32)
        Ri = scratch([128, 16, 8, 16], F32)
        cprod(Lr, Li, 0, bbr, bbi, 'bbr', 'bbi', 'Lr', 'Li')
        cprod(Rr, Ri, 8, CCre, CCim, 'CC0', 'CC1', 'Rr', 'Ri', neg_im=True)
        maskT = scratch([128, 128], F32)
        op('pool', lambda en: en.memset(maskT, 1.0), [], ['maskT'])
        op('pool', lambda en: en.affine_select(out=maskT.rearrange("p (t c) -> p t c", c=16),
                                               in_=maskT.rearrange("p (t c) -> p t c", c=16),
                                               compare_op=ALU.is_ge, fill=0.0, base=15,
                                               pattern=[[16, 8], [0, 16]], channel_multiplier=-1),
           ['maskT'], ['maskT'])
        ttmp = scratch([128, 2, 128], F32)
        for g in range(32):
            gh, g16 = g // 16, g % 16
            sl = slice(64 * gh, 64 * gh + 64)
            b = bank()
            op('pe', lambda en, b=b, sl=sl, g16=g16: en.matmul(
                psf[b][:, 0:128], lhsT=Lr[sl, g16].rearrange("p s c -> p (s c)"),
                rhs=Rr[sl, g16].rearrange("p s c -> p (s c)"), start=True, stop=False),
               ['Lr', 'Rr'], ['ps%d' % b])
            op('pe', lambda en, b=b, sl=sl, g16=g16: en.matmul(
                psf[b][:, 0:128], lhsT=Li[sl, g16].rearrange("p s c -> p (s c)"),
                rhs=Ri[sl, g16].rearrange("p s c -> p (s c)"), start=False, stop=True),
               ['Li', 'Ri'], ['ps%d' % b])
            j = g % 2
            op('dve', lambda en, b=b, j=j: en.tensor_tensor(out=ttmp[:, j, :], in0=psf[b][:, 0:128], in1=maskT,
                                                            op=ALU.mult), ['ps%d' % b, 'maskT'], ['ttmp%d' % j])
            op('dve', lambda en, g=g, j=j: en.scalar_tensor_tensor(out=Tm[:, g, :], in0=identf,
                                                                   scalar=Dcol[:, g:g + 1], in1=ttmp[:, j, :],
                                                                   op0=ALU.mult, op1=ALU.add),
               ['ttmp%d' % j, 'identf', 'Dcol'], ['Tm'])
        cprod(Rr, Ri, 9, CCre, CCim, 'CC0', 'CC1', 'Rr', 'Ri', neg_im=True)
        op('dve', lambda en: en.tensor_copy(out=Qm[:, :, 0, :], in_=Rr.rearrange("p g t c -> p g (t c)")),
           ['Rr'], ['Qm'])
        op('dve', lambda en: en.tensor_copy(out=Qm[:, :, 1, :], in_=Ri.rearrange("p g t c -> p g (t c)")),
           ['Ri'], ['Qm'])
        cprod(Lr, Li, 17, bbr, bbi, 'bbr', 'bbi', 'Lr', 'Li')
        for g in range(32):
            gh, g16 = g // 16, g % 16
            sl = slice(64 * gh, 64 * gh + 64)
            b = bank()
            for ri, LL in enumerate([Lr, Li]):
                op('pe', lambda en, b=b, sl=sl, g16=g16, ri=ri, LL=LL: en.matmul(
                    psf[b][:, ri * 64:ri * 64 + 64], lhsT=LL[sl, g16].rearrange("p s c -> p (s c)"),
                    rhs=identf[sl, sl], start=True, stop=True), ['Lr', 'Li', 'identf'], ['ps%d' % b])
            copy_any(Pm[:, g, :], psf[b][:, 0:128], ['ps%d' % b], ['Pm'])
        K8 = 16
        for blk in range(2):
            op('dve', lambda en, blk=blk: en.tensor_copy(
                out=C1[:, blk], in_=AR[:, K8, :].unsqueeze(2).broadcast_to([128, 16, 4])), ['AR'], ['C1'])
        op('dve', lambda en: en.tensor_scalar(out=C2[:, 0], in0=AI[:, K8, :].unsqueeze(2).broadcast_to([128, 16, 4]),
                                              scalar1=-1.0, scalar2=None, op0=ALU.mult), ['AI'], ['C2'])
        op('dve', lambda en: en.tensor_copy(out=C2[:, 1], in_=AI[:, K8, :].unsqueeze(2).broadcast_to([128, 16, 4])),
           ['AI'], ['C2'])

        i_prep1 = len(P.ops)
        convs = P.ops[i_conv0:i_conv1]
        preps = P.ops[i_conv1:i_prep1]
        P.ops[i_conv0:i_prep1] = merge_ops(preps, convs)
        P.barrier()
        op('pool', lambda en: en.memset(Pprev, 0.0), [], ['Pprev0', 'Pprev1'])
        op('pool', lambda en: en.memset(Pcur, 0.0), [], ['Pcur0', 'Pcur1'])
        op('pool', lambda en: en.memset(Pmeta, 0.0), [], ['Pmeta0', 'Pmeta1'])
        op('pool', lambda en: en.memset(Vt, 1.0), [], ['Vt'])
        P.barrier()
        bank_ctr['front'] = True

        def ssm_recurrence(NS, NCH, z0_pp):
            F = 16 * NS
            Vv = Vs[:, :, :, 0:NS * NCH].rearrange("p r g (s k) -> p r g s k", k=NCH)
            Hv = Hb[:, :, :, 0:NS * NCH].rearrange("p r g (s k) -> p r g s k", k=NCH)
            c1 = C1[:, :, :, 0:NS]
            c2 = C2[:, :, :, 0:NS]
            pp = z0_pp
            for k in range(NCH):
                zc = Z[:, pp, :, 0:F].rearrange("p b (g s) -> p b g s", s=NS)
                zn = Z[:, 1 - pp, :, 0:F].rearrange("p b (g s) -> p b g s", s=NS)
                t1 = T1[:, :, 0:F].rearrange("p b (g s) -> p b g s", s=NS)
                t2 = T2[:, :, 0:F].rearrange("p b (g s) -> p b g s", s=NS)
                op('pool', lambda en, zc=zc, k=k: en.tensor_copy(out=Hv[:, :, :, :, k], in_=zc[:, 0:2]),
                   ['Z%d' % pp], ['Hb'])
                op('pool', lambda en, zc=zc, t1=t1: en.tensor_tensor(out=t1, in0=zc[:, 0:2], in1=c1, op=ALU.mult),
                   ['Z%d' % pp, 'C1'], ['T1'])
                op('pool', lambda en, zc=zc, t2=t2: en.tensor_tensor(out=t2, in0=zc[:, 1:3], in1=c2, op=ALU.mult),
                   ['Z%d' % pp, 'C2'], ['T2'])
                op('pool', lambda en, t1=t1, t2=t2: en.tensor_add(out=t1, in0=t1, in1=t2), ['T1', 'T2'], ['T1'])
                op('pool', lambda en, zn=zn, t1=t1, k=k: en.tensor_add(out=zn[:, 0:2], in0=t1, in1=Vv[:, :, :, :, k]),
                   ['T1', 'Vs'], ['Z%d' % (1 - pp)])
                op('pool', lambda en, zn=zn: en.tensor_copy(out=zn[:, 2], in_=zn[:, 0]),
                   ['Z%d' % (1 - pp)], ['Z%d' % (1 - pp)])
                pp = 1 - pp
            return pp

        def norm_transpose(NS, TT, dst, dname):
            for m in range(NS):
                xb = m % 2
                op('act', lambda en, m=m, xb=xb: en.activation(out=xnb[0:TT, xb], in_=xt[0:TT, m, :], func=AF.Square,
                                                               accum_out=ss[0:TT, m:m + 1]),
                   ['xt%d' % m], ['xnb%d' % xb, 'ss%d' % m])
            ssn = ['ss%d' % m for m in range(NS)]
            rsn = ['rs%d' % m for m in range(NS)]
            op('act', lambda en: en.activation(out=rs[0:TT, 0:NS], in_=ss[0:TT, 0:NS], func=AF.Ln, scale=1.0 / D,
                                               bias=epsb[0:TT, 0:1]), ssn, rsn)
            op('act', lambda en: en.activation(out=rs[0:TT, 0:NS], in_=rs[0:TT, 0:NS], func=AF.Exp, scale=-0.5),
               rsn, rsn)
            for m in range(NS):
                xb = m % 2
                op('dve', lambda en, m=m, xb=xb: en.tensor_scalar(out=xnb[0:TT, xb], in0=xt[0:TT, m, :],
                                                                  scalar1=rs[0:TT, m:m + 1], scalar2=None,
                                                                  op0=ALU.mult),
                   ['xt%d' % m, 'rs%d' % m], ['xnb%d' % xb])
                b = bank()
                for kc in range(8):
                    op('pe', lambda en, b=b, kc=kc, xb=xb: en.transpose(
                        out=psb[b][:, kc * 128:kc * 128 + TT], in_=xnb[0:TT, xb, kc * 128:(kc + 1) * 128],
                        identity=identb[0:TT, 0:TT]), ['xnb%d' % xb, 'identb'], ['ps%d' % b])
                copy_any(dst[:, :, m * TT:(m + 1) * TT],
                         psb[b].rearrange("p (k t) -> p k t", t=128)[:, :, 0:TT], ['ps%d' % b], ['%s%d' % (dname, m)])

        XMALL_ = ['XM0', 'XM1', 'XM2', 'XM3']
        US_ = ['US'] if overlap else []
        UZ_ = ['UZ'] if overlap else []
        YA_ = ['yA0', 'yA1', 'yA2', 'yA3']
        HNALL_ = ['XM0', 'XM1', 'XM2', 'XM3']
        zstate = {'pp': 0}

        def block(kind, j, phase):
            if kind == 'P':
                NS, TT = 4, 128
            elif kind == 'S':
                NS, TT = 4, 64
            else:
                NS, TT = 1, 16
            NCH = TT // 8
            NQ = NS * NCH
            NT = NS * TT
            slot = j % 2 if kind == 'P' else 0
            pslot = 1 - slot
            full = kind != 'M'

            def ydst(m):
                if kind == 'P':
                    return yp[m, j * 128:(j + 1) * 128, :]
                return ys[m, :, :]

            def xsrc(m):
                if kind == 'P':
                    return xp[m, j * 128:(j + 1) * 128, :]
                if kind == 'S':
                    return xs[m, :, :]
                return meta

            def f1():
                for m in range(NS):
                    dma(lambda en, m=m: en.dma_start(out=xt[0:TT, m, :], in_=xsrc(m)), [], ['xt%d' % m])
                norm_transpose(NS, TT, XM, 'XM')

                if stop == 5.01 and kind == 'P':
                    return True
                for s_ in range(8):
                    b = bank()
                    for kc in range(8):
                        op('pe', lambda en, b=b, kc=kc, s_=s_: en.matmul(
                            psf[b][0:NQ, :], lhsT=XM[:, kc, 0:NT].rearrange("p (q s) -> p q s", s=8)[:, :, s_],
                            rhs=win[:, kc, 768:1280], start=(kc == 0), stop=(kc == 7)), XMALL_ + ['win'], ['ps%d' % b])
                    copy_any(u_ks[0:NQ, :, s_, :], psf[b][0:NQ, :].rearrange("p (g c) -> p g c", c=16),
                             ['ps%d' % b], ['u_ks', *US_])
                if stop == 5.05 and kind == 'P':
                    return True
                def qkv_mm(m):
                        bq = bank()
                        bk = bank()
                        for kc in range(8):
                            op('pe', lambda en, kc=kc, m=m, bq=bq: en.matmul(
                                psf[bq][0:TT, :], lhsT=XM[:, kc, m * TT:(m + 1) * TT], rhs=win[:, kc, 0:512],
                                start=(kc == 0), stop=(kc == 7)), ['XM%d' % m, 'win'], ['ps%d' % bq])
                        for kc in range(8):
                            op('pe', lambda en, kc=kc, m=m, bk=bk: en.matmul(
                                psf[bk][0:TT, 0:256], lhsT=XM[:, kc, m * TT:(m + 1) * TT], rhs=win[:, kc, 512:768],
                                start=(kc == 0), stop=(kc == 7)), ['XM%d' % m, 'win'], ['ps%d' % bk])
                        return bq, bk

                def qk_chain(m, bq, bk):
                        mp = m % 2
                        op('act', lambda en, bq=bq: en.activation(out=qsq[0:TT, 0:512], in_=psf[bq][0:TT, :], func=AF.Square),
                           ['ps%d' % bq], ['qsq'])
                        op('act', lambda en, bk=bk: en.activation(out=qsq[0:TT, 512:640], in_=psf[bk][0:TT, 0:128],
                                                                  func=AF.Square), ['ps%d' % bk], ['qsq'])
                        op('dve', lambda en, mp=mp: en.tensor_reduce(
                            out=st10[0:TT, mp, :], in_=qsq[0:TT, :].rearrange("p (h d) -> p h d", d=64), axis=AX.X,
                            op=ALU.add), ['qsq'], ['st10%d' % mp])
                        rstd_from(st10[0:TT, mp, :], r10[0:TT, mp, :], 64, 'st10%d' % mp, 'r10%d' % mp)
                        op('dve', lambda en, mp=mp, bq=bq: en.tensor_tensor(
                            out=qnb[0:TT, mp, :].rearrange("p (h d) -> p h d", d=64),
                            in0=psf[bq][0:TT, :].rearrange("p (h d) -> p h d", d=64),
                            in1=r10[0:TT, mp, 0:8].unsqueeze(2).broadcast_to([TT, 8, 64]), op=ALU.mult),
                           ['ps%d' % bq, 'r10%d' % mp], ['qnb%d' % mp])
                        op('dve', lambda en, mp=mp, bk=bk: en.tensor_tensor(
                            out=kf[0:TT, mp, :].rearrange("p (h d) -> p h d", d=64),
                            in0=psf[bk][0:TT, 0:128].rearrange("p (h d) -> p h d", d=64),
                            in1=r10[0:TT, mp, 8:10].unsqueeze(2).broadcast_to([TT, 2, 64]), op=ALU.mult),
                           ['ps%d' % bk, 'r10%d' % mp], ['kf%d' % mp])
                        op('dve', lambda en, mp=mp: en.tensor_tensor(out=kf[0:TT, mp, :], in0=kf[0:TT, mp, :],
                                                                     in1=gk_t[0:TT].rearrange("p a d -> p (a d)"),
                                                                     op=ALU.mult), ['kf%d' % mp, 'gk_t'], ['kf%d' % mp])
                        op('act', lambda en, mp=mp: en.copy(out=kb[0:TT, mp, :], in_=kf[0:TT, mp, :]),
                           ['kf%d' % mp], ['kb%d' % mp])
                        vdst = Vm[0:TT, :, 0:64] if kind == 'M' else Vt[0:TT, slot, m, :, 0:64]
                        vname = 'Vm' if kind == 'M' else 'Vt%d_%d' % (slot, m)
                        op('act', lambda en, bk=bk, vdst=vdst: en.copy(
                            out=vdst, in_=psf[bk][0:TT, 128:256].rearrange("p (h d) -> p h d", d=64)),
                           ['ps%d' % bk], [vname])
                        need_out = (kind != 'P') or (j == NBLK - 1)
                        if need_out:
                            op('dve', lambda en, bk=bk: en.tensor_copy(out=vf[0:TT, :], in_=psf[bk][0:TT, 128:256]),
                               ['ps%d' % bk], ['vf'])
                            if kind == 'P':
                                dma(lambda en, m=m, mp=mp: en.dma_start(out=kpo[m, 16:144, :], in_=kf[0:TT, mp, :]),
                                    ['kf%d' % mp], [])
                                dma(lambda en, m=m: en.dma_start(out=vpo[m, 16:144, :], in_=vf[0:TT, :]), ['vf'], [])
                            elif kind == 'S':
                                dma(lambda en, m=m, mp=mp: en.dma_start(out=kso[m, :, :], in_=kf[0:TT, mp, :]),
                                    ['kf%d' % mp], [])
                                dma(lambda en, m=m: en.dma_start(out=vso[m, :, :], in_=vf[0:TT, :]), ['vf'], [])
                            else:
                                for mm in range(NSEQ):
                                    dma(lambda en, mm=mm, mp=mp: en.dma_start(out=kpo[mm, 0:16, :], in_=kf[0:TT, mp, :]),
                                        ['kf%d' % mp], [])
                                    dma(lambda en, mm=mm: en.dma_start(out=vpo[mm, 0:16, :], in_=vf[0:TT, :]), ['vf'], [])
                        b = bank()
                        for hk in range(2):
                            op('pe', lambda en, b=b, hk=hk, mp=mp: en.transpose(
                                out=psb[b][0:64, hk * 128:hk * 128 + TT], in_=kb[0:TT, mp, hk * 64:(hk + 1) * 64],
                                identity=identb[0:TT, 0:TT]), ['kb%d' % mp, 'identb'], ['ps%d' % b])
                        if kind == 'M':
                            ktd = KTm[:, :, 0:TT]
                            ktn = 'KTm'
                        else:
                            ktd = KT[:, :, slot, m, 0:TT]
                            ktn = 'KT%d_%d' % (slot, m)
                        op('act', lambda en, b=b, ktd=ktd: en.mul(
                            out=ktd, in_=psb[b][0:64, 0:256].rearrange("p (h t) -> p h t", t=128)[:, :, 0:TT],
                            mul=gq8[:, 0:1]), ['ps%d' % b, 'gq8'], [ktn])
                        if full:
                            b = bank()
                            for h in range(8):
                                op('pe', lambda en, b=b, h=h, mp=mp: en.transpose(
                                    out=psb[b][0:64, h * 128:h * 128 + TT], in_=qnb[0:TT, mp, h * 64:(h + 1) * 64],
                                    identity=identb[0:TT, 0:TT]), ['qnb%d' % mp, 'identb'], ['ps%d' % b])
                            op('dve', lambda en, b=b, m=m: en.tensor_copy(
                                out=QT[:, m, :, 0:TT], in_=psb[b][0:64, :].rearrange("p (h t) -> p h t", t=128)[:, :, 0:TT]),
                               ['ps%d' % b], ['QT%d' % m])


                def stage_qk():
                    if overlap:
                        for m in range(NS):
                            cur = qkv_mm(m)
                            qk_chain(m, *cur)
                    else:
                        prev = None
                        for m in range(NS):
                            cur = qkv_mm(m)
                            if prev is not None:
                                qk_chain(m - 1, *prev)
                            prev = cur
                        qk_chain(NS - 1, *prev)

                if stop == 5.1 and kind == 'P':
                    return True
                for g0 in range(0, 32, 8):
                    b = bank()
                    for g in range(g0, g0 + 8):
                        op('pe', lambda en, b=b, g=g: en.transpose(
                            out=psb[b][:, (g % 8) * 64:(g % 8) * 64 + NQ],
                            in_=u_ks[0:NQ, g].rearrange("p s c -> p (s c)"), identity=identb[0:NQ, 0:NQ]),
                           ['u_ks', *US_, 'identb'], ['ps%d' % b])
                    copy_any(U_all[:, g0:g0 + 8, 0:NQ],
                             psb[b][:, 0:512].rearrange("p (g q) -> p g q", q=64)[:, :, 0:NQ], ['ps%d' % b], ['U_all', *UZ_])
                for gb in range(4):
                    b = bank()
                    for gh in range(2):
                        for gl in range(4):
                            g16 = gb * 4 + gl
                            g = gh * 16 + g16
                            for ri in range(2):
                                op('pe', lambda en, b=b, g=g, gh=gh, gl=gl, ri=ri: en.matmul(
                                    psf[b][64 * gh:64 * gh + 64, (ri * 4 + gl) * 64:(ri * 4 + gl) * 64 + NQ],
                                    lhsT=Pm[:, g, ri * 64:(ri + 1) * 64], rhs=U_all[:, g, 0:NQ], start=True, stop=True),
                                   ['Pm', 'U_all', *UZ_], ['ps%d' % b])
                    copy_any(Vs[:, :, gb * 4:gb * 4 + 4, 0:NQ],
                             psf[b].rearrange("p (r g q) -> p r g q", r=2, q=64)[:, :, :, 0:NQ], ['ps%d' % b], ['Vs'])
                if stop == 5.11 and kind == 'P':
                    return True
                F = 16 * NS
                if kind == 'M':
                    op('pool', lambda en: en.memset(Z[:, 0], 0.0), [], ['Z0'])
                    zstate['pp'] = 0
                elif kind == 'P' and j == 0:
                    zv = Z[:, 0, :, 0:F].rearrange("p b (g s) -> p b g s", s=NS)
                    op('pool', lambda en, zv=zv: en.tensor_copy(
                        out=zv, in_=Zmeta.unsqueeze(3).broadcast_to([128, 3, 16, NS])), ['Zmeta'], ['Z0'])
                    zstate['pp'] = 0
                elif kind == 'S':
                    b = bank()
                    for ri, src in enumerate([s_re, s_im]):
                        dma(lambda en, ri=ri, src=src: en.dma_start(out=attn[:, ri, 0:64], in_=src), [], ['attn%d' % ri])
                        for gh in range(2):
                            op('pe', lambda en, b=b, ri=ri, gh=gh: en.matmul(
                                psf[b][64 * gh:64 * gh + 64, ri * 128:ri * 128 + 128], lhsT=attn[:, ri, 0:64],
                                rhs=identf, start=True, stop=True), ['attn%d' % ri, 'identf'], ['ps%d' % b])
                    for gh in range(2):
                        sl = slice(64 * gh, 64 * gh + 64)
                        for blk, ri in enumerate([0, 1, 0]):
                            src = psf[b][sl, ri * 128:ri * 128 + 128].rearrange("p (s g) -> p g s", g=32)[:, 16 * gh:16 * gh + 16, :]
                            op('dve', lambda en, sl=sl, blk=blk, src=src: en.tensor_copy(
                                out=Z[sl, 0, blk, 0:64].rearrange("p (g s) -> p g s", s=4), in_=src),
                               ['ps%d' % b], ['Z0'])
                    zstate['pp'] = 0
                if stop == 5.12 and kind == 'P':
                    return True
                pp_end = ssm_recurrence(NS, NCH, zstate['pp'])
                zstate['pp'] = pp_end
                if kind == 'M':
                    op('pool', lambda en: en.tensor_copy(out=Zmeta, in_=Z[:, pp_end, :, 0:16]), ['Z%d' % pp_end], ['Zmeta'])
                if stop == 5.13 and kind == 'P':
                    return True
                if kind != 'M' and (kind == 'S' or j == NBLK - 1):
                    o_re, o_im = (srs, sis) if kind == 'S' else (srp, sip)
                    for ri, dst in enumerate([o_re, o_im]):
                        op('dve', lambda en, ri=ri: en.tensor_copy(
                            out=qsq[:, ri * 64:(ri + 1) * 64].rearrange("p (s g) -> p s g", g=16),
                            in_=Z[:, pp_end, ri, 0:64].rearrange("p (g s) -> p s g", s=4)),
                           ['Z%d' % pp_end], ['qsq'])
                        b = bank()
                        op('pe', lambda en, b=b, ri=ri: en.matmul(
                            psf[b][0:64, 0:128], lhsT=qsq[:, ri * 64:(ri + 1) * 64],
                            rhs=identf, start=True, stop=True), ['qsq', 'identf'], ['ps%d' % b])
                        op('dve', lambda en, b=b, ri=ri: en.tensor_copy(out=attn[0:64, ri, 0:128], in_=psf[b][0:64, 0:128]),
                           ['ps%d' % b], ['attn%d' % ri])
                        for s_ in range(4 if stop != 5.14 else 0):
                            for gh in range(2):
                                dma(lambda en, s_=s_, gh=gh, ri=ri, dst=dst: en.dma_start(
                                    out=dst[s_ * 32 + gh * 16:s_ * 32 + gh * 16 + 16, :],
                                    in_=attn[s_ * 16:s_ * 16 + 16, ri, gh * 64:gh * 64 + 64]), ['attn%d' % ri], [])

                stage_qk()
                if kind == 'M':
                    return
                if stop in (5.2, 5.14) and kind == "P":
                    return True
                if kind == 'S':
                    for m in range(NS):
                        dma(lambda en, m=m: en.dma_start(out=kf[:, 0, :], in_=ck[m, 16:144, :]), [], ['kf0'])
                        dma(lambda en, m=m: en.dma_start(out=kf[0:16, 1, :], in_=ck[m, 0:16, :]), [], ['kf1'])
                        dma(lambda en, m=m: en.dma_start(out=vf[:, :], in_=cv[m, 16:144, :]), [], ['vf'])
                        dma(lambda en, m=m: en.dma_start(out=qsq[0:16, 0:128], in_=cv[m, 0:16, :]), [], ['qsq'])
                        op('pool', lambda en: en.tensor_copy(out=kb[:, 0, :], in_=kf[:, 0, :]), ['kf0'], ['kb0'])
                        op('pool', lambda en: en.tensor_copy(out=kb[0:16, 1, :], in_=kf[0:16, 1, :]), ['kf1'], ['kb1'])
                        op('act', lambda en, m=m: en.copy(out=Vt[:, 1, m, :, 0:64],
                                                          in_=vf[:, :].rearrange("p (h d) -> p h d", d=64)),
                           ['vf'], ['Vt1_%d' % m])
                        op('act', lambda en, m=m: en.copy(out=VmS[:, m, :, 0:64], in_=qsq[0:16, 0:128].rearrange("p (h d) -> p h d", d=64)),
                           ['qsq'], ['VmS%d' % m])
                        b = bank()
                        for hk in range(2):
                            op('pe', lambda en, b=b, hk=hk: en.transpose(
                                out=psb[b][0:64, hk * 128:hk * 128 + 128], in_=kb[:, 0, hk * 64:(hk + 1) * 64],
                                identity=identb), ['kb0', 'identb'], ['ps%d' % b])
                            op('pe', lambda en, b=b, hk=hk: en.transpose(
                                out=psb[b][0:64, 256 + hk * 16:256 + hk * 16 + 16], in_=kb[0:16, 1, hk * 64:(hk + 1) * 64],
                                identity=identb[0:16, 0:16]), ['kb1', 'identb'], ['ps%d' % b])
                        op('act', lambda en, b=b, m=m: en.mul(
                            out=KT[:, :, 1, m, :], in_=psb[b][0:64, 0:256].rearrange("p (h t) -> p h t", t=128),
                            mul=gq8[:, 0:1]), ['ps%d' % b, 'gq8'], ['KT1_%d' % m])
                        op('act', lambda en, b=b, m=m: en.mul(
                            out=KTmS[:, m], in_=psb[b][0:64, 256:288].rearrange("p (h t) -> p h t", t=16),
                            mul=gq8[:, 0:1]), ['ps%d' % b, 'gq8'], ['KTmS%d' % m])
                has_prev = (kind == 'S') or (j > 0)
                NQC = 4 * TT

                def att_S(ui, m, hk):
                    c = dict(m=m, hk=hk, pset=ui % 2)
                    qrhs = QT[:, m, 4 * hk:4 * hk + 4, 0:TT]
                    bp = bank() if has_prev else None
                    bc = bank()
                    bm = bank()
                    c.update(bp=bp, bc=bc, bm=bm)
                    if has_prev:
                        op('pe', lambda en: en.matmul(
                            psf[bp][:, 0:NQC], lhsT=KT[:, hk, pslot, m, :], rhs=qrhs, start=True, stop=True),
                           ['KT%d_%d' % (pslot, m), 'QT%d' % m], ['ps%d' % bp])
                    op('pe', lambda en: en.matmul(
                        psf[bc][0:TT, 0:NQC], lhsT=KT[:, hk, slot, m, 0:TT], rhs=qrhs, start=True, stop=True),
                       ['KT%d_%d' % (slot, m), 'QT%d' % m], ['ps%d' % bc])
                    if kind == 'S':
                        ktm = KTmS[:, m, hk, :]
                        ktmn = 'KTmS%d' % m
                        c.update(vmv=VmS[:, m, hk, :], vmn='VmS%d' % m)
                    else:
                        ktm = KTm[:, hk, :]
                        ktmn = 'KTm'
                        c.update(vmv=Vm[:, hk, :], vmn='Vm')
                    op('pe', lambda en: en.matmul(
                        psf[bm][0:16, 0:NQC], lhsT=ktm, rhs=qrhs, start=True, stop=True),
                       [ktmn, 'QT%d' % m], ['ps%d' % bm])
                    return c

                def att_exp(c):
                    pset, bp, bc, bm = c['pset'], c['bp'], c['bc'], c['bm']
                    pv = Pprev[:, pset, :, 0:TT]
                    pc = Pcur[:, pset, :, 0:TT]
                    pm_ = Pmeta[:, pset, :, 0:TT]
                    c.update(pv=pv, pc=pc, pm_=pm_)
                    if has_prev:
                        sp = psf[bp][:, 0:NQC].rearrange("p (h t) -> p h t", t=TT)
                        if kind == 'P':
                            op('act', lambda en: en.activation(out=pv[64:128], in_=sp[64:128], func=AF.Exp),
                               ['ps%d' % bp], ['Pprev%d' % pset])
                            op('act', lambda en: en.activation(out=pv[0:64, :, 0:64], in_=sp[0:64, :, 0:64], func=AF.Exp),
                               ['ps%d' % bp], ['Pprev%d' % pset])
                        else:
                            op('act', lambda en: en.activation(out=pv, in_=sp, func=AF.Exp),
                               ['ps%d' % bp], ['Pprev%d' % pset])
                    sc_ = psf[bc][0:TT, 0:NQC].rearrange("p (h t) -> p h t", t=TT)
                    if kind == 'P':
                        op('act', lambda en: en.activation(out=pc[0:64], in_=sc_[0:64], func=AF.Exp),
                           ['ps%d' % bc], ['Pcur%d' % pset])
                        op('act', lambda en: en.activation(out=pc[64:128, :, 64:128], in_=sc_[64:128, :, 64:128],
                                                           func=AF.Exp), ['ps%d' % bc], ['Pcur%d' % pset])
                    else:
                        op('act', lambda en: en.activation(out=pc[0:TT], in_=sc_, func=AF.Exp),
                           ['ps%d' % bc], ['Pcur%d' % pset])
                    sm = psf[bm][0:16, 0:NQC].rearrange("p (h t) -> p h t", t=TT)
                    op('act', lambda en: en.activation(out=pm_, in_=sm, func=AF.Exp), ['ps%d' % bm], ['Pmeta%d' % pset])

                def att_PV(c):
                    m, hk, pset = c['m'], c['hk'], c['pset']
                    pv, pc, pm_, vmv, vmn = c['pv'], c['pc'], c['pm_'], c['vmv'], c['vmn']
                    bo = bank()
                    for h in range(4):
                        ov = psf[bo][0:TT, h * 65:h * 65 + 65]
                        first = True
                        if has_prev:
                            op('pe', lambda en, ov=ov, h=h: en.matmul(
                                ov, lhsT=pv[:, h, :], rhs=Vt[:, pslot, m, hk, :], start=True, stop=False),
                               ['Pprev%d' % pset, 'Vt%d_%d' % (pslot, m)], ['ps%d' % bo])
                            first = False
                        op('pe', lambda en, ov=ov, h=h, first=first: en.matmul(
                            ov, lhsT=pc[0:TT, h, :], rhs=Vt[0:TT, slot, m, hk, :], start=first, stop=False),
                           ['Pcur%d' % pset, 'Vt%d_%d' % (slot, m)], ['ps%d' % bo])
                        op('pe', lambda en, ov=ov, h=h: en.matmul(
                            ov, lhsT=pm_[:, h, :], rhs=vmv, start=False, stop=True),
                           ['Pmeta%d' % pset, vmn], ['ps%d' % bo])
                    o3 = psf[bo][0:TT, 0:260].rearrange("p (h e) -> p h e", e=65)
                    mp = m % 2
                    op('dve', lambda en: en.tensor_tensor(
                        out=den[0:TT, pset, :], in0=o3[:, :, 64], in1=esink[0:TT, 4 * hk:4 * hk + 4], op=ALU.add),
                       ['ps%d' % bo, 'esink'], ['den%d' % pset])
                    op('dve', lambda en: en.reciprocal(out=den[0:TT, pset, :], in_=den[0:TT, pset, :]),
                       ['den%d' % pset], ['den%d' % pset])
                    op('dve', lambda en: en.tensor_tensor(
                        out=attn[0:TT, mp, hk * 256:(hk + 1) * 256].rearrange("p (h d) -> p h d", d=64),
                        in0=o3[:, :, 0:64], in1=den[0:TT, pset, :].unsqueeze(2).broadcast_to([TT, 4, 64]),
                        op=ALU.mult), ['ps%d' % bo, 'den%d' % pset], ['attn%d' % mp])

                def att_norm(m):
                    mp = m % 2
                    op('act', lambda en: en.activation(out=anb[0:TT], in_=attn[0:TT, mp, :], func=AF.Square,
                                                       accum_out=ss[0:TT, 4 + m:5 + m]),
                       ['attn%d' % mp], ['anb', 'ssa%d' % m])
                    rstd_from(ss[0:TT, 4 + m:5 + m], rs[0:TT, 4 + m:5 + m], 512, 'ssa%d' % m, 'rsa%d' % m)
                    op('dve', lambda en: en.tensor_scalar(out=anb[0:TT], in0=attn[0:TT, mp, :],
                                                          scalar1=rs[0:TT, 4 + m:5 + m], scalar2=None, op0=ALU.mult),
                       ['attn%d' % mp, 'rsa%d' % m], ['anb'])
                    b = bank()
                    for kc in range(4):
                        op('pe', lambda en, kc=kc: en.transpose(out=psb[b][:, kc * 128:kc * 128 + TT],
                                                                in_=anb[0:TT, kc * 128:(kc + 1) * 128],
                                                                identity=identb[0:TT, 0:TT]),
                           ['anb', 'identb'], ['ps%d' % b])
                    copy_any(XM[:, 0:4, m * TT:(m + 1) * TT],
                             psb[b][:, 0:512].rearrange("p (k t) -> p k t", t=128)[:, :, 0:TT], ['ps%d' % b], ['XM%d' % m])

                units = [(m, hk) for m in range(NS) for hk in range(2)]
                ctxs = [None] * len(units)
                if overlap:
                    for ui in range(len(units)):
                        ctxs[ui] = att_S(ui, *units[ui])
                        att_exp(ctxs[ui])
                        att_PV(ctxs[ui])
                        if units[ui][1] == 1:
                            att_norm(units[ui][0])
                else:
                    for ui in range(len(units) + 1):
                        if ui < len(units):
                            ctxs[ui] = att_S(ui, *units[ui])
                            att_exp(ctxs[ui])
                        if ui >= 1:
                            att_PV(ctxs[ui - 1])
                            if units[ui - 1][1] == 1:
                                att_norm(units[ui - 1][0])


            def f2():
                if stop == 5.3 and kind == 'P':
                    return True
                op('act', lambda en: en.copy(out=dumA, in_=epsb), ['epsb'],
                   ['dumA'] + ['gT%d' % i_ for i_ in range(16)] + ['LOCK2'])
                for g0 in range(0, 32, 8):
                    b = bank()
                    for g in range(g0, g0 + 8):
                        gh, g16 = g // 16, g % 16
                        sl = slice(64 * gh, 64 * gh + 64)
                        for th in range(2):
                            ov = psf[b][64 * th:64 * th + NQ, (g % 8) * 64:(g % 8) * 64 + 64]
                            op('pe', lambda en, ov=ov, g=g, th=th: en.matmul(
                                ov, lhsT=U_all[:, g, 0:NQ], rhs=Tm[:, g, 64 * th:64 * th + 64], start=True, stop=False),
                               ['U_all', *UZ_, 'Tm'], ['ps%d' % b])
                            for ri in range(2):
                                op('pe', lambda en, ov=ov, sl=sl, g16=g16, th=th, ri=ri: en.matmul(
                                    ov, lhsT=Hb[sl, ri, g16, 0:NQ], rhs=Qm[sl, g16, ri, 64 * th:64 * th + 64],
                                    start=False, stop=(ri == 1)), ['Hb', 'Qm'], ['ps%d' % b])
                    for pr in ([slice(0, 128)] if NQ == 64 else [slice(64 * th_, 64 * th_ + NQ) for th_ in range(2)]):
                        op('act', lambda en, b=b, g0=g0, pr=pr: en.activation(
                            out=yA[pr].rearrange("p t (g c) -> p g t c", c=16)[:, g0:g0 + 8],
                            in_=psf[b][pr, :].rearrange("p (g t c) -> p g t c", t=4, c=16), func=AF.Gelu_apprx_tanh),
                           ['ps%d' % b, 'LOCK2'], YA_)
                    op('dve', lambda en, g0=g0: en.tensor_copy(out=zs_bf[:, :, g0 * 16:(g0 + 8) * 16],
                                                               in_=yA[:, :, g0 * 16:(g0 + 8) * 16]), YA_, ['zs_bf', *US_])
                YA = ['yA0', 'yA1', 'yA2', 'yA3']

                def d2_T(t4):
                    b = bank()
                    for kc in range(4):
                        op('pe', lambda en, kc=kc: en.transpose(
                            out=psb[b][:, kc * 128:(kc + 1) * 128], in_=zs_bf[:, t4, kc * 128:(kc + 1) * 128],
                            identity=identb), ['zs_bf', *US_, 'identb'], ['ps%d' % b])
                    op('dve', lambda en: en.tensor_copy(
                        out=zsT[:, :, t4, :], in_=psb[b][:, 0:512].rearrange("p (k q) -> p k q", q=128)),
                       ['ps%d' % b], ['zsT%d' % t4, *UZ_])

                def d2_G(t4):
                    b = bank()
                    for th in range(2):
                        ov = psf[b][64 * th:64 * th + NQ, :]
                        for kc in range(4):
                            op('pe', lambda en, ov=ov, kc=kc, th=th: en.matmul(
                                ov, lhsT=zsT[:, kc, t4, 64 * th:64 * th + NQ], rhs=wglu[:, kc, :], start=(kc == 0),
                                stop=False), ['zsT%d' % t4, *UZ_, 'wglu'], ['ps%d' % b])
                        op('pe', lambda en, ov=ov: en.matmul(ov, lhsT=onesr[0:1, 0:NQ], rhs=bglu[0:1, :],
                                                             start=False, stop=True),
                           ['onesr', 'bglu'], ['ps%d' % b])
                    for pr in ([slice(0, 128)] if NQ == 64 else [slice(64 * th_, 64 * th_ + NQ) for th_ in range(2)]):
                        op('act', lambda en, pr=pr: en.activation(out=yB[pr, t4, :], in_=psf[b][pr, :],
                                                                  func=AF.Sigmoid), ['ps%d' % b], ['yB%d' % t4])
                    op('dve', lambda en: en.tensor_mul(out=yB[:, t4, :], in0=yB[:, t4, :], in1=yA[:, t4, :]),
                       ['yA%d' % t4, 'yB%d' % t4], ['yB%d' % t4])
                    op('act', lambda en: en.activation(out=yA[:, t4, :], in_=yB[:, t4, :], func=AF.Square,
                                                       accum_out=ssS[:, t4:t4 + 1]),
                       ['yB%d' % t4], ['yA%d' % t4, 'ssS%d' % t4])

                def d2_N(t4):
                    op('dve', lambda en: en.tensor_scalar(out=sn_bf[:, t4, :], in0=yB[:, t4, :],
                                                          scalar1=rS[:, t4:t4 + 1], scalar2=None, op0=ALU.mult),
                       ['yB%d' % t4, 'rS'], ['sn_bf%d' % t4, *US_])

                def d2_S(t4):
                    b = bank()
                    for kc in range(4):
                        op('pe', lambda en, kc=kc: en.transpose(
                            out=psb[b][:, kc * 128:(kc + 1) * 128], in_=sn_bf[:, t4, kc * 128:(kc + 1) * 128],
                            identity=identb), ['sn_bf%d' % t4, *US_, 'identb'], ['ps%d' % b])
                    src = psb[b][:, 0:512].rearrange("p (k h q) -> p k h q", h=2, q=64)[:, :, :, 0:NQ]
                    dst = XM[:, 4:8, 0:NT].rearrange("p k (q h t) -> p k h q t", h=2, t=4)[:, :, :, :, t4]
                    copy_any(dst, src, ['ps%d' % b], XMALL_)

                for t4 in range(4):
                    d2_T(t4)
                    d2_G(t4)
                op('act', lambda en: en.activation(out=rS, in_=ssS, func=AF.Ln, scale=1.0 / 512, bias=epsb[:, 0:1]),
                   ['ssS0', 'ssS1', 'ssS2', 'ssS3'], ['rS'])
                op('act', lambda en: en.activation(out=rS, in_=rS, func=AF.Exp, scale=-0.5), ['rS'], ['rS'])
                for t4 in range(4):
                    d2_N(t4)
                for t4 in range(4):
                    d2_S(t4)

                if stop == 5.4 and kind == 'P':
                    return True
                def wout_mm(m):
                    b1 = bank()
                    b2 = bank()
                    for n, bb_ in enumerate([b1, b2]):
                        for kc in range(8):
                            op('pe', lambda en, bb_=bb_, kc=kc, n=n: en.matmul(
                                psf[bb_][0:TT, :], lhsT=XM[:, kc, m * TT:(m + 1) * TT],
                                rhs=wout[:, kc, n * 512:(n + 1) * 512],
                                start=(kc == 0), stop=(kc == 7)), ['XM%d' % m, 'wout'], ['ps%d' % bb_])
                    for n, bb_ in enumerate([b1, b2]):
                        op('dve', lambda en, bb_=bb_, n=n: en.tensor_tensor(
                            out=xt[0:TT, m, n * 512:(n + 1) * 512], in0=psf[bb_][0:TT, :],
                            in1=xt[0:TT, m, n * 512:(n + 1) * 512], op=ALU.add),
                           ['ps%d' % bb_, 'xt%d' % m], ['xt%d' % m])
                    if overlap:
                        dma(lambda en: en.dma_start(out=ydst(m), in_=xt[0:TT, m, :]), ['xt%d' % m],
                            ['yd%d_0' % m, 'yd%d_1' % m])

                def norm2(m):
                    xb = m % 2
                    op('act', lambda en: en.activation(out=xnb[0:TT, xb], in_=xt[0:TT, m, :], func=AF.Square,
                                                       accum_out=ss[0:TT, m:m + 1]),
                       ['xt%d' % m], ['xnb%d' % xb, 'ss%d' % m])
                    rstd_from(ss[0:TT, m:m + 1], rs[0:TT, m:m + 1], D, 'ss%d' % m, 'rs%d' % m)
                    op('dve', lambda en: en.tensor_scalar(out=xnb[0:TT, xb], in0=xt[0:TT, m, :],
                                                          scalar1=rs[0:TT, m:m + 1], scalar2=None, op0=ALU.mult),
                       ['xt%d' % m, 'rs%d' % m], ['xnb%d' % xb])

                def norm2_T(m):
                    xb = m % 2
                    b = bank()
                    for kc in range(8):
                        op('pe', lambda en, kc=kc: en.transpose(
                            out=psb[b][:, kc * 128:kc * 128 + TT], in_=xnb[0:TT, xb, kc * 128:(kc + 1) * 128],
                            identity=identb[0:TT, 0:TT]), ['xnb%d' % xb, 'identb'], ['ps%d' % b])
                    copy_any(hnT[:, :, m * TT:(m + 1) * TT],
                             psb[b].rearrange("p (k t) -> p k t", t=128)[:, :, 0:TT], ['ps%d' % b], ['XM%d' % m])

                for m in range(NS + 1):
                    if m < NS:
                        wout_mm(m)
                        norm2(m)
                    if m >= 1:
                        norm2_T(m - 1)
                if stop == 5.5 and kind == 'P':
                    return True

            def g():
                op('dve', lambda en: en.tensor_copy(out=dumD, in_=epsb), ['epsb'],
                   ['dumD'] + YA_ + ['yB0', 'yB1', 'yB2', 'yB3', 'LOCK1'])
                wu_i = mlp_ctr['wu']
                wd_i = mlp_ctr['wd']
                for hh in range(2):
                    for i in range(8):
                        su = wu_i % 3
                        wu_i += 1
                        h0 = (16 * hh + 2 * i) * 128
                        dma(lambda en, su=su, h0=h0: en.dma_start(out=wup_s[:, su], in_=wup_scr[:, :, h0:h0 + 256]),
                            ['wup_scr'], ['wup_s%d' % su])
                        for hl in range(2):
                            ht = 2 * i + hl
                            b = ht % 4
                            for kc in range(8):
                                op('pe', lambda en, b=b, su=su, hl=hl, kc=kc: en.matmul(
                                    psf[b][:, 0:NT], lhsT=wup_s[:, su, kc, hl * 128:(hl + 1) * 128],
                                    rhs=hnT[:, kc, 0:NT], start=(kc == 0), stop=(kc == 7)),
                                   ['wup_s%d' % su] + HNALL_, ['ps%d' % b])
                            rj = ht % 2
                            op('act', lambda en, b=b, rj=rj: en.activation(out=rt[:, rj, 0:NT], in_=psf[b][:, 0:NT],
                                                                           func=AF.Relu), ['ps%d' % b], ['rt%d' % rj])
                            op('dve', lambda en, ht=ht, rj=rj: en.tensor_mul(out=gT[:, ht, 0:NT], in0=rt[:, rj, 0:NT],
                                                                             in1=rt[:, rj, 0:NT]),
                               ['rt%d' % rj, 'LOCK1'], ['gT%d' % ht])
                    pieces = [(nh_, grp_) for nh_ in range(2) for grp_ in range(4)]

                    def wdn_load(pi, sd_):
                        nh_, grp_ = pieces[pi]
                        r0_ = 16 * hh + 4 * grp_
                        dma(lambda en: en.dma_start(
                            out=wdn_s[:, sd_], in_=wdn_scr[:, r0_:r0_ + 4, nh_ * 512:(nh_ + 1) * 512]),
                            ['wdn_scr'], ['wdn_s%d' % sd_])

                    wdn_load(0, wd_i % 3)
                    for nh in range(2):
                        cs = slice(nh * 512, (nh + 1) * 512)
                        for grp in range(4):
                            sd = wd_i % 3
                            wd_i += 1
                            pi = nh * 4 + grp
                            if pi + 1 < len(pieces):
                                wdn_load(pi + 1, wd_i % 3)
                            for m in range(NS):
                                for hl in range(4):
                                    ht = 4 * grp + hl
                                    op('pe', lambda en, m=m, sd=sd, hl=hl, ht=ht, grp=grp: en.matmul(
                                        psf[4 + m][0:TT, :], lhsT=gT[:, ht, m * TT:(m + 1) * TT],
                                        rhs=wdn_s[:, sd, hl, :],
                                        start=(grp == 0 and hl == 0), stop=(grp == 3 and hl == 3)),
                                       ['gT%d' % ht, 'wdn_s%d' % sd], ['ps%d' % (4 + m)])
                        for m in range(NS):
                            if hh == 0:
                                op('dve', lambda en, m=m, cs=cs: en.tensor_tensor(
                                    out=xt[0:TT, m, cs], in0=psf[4 + m][0:TT, :], in1=xt[0:TT, m, cs], op=ALU.add),
                                   ['ps%d' % (4 + m), 'xt%d' % m], ['xt%d' % m])
                            else:
                                yj = mlp_ctr['y'] % 2
                                mlp_ctr['y'] += 1
                                dst = ydst(m)[:, cs]
                                op('dve', lambda en, m=m, cs=cs, yj=yj: en.tensor_tensor(
                                    out=yt[0:TT, yj, :], in0=psf[4 + m][0:TT, :], in1=xt[0:TT, m, cs], op=ALU.add),
                                   ['ps%d' % (4 + m), 'xt%d' % m], ['yt%d' % yj])
                                dma(lambda en, dst=dst, yj=yj: en.dma_start(out=dst, in_=yt[0:TT, yj, :]),
                                    ['yt%d' % yj], [])
                mlp_ctr['wu'] = wu_i
                mlp_ctr['wd'] = wd_i

            if phase == 'F1':
                return f1()
            if phase == 'F2':
                return f2()
            return g()

        mlp_ctr = {'wu': 0, 'wd': 0, 'y': 0}
        op('pool', lambda en: en.memset(VmS, 1.0), [], ['VmS0', 'VmS1', 'VmS2', 'VmS3'])

        if stop == 3:
            P.emit(nc, st)
            return nc
        block('M', 0, 'F1')
        if stop == 4:
            P.emit(nc, st)
            return nc
        blocks = [('P', j_) for j_ in range(NBLK)] + ([('S', 0)] if do_sample else [])
        if block(blocks[0][0], blocks[0][1], 'F1') or block(blocks[0][0], blocks[0][1], 'F2'):
            P.emit(nc, st)
            return nc
        for bi, (bk_, bj_) in enumerate(blocks):
            i0 = len(P.ops)
            block(bk_, bj_, 'G')
            i1 = len(P.ops)
            if bi + 1 < len(blocks):
                nk_, nj_ = blocks[bi + 1]
                block(nk_, nj_, 'F1')
                i2 = len(P.ops)
                if overlap:
                    P.ops[i0:i2] = merge_ops(P.ops[i0:i1], P.ops[i1:i2])
                block(nk_, nj_, 'F2')
        P.emit(nc, st)
    return nc


_NC_CACHE = {}


def kernel(x_prompt, x_sample, cache_swa_k, cache_swa_v, state_ssm_re, state_ssm_im,
           meta_tokens, norm1_g, w_in, q_norm_g, k_norm_g, sinks,
           ssm_A_re, ssm_A_im, ssm_log_dt, ssm_B_re, ssm_B_im, ssm_C_re, ssm_C_im, ssm_D,
           w_glu, b_glu, attn_out_g, ssm_out_g, w_out, norm2_g, w_up, w_down):
    f = lambda a: np.ascontiguousarray(np.asarray(a, dtype=np.float32))
    if 'nc' not in _NC_CACHE:
        _NC_CACHE['nc'] = build()
    nc = _NC_CACHE['nc']
    shared = {
        "meta": f(meta_tokens), "norm1_g": f(norm1_g), "w_in": f(w_in[0]), "q_g": f(q_norm_g), "k_g": f(k_norm_g),
        "sinks": f(sinks), "A_re": f(ssm_A_re[0]), "A_im": f(ssm_A_im[0]), "log_dt": f(ssm_log_dt),
        "B_re": f(ssm_B_re[0]), "B_im": f(ssm_B_im[0]), "C_re": f(ssm_C_re[0]).reshape(512, 64),
        "C_im": f(ssm_C_im[0]).reshape(512, 64), "Dp": f(ssm_D[0]), "w_glu": f(w_glu[0]), "b_glu": f(b_glu),
        "ao_g": f(attn_out_g), "so_g": f(ssm_out_g), "w_out": f(w_out[0]), "norm2_g": f(norm2_g),
        "w_up": f(w_up[0]), "w_down": f(w_down[0]),
    }
    xpf, xsf = f(x_prompt), f(x_sample)
    ckf = f(cache_swa_k[0]).reshape(32, 144, 128)
    cvf = f(cache_swa_v[0]).reshape(32, 144, 128)
    srf = f(state_ssm_re[0]).reshape(32 * 32, 64)
    sif = f(state_ssm_im[0]).reshape(32 * 32, 64)
    in_maps = []
    for c in range(NCORE):
        sl = slice(c * NSEQ, (c + 1) * NSEQ)
        d = dict(shared)
        d.update({"xp": xpf[sl], "xs": xsf[sl], "ck": ckf[sl], "cv": cvf[sl],
                  "s_re": srf[c * 128:(c + 1) * 128], "s_im": sif[c * 128:(c + 1) * 128]})
        in_maps.append(d)
    res = run_bass_kernel_spmd(nc, in_maps, core_ids=list(range(NCORE)))
    R = res.results
    cat = lambda k: np.concatenate([np.asarray(R[c][k], dtype=np.float32) for c in range(NCORE)], axis=0)
    y_prompt = cat("yp")
    y_sample = cat("ys")
    kp = cat("kpo").reshape(1, 32, 144, 2, 64)
    vp = cat("vpo").reshape(1, 32, 144, 2, 64)
    srp_ = cat("srp").reshape(1, 32, 32, 64)
    sip_ = cat("sip").reshape(1, 32, 32, 64)
    ks_ = cat("kso").reshape(1, 32, 64, 2, 64)
    vs_ = cat("vso").reshape(1, 32, 64, 2, 64)
    srs_ = cat("srs").reshape(1, 32, 32, 64)
    sis_ = cat("sis").reshape(1, 32, 32, 64)
    return (y_prompt, y_sample, kp, vp, srp_, sip_, ks_, vs_, srs_, sis_)
```
